# Optimizing a Trainium2 kernel written in Bass

```python
import math
import jax, jax.numpy as jnp
from jax import lax
import numpy as np

D_MODEL = 1024
BATCH = 8
SEQ = 4096
DEPTH = 2

N_EVEN = (DEPTH + 1) // 2
N_ODD = DEPTH // 2

GRID_W = 64

MLA_HEADS = 8
MLA_Q_LORA = 256
MLA_KV_LORA = 128
MLA_NOPE = 64
MLA_ROPE = 32
MLA_V = 64
MLA_QBLOCK = 128
ROPE_THETA = 10000.0

NA_HEADS = 8
NA_HEAD_DIM = 64
NA_WIN_H = 8
NA_WIN_W = 16
NA_WIDTH = NA_HEADS * NA_HEAD_DIM

RET_HEADS = 4
RET_QK_DIM = 128
RET_V_DIM = 128
RET_CHUNK = 128
RET_QK_W = RET_HEADS * RET_QK_DIM
RET_V_W = RET_HEADS * RET_V_DIM

HY_WIDTH = 512
HY_EMB_DIM = 33
HY_FILTER_HIDDEN = 64
HY_FAST_DECAY = 0.3
HY_SLOW_DECAY = 1.5
HY_TARGET = 1e-2

D_FF = 2816
SHORT_CONV = 3
EPS = 1e-6

A_IN = MLA_Q_LORA + MLA_KV_LORA + MLA_ROPE + 3 * NA_WIDTH
A_OUT = MLA_HEADS * MLA_V + NA_WIDTH
C_IN = 2 * RET_QK_W + 2 * RET_V_W + 3 * HY_WIDTH
C_OUT = RET_V_W + HY_WIDTH

kernel_name = 'hybrid_mla_natten_retnet_hyena_encoder'


def rms_norm(x, gain):
    xf = x.astype(jnp.float32)
    y = xf * lax.rsqrt(jnp.mean(xf * xf, axis=-1, keepdims=True) + EPS)
    return (y * gain.astype(jnp.float32)).astype(x.dtype)


def dwconv3(x, w):
    xp = jnp.pad(x, ((0, 0), (1, 1), (0, 0)))
    return xp[:, :-2] * w[0] + xp[:, 1:-1] * w[1] + xp[:, 2:] * w[2]


def split_cols(t, sizes):
    out, off = [], 0
    for s in sizes:
        out.append(t[..., off:off + s])
        off += s
    return out


def rope_tables(length, dim):
    inv = ROPE_THETA ** (-jnp.arange(0, dim, 2, dtype=jnp.float32) / dim)
    ang = jnp.arange(length, dtype=jnp.float32)[:, None] * inv[None, :]
    return jnp.cos(ang), jnp.sin(ang)


def apply_rope(x, cos, sin):
    shape = (1, cos.shape[0]) + (1,) * (x.ndim - 3) + (cos.shape[1],)
    c = cos.reshape(shape).astype(x.dtype)
    s = sin.reshape(shape).astype(x.dtype)
    x1, x2 = jnp.split(x, 2, axis=-1)
    return jnp.concatenate([x1 * c - x2 * s, x1 * s + x2 * c], axis=-1)


def mla_attention(cq, ckv, k_rope, q_norm, w_q_up, kv_norm, w_kv_up):
    B, S, _ = cq.shape
    q = (rms_norm(cq, q_norm) @ w_q_up).reshape(B, S, MLA_HEADS, MLA_NOPE + MLA_ROPE)
    kv = (rms_norm(ckv, kv_norm) @ w_kv_up).reshape(B, S, MLA_HEADS, MLA_NOPE + MLA_V)
    q_nope, q_rope = q[..., :MLA_NOPE], q[..., MLA_NOPE:]
    k_nope, v = kv[..., :MLA_NOPE], kv[..., MLA_NOPE:]
    cos, sin = rope_tables(S, MLA_ROPE)
    q_rope = apply_rope(q_rope, cos, sin)
    k_rope = apply_rope(k_rope, cos, sin)
    scale = (MLA_NOPE + MLA_ROPE) ** -0.5
    nb = S // MLA_QBLOCK

    def to_blocks(t):
        return jnp.moveaxis(t.reshape((B, nb, MLA_QBLOCK) + t.shape[2:]), 1, 0)

    def block(args):
        qn, qr = args
        s = (jnp.einsum('bqhd,bkhd->bhqk', qn, k_nope)
             + jnp.einsum('bqhr,bkr->bhqk', qr, k_rope))
        p = jax.nn.softmax(s.astype(jnp.float32) * scale, axis=-1).astype(v.dtype)
        return jnp.einsum('bhqk,bkhd->bqhd', p, v)

    o = lax.map(block, (to_blocks(q_nope), to_blocks(q_rope)))
    return jnp.moveaxis(o, 0, 1).reshape(B, S, MLA_HEADS * MLA_V)


def neighbourhood_attention(q, k, v, rpb):
    B, S, _ = q.shape
    rows = S // GRID_W
    kh = min(NA_WIN_H, rows)
    kw = min(NA_WIN_W, GRID_W)

    def grid(t):
        return t.reshape(B, rows, GRID_W, NA_HEADS, NA_HEAD_DIM)

    qg, kg, vg = grid(q), grid(k), grid(v)
    cols = jnp.arange(GRID_W)
    col_start = jnp.clip(cols - kw // 2, 0, GRID_W - kw)
    col_idx = col_start[:, None] + jnp.arange(kw)[None, :]
    col_off = col_idx - cols[:, None] + (NA_WIN_W - 1)
    scale = NA_HEAD_DIM ** -0.5

    def row_block(r):
        rs = jnp.clip(r - kh // 2, 0, rows - kh)
        k_rows = lax.dynamic_slice_in_dim(kg, rs, kh, axis=1)
        v_rows = lax.dynamic_slice_in_dim(vg, rs, kh, axis=1)
        k_win = k_rows[:, :, col_idx]
        v_win = v_rows[:, :, col_idx]
        q_row = lax.dynamic_index_in_dim(qg, r, axis=1, keepdims=False)
        row_off = rs + jnp.arange(kh) - r + (NA_WIN_H - 1)
        bias = jnp.transpose(rpb[:, row_off][:, :, col_off], (0, 2, 1, 3))
        s = (jnp.einsum('bchd,bicjhd->bhcij', q_row, k_win).astype(jnp.float32) * scale
             + bias.astype(jnp.float32)[None])
        p = jax.nn.softmax(s.reshape(B, NA_HEADS, GRID_W, kh * kw), axis=-1)
        p = p.reshape(s.shape).astype(v.dtype)
        return jnp.einsum('bhcij,bicjhd->bchd', p, v_win)

    o = lax.map(row_block, jnp.arange(rows))
    return jnp.moveaxis(o, 0, 1).reshape(B, S, NA_WIDTH)


def retention_scan(q, k, v, log_g, strict):
    B, H, S, dk = q.shape
    dv = v.shape[-1]
    C = RET_CHUNK
    n = S // C
    j = jnp.arange(C, dtype=jnp.float32)
    diff = j[:, None] - j[None, :]
    mask = (diff > 0) if strict else (diff >= 0)
    dmat = jnp.where(mask[None], jnp.exp(jnp.where(mask, diff, 0.0)[None] * log_g[:, None, None]),
                     0.0).astype(q.dtype)
    xi = jnp.exp((j + 1.0)[None] * log_g[:, None]).astype(q.dtype)
    zeta = jnp.exp((C - 1.0 - j)[None] * log_g[:, None]).astype(q.dtype)
    g_chunk = jnp.exp(C * log_g).astype(q.dtype)
    qc = q.reshape(B, H, n, C, dk)
    kc = k.reshape(B, H, n, C, dk)
    vc = v.reshape(B, H, n, C, dv)
    scores = jnp.einsum('bhncd,bhnmd->bhncm', qc, kc) * dmat[None, :, None]
    o_intra = jnp.einsum('bhncm,bhnme->bhnce', scores, vc)
    kv_chunk = jnp.einsum('bhncd,bhnce->bhnde', kc * zeta[None, :, None, :, None], vc)

    def step(state, kv_n):
        return g_chunk[None, :, None, None] * state + kv_n, state

    _, states = lax.scan(step, jnp.zeros((B, H, dk, dv), q.dtype), jnp.moveaxis(kv_chunk, 2, 0))
    states = jnp.moveaxis(states, 0, 2)
    o_cross = jnp.einsum('bhncd,bhnde->bhnce', qc, states) * xi[None, :, None, :, None]
    return (o_intra + o_cross).reshape(B, H, S, dv)


def bidirectional_retention(rq, rk, rv, rg, decay_fwd, decay_bwd):
    B, S, _ = rq.shape
    cos, sin = rope_tables(S, RET_QK_DIM)
    q = apply_rope(rq.reshape(B, S, RET_HEADS, RET_QK_DIM), cos, sin)
    k = apply_rope(rk.reshape(B, S, RET_HEADS, RET_QK_DIM), cos, sin) * (RET_QK_DIM ** -0.5)
    q = jnp.transpose(q, (0, 2, 1, 3))
    k = jnp.transpose(k, (0, 2, 1, 3))
    v = jnp.transpose(rv.reshape(B, S, RET_HEADS, RET_V_DIM), (0, 2, 1, 3))
    log_f = jax.nn.log_sigmoid(decay_fwd.astype(jnp.float32))
    log_b = jax.nn.log_sigmoid(decay_bwd.astype(jnp.float32))
    o_f = retention_scan(q, k, v, log_f, False)
    o_b = retention_scan(q[:, :, ::-1], k[:, :, ::-1], v[:, :, ::-1], log_b, True)[:, :, ::-1]
    o = jnp.transpose(o_f + o_b, (0, 2, 1, 3)).astype(jnp.float32)
    o = (o * lax.rsqrt(jnp.mean(o * o, axis=-1, keepdims=True) + EPS)).astype(rv.dtype)
    gate = jax.nn.silu(rg.reshape(B, S, RET_HEADS, RET_V_DIM))
    return (o * gate).reshape(B, S, RET_V_W)


def hyena_filters(length, w1, b1, w2, b2, w3, b3, w4, freq):
    t = jnp.arange(length, dtype=jnp.float32) / (length - 1)
    bands = (HY_EMB_DIM - 1) // 2
    w = 2.0 * math.pi * jnp.arange(length, dtype=jnp.float32) / length
    f = jnp.linspace(1e-4, bands - 1, bands, dtype=jnp.float32)
    fw = f[None, :] * w[:, None]
    z = jnp.concatenate([t[:, None], jnp.cos(fw), -jnp.sin(fw)], axis=-1).astype(w1.dtype)
    h = jnp.sin(freq * (z @ w1 + b1))
    h = jnp.sin(freq * (h @ w2 + b2))
    h = jnp.sin(freq * (h @ w3 + b3))
    h = h @ w4
    max_decay = math.log(HY_TARGET) / HY_FAST_DECAY
    min_decay = math.log(HY_TARGET) / HY_SLOW_DECAY
    deltas = jnp.linspace(min_decay, max_decay, HY_WIDTH, dtype=jnp.float32)
    window = jnp.exp(-t[:, None] * jnp.abs(deltas)[None, :]).astype(h.dtype)
    return h[:, :HY_WIDTH] * window, h[:, HY_WIDTH:] * window


def hyena_operator(u_proj, short_conv, w1, b1, w2, b2, w3, b3, w4, freq, hy_bias):
    B, L, _ = u_proj.shape
    z = dwconv3(u_proj, short_conv)
    x0, x1, v = jnp.split(z, 3, axis=-1)
    h_f, h_b = hyena_filters(L, w1, b1, w2, b2, w3, b3, w4, freq)
    k_circ = jnp.concatenate([h_f, jnp.zeros((1, HY_WIDTH), h_f.dtype), h_b[1:][::-1]],
                             axis=0).astype(jnp.float32)
    u = v * x1
    uf = jnp.fft.rfft(u.astype(jnp.float32), n=2 * L, axis=1)
    kf = jnp.fft.rfft(k_circ, n=2 * L, axis=0)
    y = jnp.fft.irfft(uf * kf[None], n=2 * L, axis=1)[:, :L].astype(u.dtype)
    y = y + u * hy_bias
    return y * x0


def even_mixer(h, w_in, q_norm, w_q_up, kv_norm, w_kv_up, rpb, w_out):
    cq, ckv, k_rope, nq, nk, nv = split_cols(
        h @ w_in, [MLA_Q_LORA, MLA_KV_LORA, MLA_ROPE, NA_WIDTH, NA_WIDTH, NA_WIDTH])
    a = mla_attention(cq, ckv, k_rope, q_norm, w_q_up, kv_norm, w_kv_up)
    b = neighbourhood_attention(nq, nk, nv, rpb)
    return jnp.concatenate([a, b], axis=-1) @ w_out


def odd_mixer(h, w_in, decay_fwd, decay_bwd, short_conv, w1, b1, w2, b2, w3, b3, w4, freq,
              hy_bias, w_out):
    rq, rk, rv, rg, hy = split_cols(h @ w_in, [RET_QK_W, RET_QK_W, RET_V_W, RET_V_W, 3 * HY_WIDTH])
    c = bidirectional_retention(rq, rk, rv, rg, decay_fwd, decay_bwd)
    d = hyena_operator(hy, short_conv, w1, b1, w2, b2, w3, b3, w4, freq, hy_bias)
    return jnp.concatenate([c, d], axis=-1) @ w_out


def conv_ffn(h, w_gate, w_up, conv_w, w_down):
    g = dwconv3(h @ w_gate, conv_w)
    return (jax.nn.gelu(g, approximate=True) * (h @ w_up)) @ w_down


def setup_inputs(seed: int = 0) -> dict:
    key = jax.random.key(seed)
    ks = iter(jax.random.split(key, 48))

    def nrm(shape, scale):
        return jax.random.normal(next(ks), shape, jnp.float32) * scale

    def gain(shape):
        return 1.0 + nrm(shape, 0.01)

    ret_decay_init = jnp.log(2.0 ** (5.0 + jnp.arange(RET_HEADS, dtype=jnp.float32)) - 1.0)
    return {
        'x': nrm((BATCH, SEQ, D_MODEL), 1.0),
        'mix_pre_norm': gain((DEPTH, D_MODEL)),
        'mix_post_norm': gain((DEPTH, D_MODEL)),
        'ffn_pre_norm': gain((DEPTH, D_MODEL)),
        'ffn_post_norm': gain((DEPTH, D_MODEL)),
        'ffn_w_gate': nrm((DEPTH, D_MODEL, D_FF), D_MODEL ** -0.5),
        'ffn_w_up': nrm((DEPTH, D_MODEL, D_FF), D_MODEL ** -0.5),
        'ffn_conv': nrm((DEPTH, SHORT_CONV, D_FF), SHORT_CONV ** -0.5),
        'ffn_w_down': nrm((DEPTH, D_FF, D_MODEL), D_FF ** -0.5),
        'a_w_in': nrm((N_EVEN, D_MODEL, A_IN), D_MODEL ** -0.5),
        'a_q_norm': gain((N_EVEN, MLA_Q_LORA)),
        'a_w_q_up': nrm((N_EVEN, MLA_Q_LORA, MLA_HEADS * (MLA_NOPE + MLA_ROPE)), MLA_Q_LORA ** -0.5),
        'a_kv_norm': gain((N_EVEN, MLA_KV_LORA)),
        'a_w_kv_up': nrm((N_EVEN, MLA_KV_LORA, MLA_HEADS * (MLA_NOPE + MLA_V)), MLA_KV_LORA ** -0.5),
        'a_rpb': nrm((N_EVEN, NA_HEADS, 2 * NA_WIN_H - 1, 2 * NA_WIN_W - 1), 0.1),
        'a_w_out': nrm((N_EVEN, A_OUT, D_MODEL), A_OUT ** -0.5),
        'c_w_in': nrm((N_ODD, D_MODEL, C_IN), D_MODEL ** -0.5),
        'c_decay_fwd': ret_decay_init[None] + nrm((N_ODD, RET_HEADS), 0.1),
        'c_decay_bwd': ret_decay_init[None] + nrm((N_ODD, RET_HEADS), 0.1),
        'c_short_conv': nrm((N_ODD, SHORT_CONV, 3 * HY_WIDTH), SHORT_CONV ** -0.5),
        'c_filt_w1': nrm((N_ODD, HY_EMB_DIM, HY_FILTER_HIDDEN), HY_EMB_DIM ** -0.5),
        'c_filt_b1': nrm((N_ODD, HY_FILTER_HIDDEN), 0.02),
        'c_filt_w2': nrm((N_ODD, HY_FILTER_HIDDEN, HY_FILTER_HIDDEN), HY_FILTER_HIDDEN ** -0.5),
        'c_filt_b2': nrm((N_ODD, HY_FILTER_HIDDEN), 0.02),
        'c_filt_w3': nrm((N_ODD, HY_FILTER_HIDDEN, HY_FILTER_HIDDEN), HY_FILTER_HIDDEN ** -0.5),
        'c_filt_b3': nrm((N_ODD, HY_FILTER_HIDDEN), 0.02),
        'c_filt_w4': nrm((N_ODD, HY_FILTER_HIDDEN, 2 * HY_WIDTH), 0.1 * HY_FILTER_HIDDEN ** -0.5),
        'c_filt_freq': gain((N_ODD, HY_FILTER_HIDDEN)),
        'c_hy_bias': nrm((N_ODD, HY_WIDTH), 1.0),
        'c_w_out': nrm((N_ODD, C_OUT, D_MODEL), C_OUT ** -0.5),
    }


def reference(x, mix_pre_norm, mix_post_norm, ffn_pre_norm, ffn_post_norm, ffn_w_gate, ffn_w_up,
              ffn_conv, ffn_w_down, a_w_in, a_q_norm, a_w_q_up, a_kv_norm, a_w_kv_up, a_rpb, a_w_out,
              c_w_in, c_decay_fwd, c_decay_bwd, c_short_conv, c_filt_w1, c_filt_b1, c_filt_w2,
              c_filt_b2, c_filt_w3, c_filt_b3, c_filt_w4, c_filt_freq, c_hy_bias, c_w_out):
    for layer in range(DEPTH):
        i = layer // 2
        h = rms_norm(x, mix_pre_norm[layer])
        if layer % 2 == 0:
            m = even_mixer(h, a_w_in[i], a_q_norm[i], a_w_q_up[i], a_kv_norm[i], a_w_kv_up[i],
                           a_rpb[i], a_w_out[i])
        else:
            m = odd_mixer(h, c_w_in[i], c_decay_fwd[i], c_decay_bwd[i], c_short_conv[i],
                          c_filt_w1[i], c_filt_b1[i], c_filt_w2[i], c_filt_b2[i], c_filt_w3[i],
                          c_filt_b3[i], c_filt_w4[i], c_filt_freq[i], c_hy_bias[i], c_w_out[i])
        x = x + rms_norm(m, mix_post_norm[layer])
        h = rms_norm(x, ffn_pre_norm[layer])
        f = conv_ffn(h, ffn_w_gate[layer], ffn_w_up[layer], ffn_conv[layer], ffn_w_down[layer])
        x = x + rms_norm(f, ffn_post_norm[layer])
    return x
```

```python
import numpy as np
import ml_dtypes
from contextlib import ExitStack, contextmanager
import concourse.bass as bass
import concourse.mybir as mybir
from concourse.alu_op_type import AluOpType as ALU
from concourse.bass_utils import run_bass_kernel_spmd

F32 = mybir.dt.float32
BF16 = mybir.dt.bfloat16
AF = mybir.ActivationFunctionType
NPBF = ml_dtypes.bfloat16

S_LEN = 4096
D = 1024
NT = 32
NB = 8
EPS = 1e-6
DFF = 2816
NFC = 22
NEGINF = -30000.0
RET_OFF = 3968
RET_M = 8064


class Sem:
    __slots__ = ("h", "count", "name")

    def __init__(self, h, name):
        self.h = h
        self.count = 0
        self.name = name


class Eng:
    def __init__(self, name, h):
        self.name = name
        self.h = h
        self.sem = None
        self.seen = {}
        self.n_inst = 0


class Buf:
    __slots__ = ("name", "writer", "readers")

    def __init__(self, name):
        self.name = name
        self.writer = {}
        self.readers = {}


class Sched:
    RING = 28
    ROT = 12000

    def __init__(self, nc):
        self.nc = nc
        self.engs = {
            "pe": Eng("pe", nc.tensor),
            "act": Eng("act", nc.scalar),
            "dve": Eng("dve", nc.vector),
            "pool": Eng("pool", nc.gpsimd),
            "sp": Eng("sp", nc.sync),
        }
        self.nsem = 0
        for e in self.engs.values():
            e.sem = self.new_sem(e.name)
        self.ring = [self.new_sem("dma%d" % i) for i in range(self.RING)]
        self.ring_i = 0
        self.nbuf = 0
        self.deferred = []
        self._flushing = False

    def new_sem(self, name):
        self.nsem += 1
        return Sem(self.nc.alloc_semaphore(name="s_%s_%d" % (name, self.nsem)), name)

    def buf(self, name=None):
        self.nbuf += 1
        return Buf(name or "b%d" % self.nbuf)

    def rotate(self):
        for e in self.engs.values():
            if e.sem.count > self.ROT:
                e.sem = self.new_sem(e.name)

    def _wait(self, eng, sem, val):
        if val <= 0 or eng.seen.get(sem, 0) >= val:
            return
        eng.h.wait_ge(sem.h, val)
        eng.seen[sem] = val
        eng.n_inst += 1

    def _deps(self, eng, r, w, acc):
        deps = {}
        for b in r:
            for s, v in b.writer.items():
                if deps.get(s, 0) < v:
                    deps[s] = v
        own_ok = eng.name == "pe"
        for b in w:
            if not acc:
                for s, v in b.writer.items():
                    if s is eng.sem and own_ok:
                        continue
                    if deps.get(s, 0) < v:
                        deps[s] = v
            for s, v in b.readers.items():
                if s is eng.sem and own_ok:
                    continue
                if deps.get(s, 0) < v:
                    deps[s] = v
        return deps

    def _record(self, tok, r, w, acc):
        s, v = tok
        for b in r:
            if b.readers.get(s, 0) < v:
                b.readers[s] = v
        for b in w:
            if acc:
                b.writer[s] = v
            else:
                b.writer = {s: v}
                b.readers = {}

    def op(self, eng, fn, r=(), w=(), acc=False):
        self._tick(r, w)
        e = self.engs[eng]
        deps = self._deps(e, r, w, acc)
        for s, v in deps.items():
            if s is e.sem and eng == "pe":
                continue
            self._wait(e, s, v)
        inst = fn()
        e.sem.count += 1
        inst.then_inc(e.sem.h, 1)
        e.n_inst += 1
        tok = (e.sem, e.sem.count)
        self._record(tok, r, w, acc)
        return tok

    DEFER = 14

    def _tick(self, r, w):
        if self._flushing or not self.deferred:
            return
        keep = []
        todo = []
        for item in self.deferred:
            dr, dw = item[3], item[4]
            conflict = any(b in dr for b in w) or any(b in dw for b in w) or any(b in dw for b in r)
            item[0] -= 1
            if conflict or item[0] <= 0 or todo:
                todo.append(item)
            else:
                keep.append(item)
        self.deferred = keep
        self._flushing = True
        for _, out, in_, dr, dw, acc, kw in todo:
            self.dma("sp", out, in_, r=dr, w=dw, acc=acc, **kw)
        self._flushing = False

    def flush(self):
        todo, self.deferred = self.deferred, []
        self._flushing = True
        for _, out, in_, dr, dw, acc, kw in todo:
            self.dma("sp", out, in_, r=dr, w=dw, acc=acc, **kw)
        self._flushing = False

    def dma(self, eng, out, in_, r=(), w=(), acc=False, **kw):
        if eng == "pool":
            self.deferred.append([self.DEFER, out, in_, list(r), list(w), acc, kw])
            return None
        self._tick(r, w)
        e = self.engs[eng]
        deps = self._deps(e, r, w, acc)
        for s, v in deps.items():
            self._wait(e, s, v)
        sem = self.ring[self.ring_i % self.RING]
        self.ring_i += 1
        self._wait(e, sem, sem.count)
        inst = e.h.dma_start(out=out, in_=in_, **kw)
        sem.count += 16
        inst.then_inc(sem.h, 16)
        e.n_inst += 1
        tok = (sem, sem.count)
        self._record(tok, r, w, acc)
        return tok

    def finish(self, bufs):
        self.flush()
        e = self.engs["sp"]
        for b in bufs:
            for s, v in b.writer.items():
                self._wait(e, s, v)
        for sem in self.ring:
            self._wait(e, sem, sem.count)
        for o in self.engs.values():
            if o is not e:
                self._wait(e, o.sem, o.sem.count)


class Tile:
    __slots__ = ("t", "b")

    def __init__(self, t, b):
        self.t = t
        self.b = b

    def __getitem__(self, k):
        return self.t[k]


def _rope_tab(dim, rows_rep):
    inv = (10000.0 ** (-np.arange(0, dim, 2, dtype=np.float32) / np.float32(dim))).astype(np.float32)
    ang = (np.arange(S_LEN, dtype=np.float32)[:, None] * inv[None, :]).astype(np.float32)
    c = np.cos(ang).astype(np.float32).T
    s = np.sin(ang).astype(np.float32).T
    return (np.ascontiguousarray(np.concatenate([c, c], 0)), np.ascontiguousarray(np.concatenate([s, s], 0)))


_CONST_CACHE = {}


def host_consts():
    if _CONST_CACHE:
        return _CONST_CACHE
    c = {}
    c["ident_bf"] = np.eye(128, dtype=np.float32).astype(NPBF)
    cm, sm = _rope_tab(32, 2)
    c["mla_cs"] = np.ascontiguousarray(np.stack([cm, sm], 1))
    cr, sr = _rope_tab(128, 2)
    c["ret_cs"] = np.ascontiguousarray(np.stack([cr, sr], 1))
    i = np.arange(128, dtype=np.int64)[:, None]
    m = np.arange(RET_M, dtype=np.int64)[None, :]
    dl = i - m + RET_OFF
    c["ret_np"] = np.ascontiguousarray(
        np.stack([np.maximum(-dl, 0), np.maximum(dl, 0)], 1).astype(np.float32))
    L = S_LEN
    t = (np.arange(L, dtype=np.float32) / np.float32(L - 1)).astype(np.float32)
    w = (np.float32(2.0 * np.pi) * np.arange(L, dtype=np.float32) / np.float32(L)).astype(np.float32)
    f = np.linspace(1e-4, 15, 16, dtype=np.float32)
    fw = (f[None, :] * w[:, None]).astype(np.float32)
    z = np.concatenate([t[:, None], np.cos(fw), -np.sin(fw)], -1).astype(np.float32)
    c["hy_zT"] = np.ascontiguousarray(z.T)
    import math
    max_decay = math.log(1e-2) / 0.3
    min_decay = math.log(1e-2) / 1.5
    deltas = np.linspace(min_decay, max_decay, 512, dtype=np.float32)
    c["hy_win"] = np.exp(-t[:, None] * np.abs(deltas)[None, :]).astype(np.float32)
    N = 2 * L
    a = (2 * np.arange(L, dtype=np.int64) + 1)
    ph = (a[:, None] * a[None, :]) % (4 * N)
    ang = ph.astype(np.float64) * (2.0 * np.pi / (4 * N))

    def tile(mat):
        return np.ascontiguousarray(mat.reshape(32, 128, 32, 128).transpose(2, 1, 0, 3)).astype(NPBF)

    c["dft_c"] = tile(np.cos(ang).astype(np.float32))
    c["dft_s"] = tile(np.sin(ang).astype(np.float32))
    del ang, ph
    phi = np.pi * (np.arange(L, dtype=np.float64) + 0.5) / N
    c["dft_ph"] = np.ascontiguousarray(
        np.stack([np.cos(phi), np.sin(phi)], 0).reshape(2, 32, 128).transpose(2, 0, 1)).astype(np.float32)
    _CONST_CACHE.update(c)
    return c


def na_bias_layout(rpb):
    H = rpb.shape[0]
    out = np.full((H, 3, 16, 64, 8, 64), NEGINF, dtype=np.float32)
    qc = np.arange(64)
    cs = np.clip(qc - 8, 0, 48)
    for pat, (ws, r0) in enumerate([(0, 0), (4, 8), (48, 56)]):
        for qr in range(8):
            r = r0 + qr
            rs = min(max(r - 4, 0), 56)
            for i in range(8):
                kr = rs + i
                krl = kr - ws
                for j in range(16):
                    kc = cs + j
                    out[:, pat, krl, kc, qr, qc] = rpb[:, kr - r + 7, kc - qc + 15]
    out = out.reshape(H, 3, 1024, 512)
    return np.ascontiguousarray(out.reshape(H, 3, 8, 128, 512).transpose(0, 3, 1, 2, 4))


def na_col_ranges():
    out = []
    for (ws, r0) in [(0, 0), (4, 8), (48, 56)]:
        per = []
        for i in range(8):
            rows = (ws + 2 * i, ws + 2 * i + 1)
            rel = []
            for qr in range(8):
                r = r0 + qr
                rs = min(max(r - 4, 0), 56)
                if any(rs <= kr < rs + 8 for kr in rows):
                    rel.append(qr)
            per.append((rel[0] * 64, (rel[-1] + 1) * 64) if rel else None)
        out.append(per)
    return out

class Builder:
    def __init__(self, phases=None, debug=(), ext_in=()):
        self.ext_in = set(ext_in)
        self.nc = bass.Bass("TRN2", target_bir_lowering=False)
        self.S = Sched(self.nc)
        self.phases = phases
        self.debug = set(debug)
        self.d = {}
        self.db = {}
        self.rr = 0
        self.free_haz = {}
        self.uid = 0
        self.scopes = {}

    def din(self, name, shape, dt=F32):
        self.d[name] = self.nc.dram_tensor(name, list(shape), dt, kind="ExternalInput").ap()
        self.db[name] = self.S.buf(name)

    def dscratch(self, name, shape, dt):
        kind = "ExternalOutput" if name in self.debug else ("ExternalInput" if name in self.ext_in else "Internal")
        self.d[name] = self.nc.dram_tensor(name, list(shape), dt, kind=kind).ap()
        self.db[name] = self.S.buf(name)

    def _newbuf(self, es, name):
        b = self.S.buf(name)
        b.readers = dict(self.free_haz)
        self.scopes.setdefault(id(es), []).append(b)
        return b

    def tile(self, es, name, shape, dt):
        self.uid += 1
        t = es.enter_context(self.nc.sbuf_tensor("%s_%d" % (name, self.uid), list(shape), dt))
        return Tile(t, self._newbuf(es, name))

    def ptile(self, es, name, shape, dt=F32):
        self.uid += 1
        t = es.enter_context(self.nc.psum_tensor("%s_%d" % (name, self.uid), list(shape), dt))
        return Tile(t, self._newbuf(es, name))

    @contextmanager
    def scope(self):
        with ExitStack() as es:
            yield es
            self.S.flush()
            for b in self.scopes.pop(id(es), []):
                for dct in (b.writer, b.readers):
                    for s, v in dct.items():
                        if self.free_haz.get(s, 0) < v:
                            self.free_haz[s] = v

    def eng3(self):
        self.rr += 1
        return ("dve", "pool", "act")[self.rr % 3]

    def eng2(self):
        self.rr += 1
        return ("dve", "pool")[self.rr % 2]

    def declare(self):
        din = self.din
        din("x", [S_LEN, D])
        for n in ["mix_pre_norm", "mix_post_norm", "ffn_pre_norm", "ffn_post_norm"]:
            din(n, [2, D])
        din("ffn_w_gate", [2, D, DFF]); din("ffn_w_up", [2, D, DFF]); din("ffn_conv", [2, 3, DFF]); din("ffn_w_down", [2, DFF, D])
        din("a_w_in", [D, 1952]); din("a_q_norm", [256]); din("a_w_q_up", [256, 768]); din("a_kv_norm", [128])
        din("a_w_kv_up", [128, 1024]); din("a_w_out", [D, D])
        din("c_w_in", [D, 3584]); din("c_decay", [8]); din("c_short_conv", [3, 1536])
        din("c_filt_w1", [33, 64]); din("c_filt_b1", [64]); din("c_filt_w2", [64, 64]); din("c_filt_b2", [64])
        din("c_filt_w3", [64, 64]); din("c_filt_b3", [64]); din("c_filt_w4", [64, 1024]); din("c_filt_freq", [64])
        din("c_hy_bias", [512]); din("c_w_out", [D, D])
        din("ident_bf", [128, 128], BF16)
        din("mla_cs", [32, 2, S_LEN]); din("ret_cs", [128, 2, S_LEN]); din("ret_np", [128, 2, RET_M])
        din("na_bias", [8, 128, 3, 8, 512])
        din("hy_zT", [33, S_LEN]); din("hy_win", [S_LEN, 512])
        din("dft_c", [32, 128, 32, 128], BF16); din("dft_s", [32, 128, 32, 128], BF16); din("dft_ph", [128, 2, 32])
        ds = self.dscratch
        ds("QT", [8, 96, S_LEN], BF16); ds("KNT", [8, 64, S_LEN], BF16); ds("KRT", [32, S_LEN], BF16)
        ds("MV", [S_LEN, 512], BF16)
        ds("NQT", [512, S_LEN], BF16); ds("NKT", [512, S_LEN], BF16); ds("NV", [S_LEN, 512], BF16)
        ds("CT", [D, S_LEN], BF16)
        ds("X1", [S_LEN, D], F32); ds("X2", [S_LEN, D], F32); ds("X3", [S_LEN, D], F32)
        for l in range(2):
            ds("WG%d" % l, [NFC, 128, 8, 128], BF16); ds("WU%d" % l, [NFC, 128, 8, 128], BF16)
            ds("WD%d" % l, [NFC, 128, D], BF16)
        ds("RQT", [4, 128, S_LEN], BF16); ds("RKT", [4, 128, S_LEN], BF16); ds("RV", [S_LEN, 512], BF16)
        ds("RG", [512, S_LEN], F32); ds("HY", [S_LEN + 2, 1536], F32)
        ds("C1T", [D, S_LEN], BF16)
        ds("XD", [S_LEN, 1536], BF16); ds("UB", [S_LEN, 512], F32); ds("X0", [S_LEN, 512], F32)
        ds("YD", [32, 128, 2, 512], BF16); ds("K2D", [S_LEN, 1024], F32)
        self.d["out"] = self.nc.dram_tensor("out", [S_LEN, D], F32, kind="ExternalOutput").ap()
        self.db["out"] = self.S.buf("out")

    def setup_common(self, es):
        nc, S, d, db = self.nc, self.S, self.d, self.db
        self.ident = self.tile(es, "ident", [128, 128], BF16)
        S.dma("sp", self.ident[:, :], d["ident_bf"][:, :], r=[db["ident_bf"]], w=[self.ident.b])
        self.ones_bf = self.tile(es, "ones_bf", [128, 128], BF16)
        S.op("pool", lambda: nc.gpsimd.memset(self.ones_bf[:, :], 1.0), w=[self.ones_bf.b])
        self.epsb = self.tile(es, "epsb", [128, 1], F32)
        S.op("pool", lambda: nc.gpsimd.memset(self.epsb[:, :], EPS), w=[self.epsb.b])
        self.mhalf = self.tile(es, "mhalf", [128, 1], F32)
        S.op("pool", lambda: nc.gpsimd.memset(self.mhalf[:, :], -0.5), w=[self.mhalf.b])
        self.ones_f = self.tile(es, "ones_f", [128, 128], F32)
        S.op("pool", lambda: nc.gpsimd.memset(self.ones_f[:, :], 1.0), w=[self.ones_f.b])
        self.G = self.tile(es, "gains", [128, 40], F32)
        specs = [(d["mix_pre_norm"][0, :], 0, 8, "mix_pre_norm"), (d["ffn_pre_norm"][0, :], 8, 8, "ffn_pre_norm"),
                 (d["mix_pre_norm"][1, :], 16, 8, "mix_pre_norm"), (d["ffn_pre_norm"][1, :], 24, 8, "ffn_pre_norm"),
                 (d["a_q_norm"][:], 32, 2, "a_q_norm"), (d["a_kv_norm"][:], 34, 1, "a_kv_norm")]
        for ap, c0, n, nm in specs:
            S.dma("sp", self.G[:, c0:c0 + n], ap.rearrange("(c p) -> p c", p=128), r=[db[nm]], w=[self.G.b], acc=True,
                  allow_slow_non_contiguous=True)
        self.PG = self.tile(es, "pgains", [128, 4, D], F32)
        for i, (nm, l) in enumerate([("mix_post_norm", 0), ("ffn_post_norm", 0), ("mix_post_norm", 1), ("ffn_post_norm", 1)]):
            S.dma("sp", self.PG[:, i, :], d[nm][l, :].partition_broadcast(128), r=[db[nm]], w=[self.PG.b], acc=True)

    def conv_piece(self, dst, src, gain_ap, sign=1.0, eng=None):
        nc = self.nc
        eng = eng or self.eng3()
        if gain_ap is None:
            if eng == "act":
                return eng, (lambda: nc.scalar.mul(out=dst, in_=src, mul=float(sign)))
            e = nc.vector if eng == "dve" else nc.gpsimd
            return eng, (lambda: e.tensor_scalar(out=dst, in0=src, scalar1=float(sign), scalar2=1.0, op0=ALU.mult, op1=ALU.mult))
        if eng == "act" and sign == 1.0:
            return eng, (lambda: nc.scalar.activation(out=dst, in_=src, func=AF.Copy, scale=gain_ap))
        if eng == "act":
            eng = "dve"
        e = nc.vector if eng == "dve" else nc.gpsimd
        return eng, (lambda: e.tensor_scalar(out=dst, in0=src, scalar1=gain_ap, scalar2=float(sign), op0=ALU.mult, op1=ALU.mult))

    def load_weight(self, stg, src_ap, src_buf, kchunks, ncols, gain_c0, pieces):
        S = self.S
        for kc in range(kchunks):
            st = stg[kc % len(stg)]
            S.dma("sp", st[:, 0:ncols], src_ap[kc * 128:(kc + 1) * 128, :], r=[src_buf], w=[st.b])
            g = None if gain_c0 is None else self.G[:, gain_c0 + kc:gain_c0 + kc + 1]
            for dst, src, sign, dbuf in pieces(kc, st):
                eng, fn = self.conv_piece(dst, src, g, sign)
                rr = [st.b] + ([self.G.b] if g is not None else [])
                S.op(eng, fn, r=rr, w=[dbuf], acc=True)

    def prep_ffn_gen(self, es, l, eng=None):
        nc, S, d, db = self.nc, self.S, self.d, self.db
        stg = [self.tile(es, "fst%d" % i, [128, DFF], F32) for i in range(2)]
        tmp = [self.tile(es, "ftm%d" % i, [128, DFF], BF16) for i in range(2)]
        gc0 = 8 + 16 * l
        for wname, dname in (("ffn_w_gate", "WG%d" % l), ("ffn_w_up", "WU%d" % l)):
            for kc in range(8):
                st, tm = stg[kc % 2], tmp[kc % 2]
                S.dma("sp", st[:, :], d[wname][l, kc * 128:(kc + 1) * 128, :], r=[db[wname]], w=[st.b])
                yield
                en, fn = self.conv_piece(tm[:, :], st[:, :], self.G[:, gc0 + kc:gc0 + kc + 1], eng=eng)
                S.op(en, fn, r=[st.b, self.G.b], w=[tm.b])
                S.dma("pool", d[dname].rearrange("f p c j -> p f c j")[:, :, kc, :],
                      tm[:, :].rearrange("p (f j) -> p f j", j=128), r=[tm.b], w=[db[dname]], acc=True)
                yield
        for fc in range(NFC):
            st, tm = stg[fc % 2], tmp[fc % 2]
            S.dma("sp", st[:, 0:D], d["ffn_w_down"][l, fc * 128:(fc + 1) * 128, :], r=[db["ffn_w_down"]], w=[st.b])
            yield
            en, fn = self.conv_piece(tm[:, 0:D], st[:, 0:D], None, eng=eng)
            S.op(en, fn, r=[st.b], w=[tm.b])
            S.dma("pool", d["WD%d" % l][fc, :, :], tm[:, 0:D], r=[tm.b], w=[db["WD%d" % l]], acc=True)
            yield

    def prep_ffn_weights(self, es0, l):
        with self.scope() as es:
            for _ in self.prep_ffn_gen(es, l):
                pass

    def norm_setup(self, es, nh=2, halo=False, nhb=2):
        self.nx = [self.tile(es, "nx%d" % i, [128, D], F32) for i in range(2)]
        self.nsq = self.tile(es, "nsq", [128, D], BF16)
        self.nhb = [self.tile(es, "nhb%d" % i, [128, D], BF16) for i in range(nhb)]
        self.nss = [self.tile(es, "nss%d" % i, [128, 2], F32) for i in range(2)]
        self.ncnt_a = 0
        self.ncnt_b = 0
        w = 514 if halo else 512
        self.hT = [self.tile(es, "hT%d" % i, [128, 8, w], BF16) for i in range(nh)]
        self.halo = halo
        self.ncnt = 0

    def norm_block(self, xname, blk, hT, ptr):
        for j in range(4):
            self.norm_block_a(xname, blk, [j])
            self.norm_block_b(hT, ptr, [j])

    def norm_block_a(self, xname, blk, js=range(4)):
        nc, S, d, db = self.nc, self.S, self.d, self.db
        for j in js:
            i = self.ncnt_a
            self.ncnt_a += 1
            xt, hb, ss = self.nx[i % 2], self.nhb[i % len(self.nhb)], self.nss[i % 2]
            r0 = blk * 512 + j * 128
            S.dma("sp", xt[:, :], d[xname][r0:r0 + 128, :], r=[db[xname]], w=[xt.b])
            S.op("act", lambda: nc.scalar.activation(out=self.nsq[:, :], in_=xt[:, :], func=AF.Square, accum_out=ss[:, 0:1]),
                 r=[xt.b], w=[self.nsq.b, ss.b])
            S.op("pool", lambda: nc.gpsimd.tensor_scalar(out=ss[:, 1:2], in0=ss[:, 0:1], scalar1=1.0 / D, scalar2=EPS,
                                                         op0=ALU.mult, op1=ALU.add), r=[ss.b], w=[ss.b])
            S.op("pool", lambda: nc.gpsimd.tensor_tensor(out=ss[:, 1:2], in0=ss[:, 1:2], in1=self.mhalf[:, :], op=ALU.pow),
                 r=[ss.b, self.mhalf.b], w=[ss.b])
            S.op("pool", lambda: nc.gpsimd.tensor_scalar(out=hb[:, :], in0=xt[:, :], scalar1=ss[:, 1:2], scalar2=1.0, op0=ALU.mult, op1=ALU.mult),
                 r=[xt.b, ss.b], w=[hb.b])

    def norm_block_b(self, hT, ptr, js=range(4)):
        nc, S = self.nc, self.S
        off = 1 if self.halo else 0
        for j in js:
            i = self.ncnt_b
            self.ncnt_b += 1
            hb, pt = self.nhb[i % len(self.nhb)], ptr[i % len(ptr)]
            for c in range(8):
                S.op("pe", lambda: nc.tensor.transpose(out=pt[:, c, :], in_=hb[:, c * 128:(c + 1) * 128], identity=self.ident[:, :]),
                     r=[hb.b, self.ident.b], w=[pt.b], acc=(c > 0))
            S.op("dve", lambda: nc.vector.tensor_copy(out=hT[:, :, off + j * 128:off + (j + 1) * 128], in_=pt[:, :, :]),
                 r=[pt.b], w=[hT.b], acc=True)

    def postnorm_residual(self, ps, xin_name, xout_name, r0, gi, bufs, k):
        nc, S, d, db = self.nc, self.S, self.d, self.db
        nb_ = len(bufs["xr"])
        xr, tt, ss = bufs["xr"][k % nb_], bufs["tt"][k % nb_], bufs["ss"][k % nb_]
        S.dma("sp", xr[:, :], d[xin_name][r0:r0 + 128, :], r=[db[xin_name]], w=[xr.b])
        S.op("act", lambda: nc.scalar.activation(out=bufs["junk"][:, :], in_=ps[:, :], func=AF.Square, accum_out=ss[:, 0:1]),
             r=[ps.b], w=[bufs["junk"].b, ss.b])
        S.op("dve", lambda: nc.vector.tensor_scalar(out=ss[:, 1:2], in0=ss[:, 0:1], scalar1=1.0 / D, scalar2=EPS,
                                                    op0=ALU.mult, op1=ALU.add), r=[ss.b], w=[ss.b])
        S.op("pool", lambda: nc.gpsimd.tensor_tensor(out=ss[:, 1:2], in0=ss[:, 1:2], in1=self.mhalf[:, :], op=ALU.pow),
             r=[ss.b, self.mhalf.b], w=[ss.b])
        S.op("dve", lambda: nc.vector.scalar_tensor_tensor(out=tt[:, :], in0=ps[:, :], scalar=ss[:, 1:2], in1=self.PG[:, gi, :],
                                                           op0=ALU.mult, op1=ALU.mult), r=[ps.b, ss.b, self.PG.b], w=[tt.b])
        S.op("dve", lambda: nc.vector.tensor_tensor(out=tt[:, :], in0=tt[:, :], in1=xr[:, :], op=ALU.add),
             r=[tt.b, xr.b], w=[tt.b])
        S.dma("pool", d[xout_name][r0:r0 + 128, :], tt[:, :], r=[tt.b], w=[db[xout_name]], acc=True)

    def postnorm_bufs(self, es, n=2):
        return {"xr": [self.tile(es, "pxr%d" % i, [128, D], F32) for i in range(n)],
                "tt": [self.tile(es, "ptt%d" % i, [128, D], F32) for i in range(n)],
                "ss": [self.tile(es, "pss%d" % i, [128, 2], F32) for i in range(n)],
                "junk": self.tile(es, "pjunk", [128, D], BF16)}

    def rstd_bc(self, pss, sq_list, n, out_t, rows=128):
        nc, S = self.nc, self.S
        for i, (ap, b) in enumerate(sq_list):
            S.op("pe", lambda: nc.tensor.matmul(out=pss[:, 0:512], lhsT=self.ones_bf[:, :], rhs=ap, start=(i == 0),
                                                stop=(i == len(sq_list) - 1)), r=[self.ones_bf.b, b], w=[pss.b], acc=(i > 0))
        S.op("act", lambda: nc.scalar.activation(out=out_t[:, 0:512], in_=pss[:, 0:512], func=AF.Ln, scale=1.0 / n, bias=self.epsb[:, 0:1]),
             r=[pss.b, self.epsb.b], w=[out_t.b])
        S.op("act", lambda: nc.scalar.activation(out=out_t[:, 0:512], in_=out_t[:, 0:512], func=AF.Exp, scale=-0.5), r=[out_t.b], w=[out_t.b])

    def phase_l0a(self):
        nc, S, d, db = self.nc, self.S, self.d, self.db
        with self.scope() as es:
            T = lambda n, s, dt: self.tile(es, n, s, dt)
            win = T("win0", [128, 8, 1984], BF16)
            wq = T("wq", [128, 2, 1024], BF16)
            wkv = T("wkv", [128, 1024], BF16)
            with self.scope() as es2:
                stg = [self.tile(es2, "wst%d" % i, [128, 1952], F32) for i in range(2)]

                def p_win(kc, st):
                    return [(win[:, kc, 0:1952], st[:, 0:1952], 1.0, win.b),
                            (win[:, kc, 1952:1968], st[:, 400:416], -1.0, win.b),
                            (win[:, kc, 1968:1984], st[:, 384:400], 1.0, win.b)]
                self.load_weight(stg, d["a_w_in"], db["a_w_in"], 8, 1952, 0, p_win)

                def p_wq(kc, st):
                    sv = st[:, 0:768].rearrange("p (h e) -> p h e", h=8)
                    dv = wq[:, kc, 768:1024].rearrange("p (h e) -> p h e", h=8)
                    return [(wq[:, kc, 0:768], st[:, 0:768], 1.0, wq.b),
                            (dv[:, :, 0:16], sv[:, :, 80:96], -1.0, wq.b),
                            (dv[:, :, 16:32], sv[:, :, 64:80], 1.0, wq.b)]
                self.load_weight(stg, d["a_w_q_up"], db["a_w_q_up"], 2, 768, 32, p_wq)
                self.load_weight(stg, d["a_w_kv_up"], db["a_w_kv_up"], 1, 1024, 34,
                                 lambda kc, st: [(wkv[:, :], st[:, 0:1024], 1.0, wkv.b)])
            self.norm_setup(es, nhb=4)
            ptr = [self.ptile(es, "ptr%d" % i, [128, 8, 128], BF16) for i in range(2)]
            pp = [self.ptile(es, "pp%d" % i, [128, 512], F32) for i in range(5)]
            pcnt = [0]

            def nextp():
                pcnt[0] += 1
                return pp[pcnt[0] % len(pp)]
            cs = [T("mcs%d" % i, [128, 2, 512], F32) for i in range(2)]
            cqT = [T("cqT%d" % i, [128, 2, 512], BF16) for i in range(2)]
            ckvT = [T("ckvT%d" % i, [128, 512], BF16) for i in range(2)]
            sqq = T("sqq", [128, 2, 512], BF16)
            sqkv = T("sqkv", [128, 512], BF16)
            rq_bc = T("rq_bc", [128, 512], F32)
            rkv_bc = T("rkv_bc", [128, 512], F32)
            rkv_col = T("rkv_col", [128, 8], F32)
            ev = [T("ev%d" % i, [128, 512], BF16) for i in range(4)]
            evc = [0]
            t1 = [T("rt1_%d" % i, [128, 512], F32) for i in range(2)]
            t2 = [T("rt2_%d" % i, [128, 512], F32) for i in range(2)]
            vsb = [T("vsb%d" % i, [128, 512], BF16) for i in range(2)]

            def nextev():
                evc[0] += 1
                return ev[evc[0] % len(ev)]

            def proj(ps_ap, ps_buf, hT, c0, n, first=True, last=True):
                for kc in range(8):
                    S.op("pe", lambda: nc.tensor.matmul(out=ps_ap, lhsT=win[:, kc, c0:c0 + n], rhs=hT[:, kc, 0:512],
                                                        start=(kc == 0), stop=(kc == 7)),
                         r=[win.b, hT.b], w=[ps_buf], acc=(kc > 0))

            for blk in range(NB + 1):
                if blk < NB:
                    self.norm_block_a("x", blk)
                    if blk == 0:
                        self.norm_block_b(self.hT[0], ptr)
                    c = cs[blk % 2]
                    S.dma("sp", c[64:96, :, :], d["mla_cs"][:, :, blk * 512:(blk + 1) * 512], r=[db["mla_cs"]], w=[c.b])
                if blk == 0:
                    continue
                b = blk - 1
                hT, c = self.hT[b % 2], cs[b % 2]
                tsl = slice(b * 512, (b + 1) * 512)
                cq, ckv = cqT[b % 2], ckvT[b % 2]
                for j in range(2):
                    ps = nextp()
                    proj(ps[:, :], ps.b, hT, j * 128, 128)
                    S.op("act", lambda: nc.scalar.copy(out=cq[:, j, :], in_=ps[:, :]), r=[ps.b], w=[cq.b], acc=(j > 0))
                    S.op("act", lambda: nc.scalar.activation(out=sqq[:, j, :], in_=ps[:, :], func=AF.Square), r=[ps.b],
                         w=[sqq.b], acc=(j > 0))
                ps = nextp()
                proj(ps[:, :], ps.b, hT, 256, 128)
                S.op("act", lambda: nc.scalar.copy(out=ckv[:, :], in_=ps[:, :]), r=[ps.b], w=[ckv.b])
                S.op("act", lambda: nc.scalar.activation(out=sqkv[:, :], in_=ps[:, :], func=AF.Square), r=[ps.b], w=[sqkv.b])
                ps = nextp()
                self.rstd_bc(ps, [(sqq[:, 0, :], sqq.b), (sqq[:, 1, :], sqq.b)], 256, rq_bc)
                ps = nextp()
                self.rstd_bc(ps, [(sqkv[:, :], sqkv.b)], 128, rkv_bc)
                ps = nextp()
                for j in range(4):
                    S.op("pe", lambda: nc.tensor.matmul(out=ps[:, j:j + 1], lhsT=sqkv[:, j * 128:(j + 1) * 128], rhs=self.ones_bf[:, 0:1],
                                                        start=True, stop=True), r=[sqkv.b, self.ones_bf.b], w=[ps.b], acc=(j > 0))
                S.op("dve", lambda: nc.vector.tensor_scalar(out=rkv_col[:, 0:4], in0=ps[:, 0:4], scalar1=1.0 / 128, scalar2=EPS,
                                                            op0=ALU.mult, op1=ALU.add), r=[ps.b], w=[rkv_col.b])
                S.op("act", lambda: nc.scalar.activation(out=rkv_col[:, 0:4], in_=rkv_col[:, 0:4], func=AF.Sqrt), r=[rkv_col.b], w=[rkv_col.b])
                S.op("dve", lambda: nc.vector.reciprocal(out=rkv_col[:, 0:4], in_=rkv_col[:, 0:4]), r=[rkv_col.b], w=[rkv_col.b])
                psa, psb = nextp(), nextp()
                proj(psa[64:96, :], psa.b, hT, 384, 32)
                proj(psb[64:96, :], psb.b, hT, 1952, 32)
                ta, tb, e = t1[0], t2[0], nextev()
                S.op("dve", lambda: nc.vector.tensor_tensor(out=ta[64:96, :], in0=psa[64:96, :], in1=c[64:96, 0, :], op=ALU.mult),
                     r=[psa.b, c.b], w=[ta.b])
                S.op("dve", lambda: nc.vector.tensor_tensor(out=tb[64:96, :], in0=psb[64:96, :], in1=c[64:96, 1, :], op=ALU.mult),
                     r=[psb.b, c.b], w=[tb.b])
                S.op("pool", lambda: nc.gpsimd.tensor_tensor(out=e[64:96, :], in0=ta[64:96, :], in1=tb[64:96, :], op=ALU.add),
                     r=[ta.b, tb.b], w=[e.b])
                S.dma("pool", d["KRT"][:, tsl], e[64:96, :], r=[e.b], w=[db["KRT"]], acc=True)
                for (c0, dn) in ((416, "NQT"), (928, "NKT")):
                    for j in range(4):
                        ps = nextp()
                        proj(ps[:, :], ps.b, hT, c0 + j * 128, 128)
                        e = nextev()
                        S.op("act" if j % 2 else "dve",
                             (lambda: nc.scalar.copy(out=e[:, :], in_=ps[:, :])) if j % 2 else
                             (lambda: nc.vector.tensor_copy(out=e[:, :], in_=ps[:, :])), r=[ps.b], w=[e.b])
                        S.dma("pool", d[dn][j * 128:(j + 1) * 128, tsl], e[:, :], r=[e.b], w=[db[dn]], acc=True)
                if blk < NB:
                    self.norm_block_b(self.hT[blk % 2], ptr)
                for j in range(4):
                    ps = nextp()
                    for kc in range(8):
                        S.op("pe", lambda: nc.tensor.matmul(out=ps[:, :], lhsT=hT[:, kc, j * 128:(j + 1) * 128], rhs=win[:, kc, 1440:1952],
                                                            start=(kc == 0), stop=(kc == 7)), r=[win.b, hT.b], w=[ps.b], acc=(kc > 0))
                    e = nextev()
                    S.op("act", lambda: nc.scalar.copy(out=e[:, :], in_=ps[:, :]), r=[ps.b], w=[e.b])
                    r0 = b * 512 + j * 128
                    S.dma("pool", d["NV"][r0:r0 + 128, :], e[:, :], r=[e.b], w=[db["NV"]], acc=True)
                for h in range(8):
                    psa, psb = nextp(), nextp()
                    for kc in range(2):
                        S.op("pe", lambda: nc.tensor.matmul(out=psa[0:96, :], lhsT=wq[:, kc, h * 96:(h + 1) * 96], rhs=cq[:, kc, :],
                                                            start=(kc == 0), stop=(kc == 1)), r=[wq.b, cq.b], w=[psa.b], acc=(kc > 0))
                    for kc in range(2):
                        S.op("pe", lambda: nc.tensor.matmul(out=psb[64:96, :], lhsT=wq[:, kc, 768 + h * 32:768 + (h + 1) * 32], rhs=cq[:, kc, :],
                                                            start=(kc == 0), stop=(kc == 1)), r=[wq.b, cq.b], w=[psb.b], acc=(kc > 0))
                    e, ta, tb = nextev(), t1[h % 2], t2[h % 2]
                    S.op("dve", lambda: nc.vector.tensor_tensor(out=e[0:64, :], in0=psa[0:64, :], in1=rq_bc[0:64, :], op=ALU.mult),
                         r=[psa.b, rq_bc.b], w=[e.b])
                    S.op("dve", lambda: nc.vector.tensor_tensor(out=ta[64:96, :], in0=psa[64:96, :], in1=c[64:96, 0, :], op=ALU.mult),
                         r=[psa.b, c.b], w=[ta.b])
                    S.op("dve", lambda: nc.vector.tensor_tensor(out=tb[64:96, :], in0=psb[64:96, :], in1=c[64:96, 1, :], op=ALU.mult),
                         r=[psb.b, c.b], w=[tb.b])
                    S.op("pool", lambda: nc.gpsimd.tensor_tensor(out=ta[64:96, :], in0=ta[64:96, :], in1=tb[64:96, :], op=ALU.add),
                         r=[ta.b, tb.b], w=[ta.b])
                    S.op("pool", lambda: nc.gpsimd.tensor_tensor(out=e[64:96, :], in0=ta[64:96, :], in1=rq_bc[64:96, :], op=ALU.mult),
                         r=[ta.b, rq_bc.b], w=[e.b], acc=True)
                    S.dma("pool", d["QT"][h, :, tsl], e[0:96, :], r=[e.b], w=[db["QT"]], acc=True)
                for h in range(8):
                    ps = nextp()
                    S.op("pe", lambda: nc.tensor.matmul(out=ps[0:64, :], lhsT=wkv[:, h * 128:h * 128 + 64], rhs=ckv[:, :], start=True, stop=True),
                         r=[wkv.b, ckv.b], w=[ps.b])
                    e = nextev()
                    S.op("dve", lambda: nc.vector.tensor_tensor(out=e[0:64, :], in0=ps[0:64, :], in1=rkv_bc[0:64, :], op=ALU.mult),
                         r=[ps.b, rkv_bc.b], w=[e.b])
                    S.dma("pool", d["KNT"][h, :, tsl], e[0:64, :], r=[e.b], w=[db["KNT"]], acc=True)
                wv = wkv[:, :].rearrange("p (h t e) -> p h t e", h=8, t=2)[:, :, 1, :]
                for j in range(4):
                    ps = nextp()
                    S.op("pe", lambda: nc.tensor.matmul(out=ps[:, :].rearrange("p (h e) -> p h e", h=8), lhsT=ckv[:, j * 128:(j + 1) * 128], rhs=wv,
                                                        start=True, stop=True), r=[wkv.b, ckv.b], w=[ps.b])
                    v = vsb[j % 2]
                    S.op("act", lambda: nc.scalar.activation(out=v[:, :], in_=ps[:, :], func=AF.Copy, scale=rkv_col[:, j:j + 1]),
                         r=[ps.b, rkv_col.b], w=[v.b])
                    r0 = b * 512 + j * 128
                    S.dma("pool", d["MV"][r0:r0 + 128, :], v[:, :], r=[v.b], w=[db["MV"]], acc=True)
                S.rotate()

    def attn_phase(self, mode, bg=None, bg_every=40):
        nc, S, d, db = self.nc, self.S, self.d, self.db
        H = 4 if mode == "ret" else 8
        dk = {"mla": 96, "na": 64, "ret": 128}[mode]
        dvw = 128 if mode == "ret" else 65
        scale = {"mla": 96 ** -0.5, "na": 64 ** -0.5, "ret": 1.0}[mode]
        with self.scope() as es:
            T = lambda n, s, dt: self.tile(es, n, s, dt)
            qT = [T("aq%d" % i, [128, S_LEN], BF16) for i in range(2)]
            kT = [T("ak%d" % i, [128, S_LEN], BF16) for i in range(2)]
            vw = 128 if mode == "na" else dvw
            vt = [T("av%d" % i, [128, NT, vw], BF16) for i in range(2)]
            if mode != "ret":
                for v in vt:
                    if mode == "na":
                        S.op("dve", lambda: nc.vector.memset(v[:, :, :], 0.0), w=[v.b])
                    S.op("dve", lambda: nc.vector.memset(v[:, :, 64:65], 1.0), w=[v.b])
            if mode == "na":
                for t__ in qT + kT:
                    S.op("pool", lambda: nc.gpsimd.memset(t__[64:128, :], 0.0), w=[t__.b])
                dk = 128
            NP = 6
            pT = [T("apT%d" % i, [128, 512], BF16) for i in range(NP)]
            pS = [self.ptile(es, "aps%d" % i, [128, 512], F32) for i in range(4)]
            pO = [self.ptile(es, "apo%d" % i, [128, 512], F32) for i in range(2)]
            pB = [self.ptile(es, "apb%d" % i, [128, 512], F32) for i in range(2)]
            osb = [T("aosb%d" % i, [128, 512], F32) for i in range(2)]
            oout = [T("aoo%d" % i, [128, 512], BF16) for i in range(2)]
            rhl = [T("arhl%d" % i, [128, 2, 512], BF16) for i in range(2)]
            if mode == "na":
                bias = [T("abias%d" % i, [128, 3, 8, 512], BF16) for i in range(2)]
                bstg = [T("abstg%d" % i, [128, 8, 512], F32) for i in range(2)]
                bcnt = [0]
            if mode == "ret":
                E = [T("aE%d" % i, [128, RET_M], BF16) for i in range(2)]
                lg = T("alg", [128, 16], F32)
                npst = [T("anp%d" % i, [128, 2, 2016], F32) for i in range(2)]
                et = [T("aet%d" % i, [128, 2016], F32) for i in range(2)]
                rg = [T("arg%d" % i, [128, 512], F32) for i in range(2)]
                sqo = [T("asqo%d" % i, [128, 512], BF16) for i in range(2)]
                rbc = [T("arbc%d" % i, [128, 512], F32) for i in range(2)]
                S.dma("sp", lg[:, 0:8], d["c_decay"].partition_broadcast(128), r=[db["c_decay"]], w=[lg.b])
                S.op("act", lambda: nc.scalar.activation(out=lg[:, 8:16], in_=lg[:, 0:8], func=AF.Exp, scale=-1.0), r=[lg.b], w=[lg.b])
                S.op("act", lambda: nc.scalar.activation(out=lg[:, 8:16], in_=lg[:, 8:16], func=AF.Ln, bias=1.0), r=[lg.b], w=[lg.b])
                S.op("dve", lambda: nc.vector.tensor_scalar(out=lg[:, 8:16], in0=lg[:, 8:16], scalar1=-1.0, scalar2=None, op0=ALU.mult),
                     r=[lg.b], w=[lg.b])
            cnt = {"s": 0, "p": 0, "np": 0}

            def load_head(h):
                q, k, v = qT[h % 2], kT[h % 2], vt[h % 2]
                if mode == "mla":
                    S.dma("sp", q[0:96, :], d["QT"][h, :, :], r=[db["QT"]], w=[q.b])
                    S.dma("sp", k[0:64, :], d["KNT"][h, :, :], r=[db["KNT"]], w=[k.b])
                    S.dma("sp", k[64:96, :], d["KRT"][:, :], r=[db["KRT"]], w=[k.b], acc=True)
                    S.dma("sp", v[:, :, 0:64], d["MV"].rearrange("(t p) (h e) -> p t h e", p=128, h=8)[:, :, h, :], r=[db["MV"]], w=[v.b], acc=True)
                elif mode == "na":
                    S.dma("sp", q[0:64, :], d["NQT"][h * 64:(h + 1) * 64, :], r=[db["NQT"]], w=[q.b], acc=True)
                    S.dma("sp", k[0:64, :], d["NKT"][h * 64:(h + 1) * 64, :], r=[db["NKT"]], w=[k.b], acc=True)
                    S.dma("sp", v[:, :, 0:64], d["NV"].rearrange("(t p) (h e) -> p t h e", p=128, h=8)[:, :, h, :], r=[db["NV"]], w=[v.b])
                    bt = bias[h % 2]
                    for pat in range(3):
                        st_ = bstg[bcnt[0] % 2]
                        bcnt[0] += 1
                        S.dma("sp", st_[:, :, :], d["na_bias"][h, :, pat, :, :], r=[db["na_bias"]], w=[st_.b])
                        S.op("pool", lambda: nc.gpsimd.tensor_scalar(out=bt[:, pat, :, :], in0=st_[:, :, :], scalar1=1.0 / scale, scalar2=1.0, op0=ALU.mult, op1=ALU.mult),
                             r=[st_.b], w=[bt.b], acc=(pat > 0))
                else:
                    S.dma("sp", q[:, :], d["RQT"][h, :, :], r=[db["RQT"]], w=[q.b])
                    S.dma("sp", k[:, :], d["RKT"][h, :, :], r=[db["RKT"]], w=[k.b])
                    S.dma("sp", v[:, :, :], d["RV"].rearrange("(t p) (h e) -> p t h e", p=128, h=4)[:, :, h, :], r=[db["RV"]], w=[v.b])
                    Eh = E[h % 2]
                    for cch in range(4):
                        st, e_ = npst[cnt["np"] % 2], et[cnt["np"] % 2]
                        cnt["np"] += 1
                        sl = slice(cch * 2016, (cch + 1) * 2016)
                        S.dma("sp", st[:, :, :], d["ret_np"][:, :, sl], r=[db["ret_np"]], w=[st.b])
                        S.op("pool", lambda: nc.gpsimd.tensor_scalar(out=e_[:, :], in0=st[:, 0, :], scalar1=lg[:, 8 + h:9 + h], scalar2=0.0, op0=ALU.mult, op1=ALU.add),
                             r=[st.b, lg.b], w=[e_.b])
                        S.op("dve", lambda: nc.vector.scalar_tensor_tensor(out=e_[:, :], in0=st[:, 1, :], scalar=lg[:, 12 + h:13 + h], in1=e_[:, :],
                                                                           op0=ALU.mult, op1=ALU.add), r=[st.b, lg.b, e_.b], w=[e_.b])
                        S.op("act", lambda: nc.scalar.activation(out=Eh[:, sl], in_=e_[:, :], func=AF.Exp, bias=self.lnscale[:, 0:1]),
                             r=[e_.b, self.lnscale.b], w=[Eh.b], acc=(cch > 0))

            if mode == "ret":
                self.lnscale = T("alnsc", [128, 1], F32)
                import math
                S.op("pool", lambda: nc.gpsimd.memset(self.lnscale[:, :], math.log(128 ** -0.5)), w=[self.lnscale.b])

            tasks = []
            ncr = na_col_ranges()
            for h in range(H):
                for qc in range(NB):
                    if mode == "na":
                        pat_ = 0 if qc == 0 else (2 if qc == 7 else 1)
                        kt0 = min(max(4 * qc - 2, 0), 24)
                        kts = [(kt0 + r_, r_, ncr[pat_][r_]) for r_ in range(8) if ncr[pat_][r_] is not None]
                    else:
                        kts = [(kt_, kt_, (0, 512)) for kt_ in range(NT)]
                    for i, (kt, krel, cr) in enumerate(kts):
                        tasks.append((h, qc, i, kt, len(kts), krel, cr))
            LA = 3
            sps = {}
            pending = []
            heads_loaded = [0]

            def ensure_head(h):
                while heads_loaded[0] <= min(h, H - 1):
                    load_head(heads_loaded[0])
                    heads_loaded[0] += 1

            def emit_s(n):
                h, qc, i, kt, nk, krel, (c_lo, c_hi) = tasks[n]
                ensure_head(h)
                q, k = qT[h % 2], kT[h % 2]
                ps = pS[n % 4]
                S.op("pe", lambda: nc.tensor.matmul(out=ps[:, c_lo:c_hi], lhsT=k[0:dk, kt * 128:(kt + 1) * 128],
                                                    rhs=q[0:dk, qc * 512 + c_lo:qc * 512 + c_hi],
                                                    start=True, stop=(mode != "na")), r=[k.b, q.b], w=[ps.b])
                if mode == "na":
                    pat = 0 if qc == 0 else (2 if qc == 7 else 1)
                    bt = bias[h % 2]
                    S.op("pe", lambda: nc.tensor.matmul(out=ps[:, c_lo:c_hi], lhsT=self.ident[:, :], rhs=bt[:, pat, krel, c_lo:c_hi], start=False, stop=True),
                         r=[self.ident.b, bt.b], w=[ps.b], acc=True)
                sps[n] = ps

            ensure_head(0)
            for n in range(min(LA, len(tasks))):
                emit_s(n)
            f = -1
            bgen = bg(es) if bg is not None else None
            for n, (h, qc, i, kt, nk, krel, (c_lo, c_hi)) in enumerate(tasks):
                if bgen is not None and n % bg_every == bg_every - 1:
                    if next(bgen, "done") == "done":
                        bgen = None
                if i == 0:
                    f += 1
                    if qc == 0:
                        ensure_head(h + 1)
                if n + LA < len(tasks):
                    emit_s(n + LA)
                qsl = slice(qc * 512, (qc + 1) * 512)
                v = vt[h % 2]
                po = pO[f % 2]
                ps = sps.pop(n)
                p = pT[n % NP]
                if mode != "ret":
                    S.op("act", lambda: nc.scalar.activation(out=p[:, c_lo:c_hi], in_=ps[:, c_lo:c_hi], func=AF.Exp, scale=scale), r=[ps.b], w=[p.b])
                else:
                    Eh = E[h % 2]
                    c0 = RET_OFF - (kt * 128 - qc * 512)
                    S.op("dve", lambda: nc.vector.tensor_tensor(out=p[:, :], in0=ps[:, :], in1=Eh[:, c0:c0 + 512], op=ALU.mult),
                         r=[ps.b, Eh.b], w=[p.b])
                S.op("pe", lambda: nc.tensor.matmul(out=po[0:vw, c_lo:c_hi], lhsT=v[:, kt, :], rhs=p[:, c_lo:c_hi], start=(i == 0), stop=(i == nk - 1),
                                                    skip_group_check=(mode == "na")),
                     r=[v.b, p.b], w=[po.b], acc=(i > 0))
                for stage in (1, 3, 5):
                    if i == stage and pending and pending[0][0] == stage:
                        pending.pop(0)[1]()
                if i < nk - 1:
                    continue
                ob, oo, pb = osb[f % 2], oout[f % 2], pB[f % 2]
                if mode != "ret":
                    rr_ = rhl[f % 2]
                    S.op("act", lambda: nc.scalar.activation(out=ob[64:65, :], in_=po[64:65, :], func=AF.Ln), r=[po.b], w=[ob.b])
                    S.op("act", lambda: nc.scalar.activation(out=ob[64:65, :], in_=ob[64:65, :], func=AF.Exp, scale=-1.0), r=[ob.b], w=[ob.b])

                    def st1(ob=ob, po=po, rr_=rr_):
                        S.op("dve", lambda: nc.vector.tensor_copy(out=ob[0:64, :], in_=po[0:64, :]), r=[po.b], w=[ob.b], acc=True)
                        S.op("dve", lambda: nc.vector.tensor_copy(out=rr_[64:65, 0, :], in_=ob[64:65, :]), r=[ob.b], w=[rr_.b])
                        S.op("dve", lambda: nc.vector.tensor_tensor(out=rr_[64:65, 1, :], in0=ob[64:65, :], in1=rr_[64:65, 0, :], op=ALU.subtract),
                             r=[ob.b, rr_.b], w=[rr_.b])

                    def st3(pb=pb, rr_=rr_):
                        for t_ in range(2):
                            S.op("pe", lambda: nc.tensor.matmul(out=pb[0:64, :], lhsT=self.ones_bf[64:65, 0:64], rhs=rr_[64:65, t_, :],
                                                                start=(t_ == 0), stop=(t_ == 1)), r=[self.ones_bf.b, rr_.b], w=[pb.b], acc=(t_ > 0))

                    def st5(ob=ob, oo=oo, pb=pb, h=h, qsl=qsl):
                        S.op("dve", lambda: nc.vector.tensor_tensor(out=oo[0:64, :], in0=ob[0:64, :], in1=pb[0:64, :], op=ALU.mult),
                             r=[ob.b, pb.b], w=[oo.b])
                        row0 = (0 if mode == "mla" else 512) + h * 64
                        S.dma("pool", d["CT"][row0:row0 + 64, qsl], oo[0:64, :], r=[oo.b], w=[db["CT"]], acc=True)
                else:
                    g = rg[f % 2]
                    sq_ = sqo[f % 2]
                    rb = rbc[f % 2]
                    S.dma("sp", g[:, :], d["RG"][h * 128:(h + 1) * 128, qsl], r=[db["RG"]], w=[g.b])
                    S.op("act", lambda: nc.scalar.copy(out=ob[:, :], in_=po[:, :]), r=[po.b], w=[ob.b])
                    S.op("act", lambda: nc.scalar.activation(out=sq_[:, :], in_=po[:, :], func=AF.Square), r=[po.b], w=[sq_.b])

                    def st1(ob=ob, g=g):
                        S.op("pool", lambda: nc.gpsimd.tensor_tensor(out=ob[:, :], in0=ob[:, :], in1=g[:, :], op=ALU.mult), r=[ob.b, g.b], w=[ob.b])

                    def st3(pb=pb, sq_=sq_, rb=rb):
                        self.rstd_bc(pb, [(sq_[:, :], sq_.b)], 128, rb)

                    def st5(ob=ob, oo=oo, rb=rb, h=h, qsl=qsl):
                        S.op("dve", lambda: nc.vector.tensor_tensor(out=oo[:, :], in0=ob[:, :], in1=rb[:, :], op=ALU.mult), r=[ob.b, rb.b], w=[oo.b])
                        S.dma("pool", d["C1T"][h * 128:(h + 1) * 128, qsl], oo[:, :], r=[oo.b], w=[db["C1T"]], acc=True)
                pending.extend([(1, st1), (3, st3), (5, st5)])
                if qc == NB - 1:
                    S.rotate()
            while pending:
                pending.pop(0)[1]()
            if bgen is not None:
                for _ in bgen:
                    pass

    def phase_outproj(self, cname, wname, xin, xout, gi):
        nc, S, d, db = self.nc, self.S, self.d, self.db
        with self.scope() as es:
            T = lambda n, s, dt: self.tile(es, n, s, dt)
            wo = T("wo", [128, 8, D], BF16)
            with self.scope() as es2:
                stg = [self.tile(es2, "ost%d" % i, [128, D], F32) for i in range(2)]
                self.load_weight(stg, d[wname], db[wname], 8, D, None, lambda kc, st: [(wo[:, kc, :], st[:, 0:D], 1.0, wo.b)])
            cT = [T("ocT%d" % i, [128, 8, 512], BF16) for i in range(2)]
            pm = [self.ptile(es, "opm%d" % i, [128, D], F32) for i in range(4)]
            pb = self.postnorm_bufs(es, n=4)
            k = 0
            for blk in range(NB + 1):
                if blk < NB:
                    c = cT[blk % 2]
                    S.dma("sp", c[:, :, :], d[cname].rearrange("(c p) t -> p c t", p=128)[:, :, blk * 512:(blk + 1) * 512], r=[db[cname]], w=[c.b])
                if blk == 0:
                    continue
                b = blk - 1
                c = cT[b % 2]
                for j in range(4):
                    ps = pm[k % 4]
                    for n in range(2):
                        for kc in range(8):
                            S.op("pe", lambda: nc.tensor.matmul(out=ps[:, n * 512:(n + 1) * 512], lhsT=c[:, kc, j * 128:(j + 1) * 128],
                                                                rhs=wo[:, kc, n * 512:(n + 1) * 512], start=(kc == 0), stop=(kc == 7)),
                                 r=[c.b, wo.b], w=[ps.b], acc=(kc > 0 or n > 0))
                    self.postnorm_residual(ps, xin, xout, b * 512 + j * 128, gi, pb, k)
                    k += 1
                S.rotate()

    def phase_ffn(self, l, xin, xout):
        nc, S, d, db = self.nc, self.S, self.d, self.db
        gi = 1 + 2 * l
        with self.scope() as es:
            T = lambda n, s, dt: self.tile(es, n, s, dt)
            wd = T("fwd", [128, NFC, D], BF16)
            S.dma("sp", wd[:, :, :], d["WD%d" % l].rearrange("f p n -> p f n"), r=[db["WD%d" % l]], w=[wd.b])
            cw = T("fcw", [128, NFC, 3], F32)
            for t_ in range(3):
                S.dma("sp", cw[:, :, t_], d["ffn_conv"][l, t_, :].rearrange("(c p) -> p c", p=128), r=[db["ffn_conv"]], w=[cw.b],
                      acc=(t_ > 0), allow_slow_non_contiguous=True)
            self.norm_setup(es, nh=3, halo=True, nhb=4)
            for hT in self.hT:
                S.op("dve", lambda: nc.vector.memset(hT[:, :, :], 0.0), w=[hT.b])
            ptr = [self.ptile(es, "fptr", [128, 8, 128], BF16)]
            pg = [self.ptile(es, "fpg%d" % i, [128, 512], F32) for i in range(2)]
            pu = [self.ptile(es, "fpu%d" % i, [128, 512], F32) for i in range(2)]
            ph = self.ptile(es, "fph", [128, 512], F32)
            phb = [ph.b, ph.b]
            pdn = self.ptile(es, "fpd", [128, D], F32)
            wgu = [T("fwgu%d" % i, [128, 2, 8, 128], BF16) for i in range(4)]
            gsb = [T("fg%d" % i, [128, 514], F32) for i in range(3)]
            csb = [T("fc%d" % i, [128, 512], F32) for i in range(3)]
            gel = [T("fge%d" % i, [128, 512], F32) for i in range(3)]
            aTs = [T("faT%d" % i, [128, NFC, 512], BF16) for i in range(2)]
            pb = self.postnorm_bufs(es)
            k = 0
            wcnt = 0
            def halo_x(bl, br):
                hl, hr = self.hT[bl % 3], self.hT[br % 3]
                S.op("pool", lambda: nc.gpsimd.tensor_copy(out=hl[:, :, 513:514], in_=hr[:, :, 1:2]), r=[hr.b], w=[hl.b], acc=True)
                S.op("pool", lambda: nc.gpsimd.tensor_copy(out=hr[:, :, 0:1], in_=hl[:, :, 512:513]), r=[hl.b], w=[hr.b], acc=True)
                if br == NB - 1:
                    S.op("dve", lambda: nc.vector.memset(hr[:, :, 513:514], 0.0), w=[hr.b], acc=True)
            self.norm_block_a(xin, 0)
            self.norm_block_b(self.hT[0], ptr)
            self.norm_block_a(xin, 1)
            self.norm_block_b(self.hT[1], ptr)
            halo_x(0, 1)
            kcnt = [0]

            def emit_down_mm(bb, j):
                aT_ = aTs[bb % 2]
                for n in range(2):
                    for fc_ in range(NFC):
                        S.op("pe", lambda: nc.tensor.matmul(out=pdn[:, n * 512:(n + 1) * 512], lhsT=aT_[:, fc_, j * 128:(j + 1) * 128],
                                                            rhs=wd[:, fc_, n * 512:(n + 1) * 512], start=(fc_ == 0), stop=(fc_ == NFC - 1)),
                             r=[aT_.b, wd.b], w=[pdn.b], acc=(fc_ > 0 or n > 0))

            def emit_down_pn(bb, j):
                self.postnorm_residual(pdn, xin, xout, bb * 512 + j * 128, gi, pb, kcnt[0])
                kcnt[0] += 1

            for b in range(NB):
                hT = self.hT[b % 3]
                aT = aTs[b % 2]
                for fc in range(NFC):
                    if b >= 1 and fc in (2, 7, 12, 17):
                        emit_down_mm(b - 1, (2, 7, 12, 17).index(fc))
                    if b >= 1 and fc in (5, 10, 15, 20):
                        emit_down_pn(b - 1, (5, 10, 15, 20).index(fc))
                    if b + 2 < NB and fc in (1, 6, 11, 16):
                        self.norm_block_a(xin, b + 2, [(1, 6, 11, 16).index(fc)])
                    if b + 2 < NB and fc in (4, 9, 14, 19):
                        self.norm_block_b(self.hT[(b + 2) % 3], ptr, [(4, 9, 14, 19).index(fc)])
                        if fc == 19:
                            halo_x(b + 1, b + 2)
                    w = wgu[wcnt % 4]
                    wcnt += 1
                    S.dma("sp", w[:, 0, :, :], d["WG%d" % l][fc], r=[db["WG%d" % l]], w=[w.b])
                    S.dma("sp", w[:, 1, :, :], d["WU%d" % l][fc], r=[db["WU%d" % l]], w=[w.b], acc=True)
                    g, u = pg[fc % 2], pu[fc % 2]
                    for kc in range(8):
                        S.op("pe", lambda: nc.tensor.matmul(out=g[:, :], lhsT=w[:, 0, kc, :], rhs=hT[:, kc, 1:513], start=(kc == 0), stop=(kc == 7)),
                             r=[w.b, hT.b], w=[g.b], acc=(kc > 0))
                    for kc in range(8):
                        S.op("pe", lambda: nc.tensor.matmul(out=u[:, :], lhsT=w[:, 1, kc, :], rhs=hT[:, kc, 1:513], start=(kc == 0), stop=(kc == 7)),
                             r=[w.b, hT.b], w=[u.b], acc=(kc > 0))
                    hb_ = phb[fc % 2]
                    if fc < NFC - 1 or b + 1 >= NB or True:
                        for kc in range(8):
                            S.op("pe", lambda: nc.tensor.matmul(out=ph[:, 2 * fc:2 * fc + 2], lhsT=w[:, 0, kc, :], rhs=hT[:, kc, 0:514:513],
                                                                start=(kc == 0), stop=(kc == 7)), r=[w.b, hT.b], w=[hb_], acc=(kc > 0))
                    gs, cs_, ge = gsb[fc % 3], csb[fc % 3], gel[fc % 3]
                    S.op("act", lambda: nc.scalar.copy(out=gs[:, 1:513], in_=g[:, :]), r=[g.b], w=[gs.b])
                    S.op("act", lambda: nc.scalar.activation(out=cs_[:, :], in_=g[:, :], func=AF.Copy, scale=cw[:, fc, 1:2]), r=[g.b, cw.b], w=[cs_.b])
                    S.op("act", lambda: nc.scalar.copy(out=gs[:, 0:514:513], in_=ph[:, 2 * fc:2 * fc + 2]), r=[hb_], w=[gs.b], acc=True)
                    S.op("dve", lambda: nc.vector.scalar_tensor_tensor(out=cs_[:, :], in0=gs[:, 0:512], scalar=cw[:, fc, 0:1], in1=cs_[:, :],
                                                                       op0=ALU.mult, op1=ALU.add), r=[gs.b, cw.b, cs_.b], w=[cs_.b])
                    S.op("dve", lambda: nc.vector.scalar_tensor_tensor(out=cs_[:, :], in0=gs[:, 2:514], scalar=cw[:, fc, 2:3], in1=cs_[:, :],
                                                                       op0=ALU.mult, op1=ALU.add), r=[gs.b, cw.b, cs_.b], w=[cs_.b])
                    S.op("act", lambda: nc.scalar.activation(out=ge[:, :], in_=cs_[:, :], func=AF.Gelu_apprx_tanh), r=[cs_.b], w=[ge.b])
                    S.op("dve", lambda: nc.vector.tensor_tensor(out=aT[:, fc, :], in0=u[:, :], in1=ge[:, :], op=ALU.mult),
                         r=[u.b, ge.b], w=[aT.b], acc=(fc > 0))
                S.rotate()
            for j in range(4):
                emit_down_mm(NB - 1, j)
                emit_down_pn(NB - 1, j)

    def phase_l1a(self):
        nc, S, d, db = self.nc, self.S, self.d, self.db
        with self.scope() as es:
            T = lambda n, s, dt: self.tile(es, n, s, dt)
            win = T("win1", [128, 8, 4608], BF16)
            with self.scope() as es2:
                stg = [self.tile(es2, "w1st%d" % i, [128, 3584], F32) for i in range(2)]

                def p_win(kc, st):
                    out = [(win[:, kc, 0:3584], st[:, 0:3584], 1.0, win.b)]
                    for (s0, d0) in ((0, 3584), (512, 4096)):
                        sv = st[:, s0:s0 + 512].rearrange("p (h t e) -> p h t e", h=4, t=2)
                        dv = win[:, kc, d0:d0 + 512].rearrange("p (h t e) -> p h t e", h=4, t=2)
                        out.append((dv[:, :, 0, :], sv[:, :, 1, :], -1.0, win.b))
                        out.append((dv[:, :, 1, :], sv[:, :, 0, :], 1.0, win.b))
                    return out
                self.load_weight(stg, d["c_w_in"], db["c_w_in"], 8, 3584, 16, p_win)
            self.norm_setup(es, nhb=4)
            ptr = [self.ptile(es, "l1ptr%d" % i, [128, 8, 128], BF16) for i in range(2)]
            pp = [self.ptile(es, "l1pp%d" % i, [128, 512], F32) for i in range(6)]
            pcnt = [0]

            def nextp():
                pcnt[0] += 1
                return pp[pcnt[0] % len(pp)]
            cs = [T("rcs%d" % i, [128, 2, 512], F32) for i in range(2)]
            t1 = [T("l1t1_%d" % i, [128, 512], F32) for i in range(2)]
            t2 = [T("l1t2_%d" % i, [128, 512], F32) for i in range(2)]
            ev = [T("l1ev%d" % i, [128, 512], BF16) for i in range(4)]
            evf = [T("l1evf%d" % i, [128, 512], F32) for i in range(4)]
            zt = T("l1z", [128, 1536], F32)
            S.op("pool", lambda: nc.gpsimd.memset(zt[:, :], 0.0), w=[zt.b])
            S.dma("pool", d["HY"][0:1, :], zt[0:1, :], r=[zt.b], w=[db["HY"]], acc=True)
            S.dma("pool", d["HY"][S_LEN + 1:S_LEN + 2, :], zt[0:1, :], r=[zt.b], w=[db["HY"]], acc=True)
            ec = [0]

            def proj(ps, hT, c0):
                for kc in range(8):
                    S.op("pe", lambda: nc.tensor.matmul(out=ps[:, :], lhsT=win[:, kc, c0:c0 + 128], rhs=hT[:, kc, 0:512],
                                                        start=(kc == 0), stop=(kc == 7)), r=[win.b, hT.b], w=[ps.b], acc=(kc > 0))

            for blk in range(NB + 1):
                if blk < NB:
                    self.norm_block_a("X2", blk)
                    if blk == 0:
                        self.norm_block_b(self.hT[0], ptr)
                    c = cs[blk % 2]
                    S.dma("sp", c[:, :, :], d["ret_cs"][:, :, blk * 512:(blk + 1) * 512], r=[db["ret_cs"]], w=[c.b])
                if blk == 0:
                    continue
                b = blk - 1
                hT, c = self.hT[b % 2], cs[b % 2]
                tsl = slice(b * 512, (b + 1) * 512)
                for (c0, r0c, dn) in ((0, 3584, "RQT"), (512, 4096, "RKT")):
                    for h in range(4):
                        psa, psb = nextp(), nextp()
                        proj(psa, hT, c0 + h * 128)
                        proj(psb, hT, r0c + h * 128)
                        ta, tb = t1[h % 2], t2[h % 2]
                        ec[0] += 1
                        e = ev[ec[0] % 4]
                        S.op("dve", lambda: nc.vector.tensor_tensor(out=ta[:, :], in0=psa[:, :], in1=c[:, 0, :], op=ALU.mult), r=[psa.b, c.b], w=[ta.b])
                        S.op("dve", lambda: nc.vector.tensor_tensor(out=tb[:, :], in0=psb[:, :], in1=c[:, 1, :], op=ALU.mult), r=[psb.b, c.b], w=[tb.b])
                        S.op("pool", lambda: nc.gpsimd.tensor_tensor(out=e[:, :], in0=ta[:, :], in1=tb[:, :], op=ALU.add), r=[ta.b, tb.b], w=[e.b])
                        S.dma("pool", d[dn][h, :, tsl], e[:, :], r=[e.b], w=[db[dn]], acc=True)
                if blk < NB:
                    self.norm_block_b(self.hT[blk % 2], ptr)
                for j in range(4):
                    ps = nextp()
                    proj(ps, hT, 1536 + j * 128)
                    ec[0] += 1
                    e = evf[ec[0] % 4]
                    S.op("act", lambda: nc.scalar.activation(out=e[:, :], in_=ps[:, :], func=AF.Silu), r=[ps.b], w=[e.b])
                    S.dma("pool", d["RG"][j * 128:(j + 1) * 128, tsl], e[:, :], r=[e.b], w=[db["RG"]], acc=True)
                for j in range(4):
                    r0 = b * 512 + j * 128
                    for n in range(4):
                        ps = nextp()
                        c0 = 1024 if n == 0 else 2048 + (n - 1) * 512
                        for kc in range(8):
                            S.op("pe", lambda: nc.tensor.matmul(out=ps[:, :], lhsT=hT[:, kc, j * 128:(j + 1) * 128], rhs=win[:, kc, c0:c0 + 512],
                                                                start=(kc == 0), stop=(kc == 7)), r=[win.b, hT.b], w=[ps.b], acc=(kc > 0))
                        ec[0] += 1
                        if n == 0:
                            e = ev[ec[0] % 4]
                            S.op("act", lambda: nc.scalar.copy(out=e[:, :], in_=ps[:, :]), r=[ps.b], w=[e.b])
                            S.dma("pool", d["RV"][r0:r0 + 128, :], e[:, :], r=[e.b], w=[db["RV"]], acc=True)
                        else:
                            e = evf[ec[0] % 4]
                            if n % 2:
                                S.op("act", lambda: nc.scalar.copy(out=e[:, :], in_=ps[:, :]), r=[ps.b], w=[e.b])
                            else:
                                S.op("dve", lambda: nc.vector.tensor_copy(out=e[:, :], in_=ps[:, :]), r=[ps.b], w=[e.b])
                            S.dma("pool", d["HY"][1 + r0:1 + r0 + 128, (n - 1) * 512:n * 512], e[:, :], r=[e.b], w=[db["HY"]], acc=True)
                S.rotate()

    def phase_hyena(self):
        nc, S, d, db = self.nc, self.S, self.d, self.db
        with self.scope() as es:
            T = lambda n, s, dt: self.tile(es, n, s, dt)
            zT = T("hzT", [128, S_LEN], F32)
            S.dma("sp", zT[0:33, :], d["hy_zT"][:, :], r=[db["hy_zT"]], w=[zT.b])
            w1 = T("hw1", [128, 64], F32); w2 = T("hw2", [128, 64], F32); w3 = T("hw3", [128, 64], F32); w4 = T("hw4", [128, 1024], F32)
            S.dma("sp", w1[0:33, :], d["c_filt_w1"][:, :], r=[db["c_filt_w1"]], w=[w1.b])
            S.dma("sp", w2[0:64, :], d["c_filt_w2"][:, :], r=[db["c_filt_w2"]], w=[w2.b])
            S.dma("sp", w3[0:64, :], d["c_filt_w3"][:, :], r=[db["c_filt_w3"]], w=[w3.b])
            S.dma("sp", w4[0:64, :], d["c_filt_w4"][:, :], r=[db["c_filt_w4"]], w=[w4.b])
            fb = T("hfb", [128, 8], F32)
            for i, nm in enumerate(["c_filt_freq", "c_filt_b1", "c_filt_b2", "c_filt_b3"]):
                S.dma("sp", fb[0:64, i:i + 1], d[nm].rearrange("(p o) -> p o", o=1), r=[db[nm]], w=[fb.b], acc=(i > 0))
            S.op("dve", lambda: nc.vector.tensor_scalar(out=fb[0:64, 4:5], in0=fb[0:64, 0:1], scalar1=1.0 / 3.0, scalar2=None, op0=ALU.mult),
                 r=[fb.b], w=[fb.b])
            S.op("dve", lambda: nc.vector.tensor_scalar(out=fb[0:64, 5:8], in0=fb[0:64, 1:4], scalar1=fb[0:64, 4:5], scalar2=None, op0=ALU.mult),
                 r=[fb.b], w=[fb.b])
            h3T = T("hh3T", [128, S_LEN], F32)
            hA = [T("hhA%d" % i, [128, 512], F32) for i in range(2)]
            hB = [T("hhB%d" % i, [128, 512], F32) for i in range(2)]
            sS = [T("hsS%d" % i, [128, 512], F32) for i in range(2)]
            tS = [T("htS%d" % i, [128, 512], F32) for i in range(2)]
            pp = [self.ptile(es, "hpp%d" % i, [128, 512], F32) for i in range(4)]
            pc = [0]

            def nextp():
                pc[0] += 1
                return pp[pc[0] % 4]

            def sin3(ps, li, out_ap, out_buf, k, acc):
                s_, t_ = sS[k % 2], tS[k % 2]
                S.op("act", lambda: nc.scalar.activation(out=s_[0:64, :], in_=ps[0:64, :], func=AF.Sin, scale=fb[0:64, 4:5], bias=fb[0:64, 4 + li:5 + li]),
                     r=[ps.b, fb.b], w=[s_.b])
                S.op("pool", lambda: nc.gpsimd.tensor_tensor(out=t_[0:64, :], in0=s_[0:64, :], in1=s_[0:64, :], op=ALU.mult), r=[s_.b], w=[t_.b])
                S.op("dve", lambda: nc.vector.tensor_scalar(out=t_[0:64, :], in0=t_[0:64, :], scalar1=-4.0, scalar2=3.0, op0=ALU.mult, op1=ALU.add),
                     r=[t_.b], w=[t_.b])
                S.op("dve", lambda: nc.vector.tensor_tensor(out=out_ap, in0=t_[0:64, :], in1=s_[0:64, :], op=ALU.mult), r=[t_.b, s_.b], w=[out_buf], acc=acc)

            for c in range(8):
                csl = slice(c * 512, (c + 1) * 512)
                ps = nextp()
                S.op("pe", lambda: nc.tensor.matmul(out=ps[0:64, :], lhsT=w1[0:33, :], rhs=zT[0:33, csl], start=True, stop=True), r=[w1.b, zT.b], w=[ps.b])
                a, b_ = hA[c % 2], hB[c % 2]
                sin3(ps, 1, a[0:64, :], a.b, 3 * c, False)
                ps = nextp()
                S.op("pe", lambda: nc.tensor.matmul(out=ps[0:64, :], lhsT=w2[0:64, :], rhs=a[0:64, :], start=True, stop=True), r=[w2.b, a.b], w=[ps.b])
                sin3(ps, 2, b_[0:64, :], b_.b, 3 * c + 1, False)
                ps = nextp()
                S.op("pe", lambda: nc.tensor.matmul(out=ps[0:64, :], lhsT=w3[0:64, :], rhs=b_[0:64, :], start=True, stop=True), r=[w3.b, b_.b], w=[ps.b])
                sin3(ps, 3, h3T[0:64, csl], h3T.b, 3 * c + 2, c > 0)
            win = [T("hwin%d" % i, [128, 512], F32) for i in range(2)]
            xo = [T("hxo%d" % i, [128, 512], BF16) for i in range(4)]
            zb = T("hzb", [128, 512], BF16)
            S.op("pool", lambda: nc.gpsimd.memset(zb[:, :], 0.0), w=[zb.b])
            S.dma("pool", d["XD"][S_LEN - 1:S_LEN, 1024:1536], zb[0:1, :], r=[zb.b], w=[db["XD"]], acc=True)
            k = 0
            for j in range(NT):
                wt = win[j % 2]
                S.dma("sp", wt[:, :], d["hy_win"][j * 128:(j + 1) * 128, :], r=[db["hy_win"]], w=[wt.b])
                for n in range(2):
                    ps = nextp()
                    S.op("pe", lambda: nc.tensor.matmul(out=ps[:, :], lhsT=h3T[0:64, j * 128:(j + 1) * 128], rhs=w4[0:64, n * 512:(n + 1) * 512],
                                                        start=True, stop=True), r=[h3T.b, w4.b], w=[ps.b])
                    o = xo[k % 4]
                    k += 1
                    S.op("dve", lambda: nc.vector.tensor_tensor(out=o[:, :], in0=ps[:, :], in1=wt[:, :], op=ALU.mult), r=[ps.b, wt.b], w=[o.b])
                    if n == 0:
                        S.dma("pool", d["XD"][j * 128:(j + 1) * 128, 512:1024], o[:, :], r=[o.b], w=[db["XD"]], acc=True)
                    elif j == 0:
                        S.dma("pool", d["XD"][0:127, 1024:1536], o[1:128, :], r=[o.b], w=[db["XD"]], acc=True)
                    else:
                        S.dma("pool", d["XD"][j * 128 - 1:j * 128 + 127, 1024:1536], o[:, :], r=[o.b], w=[db["XD"]], acc=True)
        def conv_gen(es):
            T = lambda n, s, dt: self.tile(es, n, s, dt)
            cwb = T("hcwb", [128, 3, 1536], F32)
            for t_ in range(3):
                S.dma("sp", cwb[:, t_, :], d["c_short_conv"][t_, :].partition_broadcast(128), r=[db["c_short_conv"]], w=[cwb.b], acc=(t_ > 0))
            bb = T("hbb", [128, 512], F32)
            S.dma("sp", bb[:, :], d["c_hy_bias"].partition_broadcast(128), r=[db["c_hy_bias"]], w=[bb.b])
            sh = [[T("hsh%d_%d" % (i, t_), [128, 1536], F32) for t_ in range(3)] for i in range(2)]
            za = [T("hza%d" % i, [128, 1536], F32) for i in range(1)]
            zc = [T("hzc%d" % i, [128, 1536], F32) for i in range(1)]
            uu = [T("huu%d" % i, [128, 512], F32) for i in range(2)]
            ub = [T("hub%d" % i, [128, 512], F32) for i in range(2)]
            ubf = [T("hubf%d" % i, [128, 512], BF16) for i in range(2)]
            for j in range(NT):
                s3, a, c_ = sh[j % 2], za[0], zc[0]
                for t_ in range(3):
                    S.dma("sp", s3[t_][:, :], d["HY"][j * 128 + t_:j * 128 + t_ + 128, :], r=[db["HY"]], w=[s3[t_].b])
                yield
                for (en, E_, c0_, c1_) in (("dve", nc.vector, 0, 1152), ("pool", nc.gpsimd, 1152, 1536)):
                    S.op(en, lambda: E_.tensor_tensor(out=a[:, c0_:c1_], in0=s3[0][:, c0_:c1_], in1=cwb[:, 0, c0_:c1_], op=ALU.mult),
                         r=[s3[0].b, cwb.b], w=[a.b], acc=(c0_ > 0))
                    S.op(en, lambda: E_.tensor_tensor(out=c_[:, c0_:c1_], in0=s3[1][:, c0_:c1_], in1=cwb[:, 1, c0_:c1_], op=ALU.mult),
                         r=[s3[1].b, cwb.b], w=[c_.b], acc=(c0_ > 0))
                    S.op(en, lambda: E_.tensor_tensor(out=a[:, c0_:c1_], in0=a[:, c0_:c1_], in1=c_[:, c0_:c1_], op=ALU.add), r=[a.b, c_.b], w=[a.b], acc=True)
                    S.op(en, lambda: E_.tensor_tensor(out=c_[:, c0_:c1_], in0=s3[2][:, c0_:c1_], in1=cwb[:, 2, c0_:c1_], op=ALU.mult),
                         r=[s3[2].b, cwb.b], w=[c_.b], acc=True)
                    S.op(en, lambda: E_.tensor_tensor(out=a[:, c0_:c1_], in0=a[:, c0_:c1_], in1=c_[:, c0_:c1_], op=ALU.add), r=[a.b, c_.b], w=[a.b], acc=True)
                u, ub_, uf = uu[j % 2], ub[j % 2], ubf[j % 2]
                rows = slice(j * 128, (j + 1) * 128)
                S.op("dve", lambda: nc.vector.tensor_tensor(out=u[:, :], in0=a[:, 1024:1536], in1=a[:, 512:1024], op=ALU.mult), r=[a.b], w=[u.b])
                S.op("act", lambda: nc.scalar.copy(out=uf[:, :], in_=u[:, :]), r=[u.b], w=[uf.b])
                S.op("dve", lambda: nc.vector.tensor_tensor(out=ub_[:, :], in0=u[:, :], in1=bb[:, :], op=ALU.mult), r=[u.b, bb.b], w=[ub_.b])
                S.dma("pool", d["XD"][rows, 0:512], uf[:, :], r=[uf.b], w=[db["XD"]], acc=True)
                S.dma("pool", d["UB"][rows, :], ub_[:, :], r=[ub_.b], w=[db["UB"]], acc=True)
                S.dma("pool", d["X0"][rows, :], a[:, 0:512], r=[a.b], w=[db["X0"]], acc=True)
                yield

        def load_tab(ft, t_):
            S.dma("sp", t_[:, 0, :, :], d["dft_c"][ft], r=[db["dft_c"]], w=[t_.b])
            S.dma("sp", t_[:, 1, :, :], d["dft_s"][ft], r=[db["dft_s"]], w=[t_.b], acc=True)

        with self.scope() as es:
            T = lambda n, s, dt: self.tile(es, n, s, dt)
            Xf = T("hXf", [128, NT, 1024], BF16)
            for q4 in range(4):
                S.dma("sp", Xf[:, q4 * 8:(q4 + 1) * 8, :], d["XD"].rearrange("(t p) c -> p t c", p=128)[:, q4 * 8:(q4 + 1) * 8, 512:1536],
                      r=[db["XD"]], w=[Xf.b], acc=(q4 > 0))
            ph = T("hph", [128, 3, 32], F32)
            S.dma("sp", ph[:, 0:2, :], d["dft_ph"][:, :, :], r=[db["dft_ph"]], w=[ph.b])
            S.op("dve", lambda: nc.vector.tensor_scalar(out=ph[:, 2, :], in0=ph[:, 1, :], scalar1=-1.0, scalar2=None, op0=ALU.mult), r=[ph.b], w=[ph.b])
            tb = [T("htb%d" % i, [128, 2, NT, 128], BF16) for i in range(2)]
            bank = [self.ptile(es, "hbk%d" % i, [128, 512], F32) for i in range(8)]
            A1 = T("hA1", [128, 512], F32); A2 = T("hA2", [128, 512], F32)
            k2o = [T("hk2o%d" % i, [128, 2, 512], F32) for i in range(2)]
            cg = conv_gen(es)
            bi = 0
            load_tab(0, tb[0])
            for ft in range(NT):
                if ft + 1 < NT:
                    load_tab(ft + 1, tb[(ft + 1) % 2])
                t_ = tb[ft % 2]
                cs_ = []
                for n in range(2):
                    Cb_, Sb_ = bank[bi % 8], bank[(bi + 1) % 8]
                    bi += 2
                    cs_.append((Cb_, Sb_))
                    for (pb_, ti) in ((Cb_, 0), (Sb_, 1)):
                        for st in range(NT):
                            S.op("pe", lambda: nc.tensor.matmul(out=pb_[:, :], lhsT=t_[:, ti, st, :], rhs=Xf[:, st, n * 512:(n + 1) * 512],
                                                                start=(st == 0), stop=(st == NT - 1)), r=[t_.b, Xf.b], w=[pb_.b], acc=(st > 0))
                    if n == 0:
                        next(cg, None)
                (Cf, Sf), (Cbk, Sbk) = cs_
                V, P = nc.vector, nc.gpsimd
                k2 = k2o[ft % 2]
                S.op("act", lambda: nc.scalar.copy(out=A1[:, :], in_=Cf[:, :]), r=[Cf.b], w=[A1.b])
                S.op("act", lambda: nc.scalar.copy(out=A2[:, :], in_=Sf[:, :]), r=[Sf.b], w=[A2.b])
                S.op("dve", lambda: V.tensor_tensor(out=A1[:, :], in0=Cbk[:, :], in1=A1[:, :], op=ALU.add), r=[Cbk.b, A1.b], w=[A1.b])
                S.op("dve", lambda: V.tensor_tensor(out=A2[:, :], in0=Sbk[:, :], in1=A2[:, :], op=ALU.subtract), r=[Sbk.b, A2.b], w=[A2.b])
                S.op("pool", lambda: P.tensor_scalar(out=k2[:, 0, :], in0=A1[:, :], scalar1=ph[:, 0, ft:ft + 1], scalar2=0.0, op0=ALU.mult, op1=ALU.add),
                     r=[A1.b, ph.b], w=[k2.b])
                S.op("dve", lambda: V.scalar_tensor_tensor(out=k2[:, 0, :], in0=A2[:, :], scalar=ph[:, 2, ft:ft + 1], in1=k2[:, 0, :], op0=ALU.mult, op1=ALU.add),
                     r=[A2.b, ph.b, k2.b], w=[k2.b])
                S.op("pool", lambda: P.tensor_scalar(out=k2[:, 1, :], in0=A1[:, :], scalar1=ph[:, 1, ft:ft + 1], scalar2=0.0, op0=ALU.mult, op1=ALU.add),
                     r=[A1.b, ph.b], w=[k2.b], acc=True)
                S.op("dve", lambda: V.scalar_tensor_tensor(out=k2[:, 1, :], in0=A2[:, :], scalar=ph[:, 0, ft:ft + 1], in1=k2[:, 1, :], op0=ALU.mult, op1=ALU.add),
                     r=[A2.b, ph.b, k2.b], w=[k2.b], acc=True)
                S.dma("pool", d["K2D"][ft * 128:(ft + 1) * 128, :].rearrange("p (r c) -> p r c", r=2), k2[:, :, :], r=[k2.b], w=[db["K2D"]], acc=True)
                next(cg, None)
                S.rotate()
            for _ in cg:
                pass
        with self.scope() as es:
            T = lambda n, s, dt: self.tile(es, n, s, dt)
            Xu = T("hXu", [128, NT, 512], BF16)
            for q4 in range(4):
                S.dma("sp", Xu[:, q4 * 8:(q4 + 1) * 8, :], d["XD"].rearrange("(t p) c -> p t c", p=128)[:, q4 * 8:(q4 + 1) * 8, 0:512],
                      r=[db["XD"]], w=[Xu.b], acc=(q4 > 0))
            tb = [T("htc%d" % i, [128, 2, NT, 128], BF16) for i in range(3)]
            bank = [self.ptile(es, "hbl%d" % i, [128, 512], F32) for i in range(8)]
            k2i = [T("hk2i%d" % i, [128, 2, 512], F32) for i in range(3)]
            Dm = [[T("hD%d_%d" % (i, j_), [128, 512], F32) for j_ in range(4)] for i in range(2)]
            yo = [T("hyo%d" % i, [128, 2, 512], BF16) for i in range(2)]
            bi = 0
            load_tab(0, tb[0])
            load_tab(1, tb[1])
            for ft in range(NT):
                if ft + 2 < NT:
                    load_tab(ft + 2, tb[(ft + 2) % 3])
                t_ = tb[ft % 3]
                k2 = k2i[ft % 3]
                S.dma("sp", k2[:, :, :], d["K2D"][ft * 128:(ft + 1) * 128, :].rearrange("p (r c) -> p r c", r=2), r=[db["K2D"]], w=[k2.b])
                Cu, Su = bank[bi % 8], bank[(bi + 1) % 8]
                bi += 2
                for (pb_, ti) in ((Cu, 0), (Su, 1)):
                    for st in range(NT):
                        S.op("pe", lambda: nc.tensor.matmul(out=pb_[:, :], lhsT=t_[:, ti, st, :], rhs=Xu[:, st, :],
                                                            start=(st == 0), stop=(st == NT - 1)), r=[t_.b, Xu.b], w=[pb_.b], acc=(st > 0))
                V, P = nc.vector, nc.gpsimd
                D1, D2, D3, D4 = Dm[ft % 2]
                S.op("dve", lambda: V.tensor_tensor(out=D1[:, :], in0=Cu[:, :], in1=k2[:, 0, :], op=ALU.mult), r=[Cu.b, k2.b], w=[D1.b])
                S.op("dve", lambda: V.tensor_tensor(out=D2[:, :], in0=Su[:, :], in1=k2[:, 1, :], op=ALU.mult), r=[Su.b, k2.b], w=[D2.b])
                S.op("dve", lambda: V.tensor_tensor(out=D3[:, :], in0=Su[:, :], in1=k2[:, 0, :], op=ALU.mult), r=[Su.b, k2.b], w=[D3.b])
                S.op("dve", lambda: V.tensor_tensor(out=D4[:, :], in0=Cu[:, :], in1=k2[:, 1, :], op=ALU.mult), r=[Cu.b, k2.b], w=[D4.b])
                y = yo[ft % 2]
                S.op("pool", lambda: P.tensor_tensor(out=y[:, 0, :], in0=D1[:, :], in1=D2[:, :], op=ALU.add), r=[D1.b, D2.b], w=[y.b])
                S.op("pool", lambda: P.tensor_tensor(out=y[:, 1, :], in0=D3[:, :], in1=D4[:, :], op=ALU.subtract), r=[D3.b, D4.b], w=[y.b], acc=True)
                S.dma("pool", d["YD"][ft], y[:, :, :], r=[y.b], w=[db["YD"]], acc=True)
                S.rotate()
        with self.scope() as es:
            T = lambda n, s, dt: self.tile(es, n, s, dt)
            Y = T("hY", [128, NT, 2, 512], BF16)
            for q4 in range(4):
                S.dma("sp", Y[:, q4 * 8:(q4 + 1) * 8, :, :], d["YD"].rearrange("f p r c -> p f r c")[:, q4 * 8:(q4 + 1) * 8, :, :],
                      r=[db["YD"]], w=[Y.b], acc=(q4 > 0))
            tb = [T("hitb%d" % i, [128, 2, NT, 128], BF16) for i in range(3)]
            bank = [self.ptile(es, "hibk%d" % i, [128, 512], F32) for i in range(3)]
            ptr = [self.ptile(es, "hiptr%d" % i, [128, 8, 128], BF16) for i in range(2)]
            ubt = [T("hiub%d" % i, [128, 512], F32) for i in range(2)]
            x0t = [T("hix0%d" % i, [128, 512], F32) for i in range(2)]
            t1 = [T("hit1%d" % i, [128, 512], F32) for i in range(2)]
            dd = [T("hidd%d" % i, [128, 512], BF16) for i in range(2)]
            dT = [T("hidT%d" % i, [128, 4, 128], BF16) for i in range(2)]

            def load_tab2(tt, t_):
                S.dma("sp", t_[:, 0, :, :], d["dft_c"][tt], r=[db["dft_c"]], w=[t_.b])
                S.dma("sp", t_[:, 1, :, :], d["dft_s"][tt], r=[db["dft_s"]], w=[t_.b], acc=True)
            load_tab2(0, tb[0])
            load_tab2(1, tb[1])
            for tt in range(NT):
                if tt + 2 < NT:
                    load_tab2(tt + 2, tb[(tt + 2) % 3])
                t_, pb_ = tb[tt % 3], bank[tt % 3]
                rows = slice(tt * 128, (tt + 1) * 128)
                u_, x_ = ubt[tt % 2], x0t[tt % 2]
                S.dma("sp", u_[:, :], d["UB"][rows, :], r=[db["UB"]], w=[u_.b])
                S.dma("sp", x_[:, :], d["X0"][rows, :], r=[db["X0"]], w=[x_.b])
                for ft in range(NT):
                    for ri in range(2):
                        S.op("pe", lambda: nc.tensor.matmul(out=pb_[:, :], lhsT=t_[:, ri, ft, :], rhs=Y[:, ft, ri, :],
                                                            start=(ft == 0 and ri == 0), stop=(ft == NT - 1 and ri == 1)),
                             r=[t_.b, Y.b], w=[pb_.b], acc=(ft > 0 or ri > 0))
                a, o, p_, dt_ = t1[tt % 2], dd[tt % 2], ptr[tt % 2], dT[tt % 2]
                S.op("dve", lambda: nc.vector.scalar_tensor_tensor(out=a[:, :], in0=pb_[:, :], scalar=2.0 / (2 * S_LEN), in1=u_[:, :],
                                                                   op0=ALU.mult, op1=ALU.add), r=[pb_.b, u_.b], w=[a.b])
                S.op("pool", lambda: nc.gpsimd.tensor_tensor(out=o[:, :], in0=a[:, :], in1=x_[:, :], op=ALU.mult), r=[a.b, x_.b], w=[o.b])
                for cc in range(4):
                    S.op("pe", lambda: nc.tensor.transpose(out=p_[:, cc, :], in_=o[:, cc * 128:(cc + 1) * 128], identity=self.ident[:, :]),
                         r=[o.b, self.ident.b], w=[p_.b], acc=(cc > 0))
                S.op("act", lambda: nc.scalar.copy(out=dt_[:, :, :], in_=p_[:, 0:4, :]), r=[p_.b], w=[dt_.b])
                S.dma("pool", d["C1T"][512:1024, rows].rearrange("(c p) t -> p c t", p=128), dt_[:, :, :], r=[dt_.b], w=[db["C1T"]], acc=True)
            S.rotate()

    def build(self):
        self.declare()
        ph = self.phases
        with self.scope() as es:
            self.setup_common(es)
            if ph is None or "l0a" in ph:
                self.phase_l0a()
            if ph is None or "mla" in ph:
                self.attn_phase("mla", bg=(lambda es_: self.prep_ffn_gen(es_, 0, eng="pool")), bg_every=24)
            if ph is None or "na" in ph:
                self.attn_phase("na")
            if ph is None or "out0" in ph:
                self.phase_outproj("CT", "a_w_out", "x", "X1", 0)
            if ph is None or "ffn0" in ph:
                if ph is not None and "mla" not in ph:
                    self.prep_ffn_weights(es, 0)
                self.phase_ffn(0, "X1", "X2")
            if ph is None or "l1a" in ph:
                self.phase_l1a()
            if ph is None or "ret" in ph:
                self.attn_phase("ret", bg=(lambda es_: self.prep_ffn_gen(es_, 1, eng="pool")), bg_every=12)
            if ph is None or "hy" in ph:
                self.phase_hyena()
            if ph is None or "out1" in ph:
                self.phase_outproj("C1T", "c_w_out", "X2", "X3", 2)
            if ph is None or "ffn1" in ph:
                if ph is not None and "ret" not in ph:
                    self.prep_ffn_weights(es, 1)
                self.phase_ffn(1, "X3", "out")
            outs = [self.db[n] for n in self.debug] + [self.db["out"]]
            self.S.finish(outs)
        return self.nc


def make_inputs(inputs, b):
    c = host_consts()
    f = lambda a: np.ascontiguousarray(np.asarray(a, dtype=np.float32))
    m = {"x": f(inputs["x"][b])}
    for n in ["mix_pre_norm", "mix_post_norm", "ffn_pre_norm", "ffn_post_norm", "ffn_w_gate", "ffn_w_up", "ffn_conv", "ffn_w_down"]:
        m[n] = f(inputs[n])
    for n in ["a_w_in", "a_q_norm", "a_w_q_up", "a_kv_norm", "a_w_kv_up", "a_w_out", "c_w_in", "c_short_conv", "c_filt_w1", "c_filt_b1",
              "c_filt_w2", "c_filt_b2", "c_filt_w3", "c_filt_b3", "c_filt_w4", "c_filt_freq", "c_hy_bias", "c_w_out"]:
        m[n] = f(inputs[n][0])
    m["c_decay"] = f(np.concatenate([np.asarray(inputs["c_decay_fwd"][0]), np.asarray(inputs["c_decay_bwd"][0])]))
    m["na_bias"] = na_bias_layout(np.asarray(inputs["a_rpb"][0], dtype=np.float32))
    for k_ in ["ident_bf", "mla_cs", "ret_cs", "ret_np", "hy_zT", "hy_win", "dft_c", "dft_s", "dft_ph"]:
        m[k_] = c[k_]
    return m


def kernel(**inputs):
    bld = Builder()
    nc = bld.build()
    shared = make_inputs(inputs, 0)
    in_maps = []
    for b in range(8):
        m = dict(shared)
        m["x"] = np.ascontiguousarray(np.asarray(inputs["x"][b], dtype=np.float32))
        in_maps.append(m)
    res = run_bass_kernel_spmd(nc, in_maps, core_ids=list(range(8)))
    return np.stack([r["out"] for r in res.results], 0).astype(np.float32)
```

```python
import numpy as np
import ml_dtypes
from contextlib import ExitStack, contextmanager
import concourse.bass as bass
import concourse.mybir as mybir
from concourse.alu_op_type import AluOpType as ALU
from concourse.bass_utils import run_bass_kernel_spmd

F32 = mybir.dt.float32
BF16 = mybir.dt.bfloat16
AF = mybir.ActivationFunctionType
NPBF = ml_dtypes.bfloat16

S_LEN = 4096
D = 1024
NT = 32
NB = 8
EPS = 1e-6
DFF = 2816
NFC = 22
NEGINF = -30000.0
RET_OFF = 3968
RET_M = 8064


class Sem:
    __slots__ = ("h", "count", "name")

    def __init__(self, h, name):
        self.h = h
        self.count = 0
        self.name = name


class Eng:
    def __init__(self, name, h):
        self.name = name
        self.h = h
        self.sem = None
        self.seen = {}
        self.n_inst = 0


class Buf:
    __slots__ = ("name", "writer", "readers")

    def __init__(self, name):
        self.name = name
        self.writer = {}
        self.readers = {}


class Sched:
    RING = 28
    ROT = 12000

    def __init__(self, nc):
        self.nc = nc
        self.engs = {
            "pe": Eng("pe", nc.tensor),
            "act": Eng("act", nc.scalar),
            "dve": Eng("dve", nc.vector),
            "pool": Eng("pool", nc.gpsimd),
            "sp": Eng("sp", nc.sync),
        }
        self.nsem = 0
        for e in self.engs.values():
            e.sem = self.new_sem(e.name)
        self.ring = [self.new_sem("dma%d" % i) for i in range(self.RING)]
        self.ring_i = 0
        self.nbuf = 0
        self.deferred = []
        self._flushing = False

    def new_sem(self, name):
        self.nsem += 1
        return Sem(self.nc.alloc_semaphore(name="s_%s_%d" % (name, self.nsem)), name)

    def buf(self, name=None):
        self.nbuf += 1
        return Buf(name or "b%d" % self.nbuf)

    def rotate(self):
        for e in self.engs.values():
            if e.sem.count > self.ROT:
                e.sem = self.new_sem(e.name)

    def _wait(self, eng, sem, val):
        if val <= 0 or eng.seen.get(sem, 0) >= val:
            return
        eng.h.wait_ge(sem.h, val)
        eng.seen[sem] = val
        eng.n_inst += 1

    def _deps(self, eng, r, w, acc):
        deps = {}
        for b in r:
            for s, v in b.writer.items():
                if deps.get(s, 0) < v:
                    deps[s] = v
        own_ok = eng.name == "pe"
        for b in w:
            if not acc:
                for s, v in b.writer.items():
                    if s is eng.sem and own_ok:
                        continue
                    if deps.get(s, 0) < v:
                        deps[s] = v
            for s, v in b.readers.items():
                if s is eng.sem and own_ok:
                    continue
                if deps.get(s, 0) < v:
                    deps[s] = v
        return deps

    def _record(self, tok, r, w, acc):
        s, v = tok
        for b in r:
            if b.readers.get(s, 0) < v:
                b.readers[s] = v
        for b in w:
            if acc:
                b.writer[s] = v
            else:
                b.writer = {s: v}
                b.readers = {}

    def op(self, eng, fn, r=(), w=(), acc=False):
        self._tick(r, w)
        e = self.engs[eng]
        deps = self._deps(e, r, w, acc)
        for s, v in deps.items():
            if s is e.sem and eng == "pe":
                continue
            self._wait(e, s, v)
        inst = fn()
        e.sem.count += 1
        inst.then_inc(e.sem.h, 1)
        e.n_inst += 1
        tok = (e.sem, e.sem.count)
        self._record(tok, r, w, acc)
        return tok

    DEFER = 14

    def _tick(self, r, w):
        if self._flushing or not self.deferred:
            return
        keep = []
        todo = []
        for item in self.deferred:
            dr, dw = item[3], item[4]
            conflict = any(b in dr for b in w) or any(b in dw for b in w) or any(b in dw for b in r)
            item[0] -= 1
            if conflict or item[0] <= 0 or todo:
                todo.append(item)
            else:
                keep.append(item)
        self.deferred = keep
        self._flushing = True
        for _, out, in_, dr, dw, acc, kw in todo:
            self.dma("sp", out, in_, r=dr, w=dw, acc=acc, **kw)
        self._flushing = False

    def flush(self):
        todo, self.deferred = self.deferred, []
        self._flushing = True
        for _, out, in_, dr, dw, acc, kw in todo:
            self.dma("sp", out, in_, r=dr, w=dw, acc=acc, **kw)
        self._flushing = False

    def dma(self, eng, out, in_, r=(), w=(), acc=False, **kw):
        if eng == "pool":
            self.deferred.append([self.DEFER, out, in_, list(r), list(w), acc, kw])
            return None
        self._tick(r, w)
        e = self.engs[eng]
        deps = self._deps(e, r, w, acc)
        for s, v in deps.items():
            self._wait(e, s, v)
        sem = self.ring[self.ring_i % self.RING]
        self.ring_i += 1
        self._wait(e, sem, sem.count)
        inst = e.h.dma_start(out=out, in_=in_, **kw)
        sem.count += 16
        inst.then_inc(sem.h, 16)
        e.n_inst += 1
        tok = (sem, sem.count)
        self._record(tok, r, w, acc)
        return tok

    def finish(self, bufs):
        self.flush()
        e = self.engs["sp"]
        for b in bufs:
            for s, v in b.writer.items():
                self._wait(e, s, v)
        for sem in self.ring:
            self._wait(e, sem, sem.count)
        for o in self.engs.values():
            if o is not e:
                self._wait(e, o.sem, o.sem.count)


class Tile:
    __slots__ = ("t", "b")

    def __init__(self, t, b):
        self.t = t
        self.b = b

    def __getitem__(self, k):
        return self.t[k]


def _rope_tab(dim, rows_rep):
    inv = (10000.0 ** (-np.arange(0, dim, 2, dtype=np.float32) / np.float32(dim))).astype(np.float32)
    ang = (np.arange(S_LEN, dtype=np.float32)[:, None] * inv[None, :]).astype(np.float32)
    c = np.cos(ang).astype(np.float32).T
    s = np.sin(ang).astype(np.float32).T
    return (np.ascontiguousarray(np.concatenate([c, c], 0)), np.ascontiguousarray(np.concatenate([s, s], 0)))


_CONST_CACHE = {}


def host_consts():
    if _CONST_CACHE:
        return _CONST_CACHE
    c = {}
    c["ident_bf"] = np.eye(128, dtype=np.float32).astype(NPBF)
    cm, sm = _rope_tab(32, 2)
    c["mla_cs"] = np.ascontiguousarray(np.stack([cm, sm], 1))
    cr, sr = _rope_tab(128, 2)
    c["ret_cs"] = np.ascontiguousarray(np.stack([cr, sr], 1))
    i = np.arange(128, dtype=np.int64)[:, None]
    m = np.arange(RET_M, dtype=np.int64)[None, :]
    dl = i - m + RET_OFF
    c["ret_np"] = np.ascontiguousarray(
        np.stack([np.maximum(-dl, 0), np.maximum(dl, 0)], 1).astype(np.float32))
    L = S_LEN
    t = (np.arange(L, dtype=np.float32) / np.float32(L - 1)).astype(np.float32)
    w = (np.float32(2.0 * np.pi) * np.arange(L, dtype=np.float32) / np.float32(L)).astype(np.float32)
    f = np.linspace(1e-4, 15, 16, dtype=np.float32)
    fw = (f[None, :] * w[:, None]).astype(np.float32)
    z = np.concatenate([t[:, None], np.cos(fw), -np.sin(fw)], -1).astype(np.float32)
    c["hy_zT"] = np.ascontiguousarray(z.T)
    import math
    max_decay = math.log(1e-2) / 0.3
    min_decay = math.log(1e-2) / 1.5
    deltas = np.linspace(min_decay, max_decay, 512, dtype=np.float32)
    c["hy_win"] = np.exp(-t[:, None] * np.abs(deltas)[None, :]).astype(np.float32)
    N = 2 * L
    a = (2 * np.arange(L, dtype=np.int64) + 1)
    ph = (a[:, None] * a[None, :]) % (4 * N)
    ang = ph.astype(np.float64) * (2.0 * np.pi / (4 * N))

    def tile(mat):
        return np.ascontiguousarray(mat.reshape(32, 128, 32, 128).transpose(2, 1, 0, 3)).astype(NPBF)

    c["dft_c"] = tile(np.cos(ang).astype(np.float32))
    c["dft_s"] = tile(np.sin(ang).astype(np.float32))
    del ang, ph
    phi = np.pi * (np.arange(L, dtype=np.float64) + 0.5) / N
    c["dft_ph"] = np.ascontiguousarray(
        np.stack([np.cos(phi), np.sin(phi)], 0).reshape(2, 32, 128).transpose(2, 0, 1)).astype(np.float32)
    _CONST_CACHE.update(c)
    return c


def na_bias_layout(rpb):
    H = rpb.shape[0]
    out = np.full((H, 3, 16, 64, 8, 64), NEGINF, dtype=np.float32)
    qc = np.arange(64)
    cs = np.clip(qc - 8, 0, 48)
    for pat, (ws, r0) in enumerate([(0, 0), (4, 8), (48, 56)]):
        for qr in range(8):
            r = r0 + qr
            rs = min(max(r - 4, 0), 56)
            for i in range(8):
                kr = rs + i
                krl = kr - ws
                for j in range(16):
                    kc = cs + j
                    out[:, pat, krl, kc, qr, qc] = rpb[:, kr - r + 7, kc - qc + 15]
    out = out.reshape(H, 3, 1024, 512)
    return np.ascontiguousarray(out.reshape(H, 3, 8, 128, 512).transpose(0, 3, 1, 2, 4))


def na_col_ranges():
    out = []
    for (ws, r0) in [(0, 0), (4, 8), (48, 56)]:
        per = []
        for i in range(8):
            rows = (ws + 2 * i, ws + 2 * i + 1)
            rel = []
            for qr in range(8):
                r = r0 + qr
                rs = min(max(r - 4, 0), 56)
                if any(rs <= kr < rs + 8 for kr in rows):
                    rel.append(qr)
            per.append((rel[0] * 64, (rel[-1] + 1) * 64) if rel else None)
        out.append(per)
    return out

class Builder:
    def __init__(self, phases=None, debug=(), ext_in=()):
        self.ext_in = set(ext_in)
        self.nc = bass.Bass("TRN2", target_bir_lowering=False)
        self.S = Sched(self.nc)
        self.phases = phases
        self.debug = set(debug)
        self.d = {}
        self.db = {}
        self.rr = 0
        self.free_haz = {}
        self.uid = 0
        self.scopes = {}

    def din(self, name, shape, dt=F32):
        self.d[name] = self.nc.dram_tensor(name, list(shape), dt, kind="ExternalInput").ap()
        self.db[name] = self.S.buf(name)

    def dscratch(self, name, shape, dt):
        kind = "ExternalOutput" if name in self.debug else ("ExternalInput" if name in self.ext_in else "Internal")
        self.d[name] = self.nc.dram_tensor(name, list(shape), dt, kind=kind).ap()
        self.db[name] = self.S.buf(name)

    def _newbuf(self, es, name):
        b = self.S.buf(name)
        b.readers = dict(self.free_haz)
        self.scopes.setdefault(id(es), []).append(b)
        return b

    def tile(self, es, name, shape, dt):
        self.uid += 1
        t = es.enter_context(self.nc.sbuf_tensor("%s_%d" % (name, self.uid), list(shape), dt))
        return Tile(t, self._newbuf(es, name))

    def ptile(self, es, name, shape, dt=F32):
        self.uid += 1
        t = es.enter_context(self.nc.psum_tensor("%s_%d" % (name, self.uid), list(shape), dt))
        return Tile(t, self._newbuf(es, name))

    @contextmanager
    def scope(self):
        with ExitStack() as es:
            yield es
            self.S.flush()
            for b in self.scopes.pop(id(es), []):
                for dct in (b.writer, b.readers):
                    for s, v in dct.items():
                        if self.free_haz.get(s, 0) < v:
                            self.free_haz[s] = v

    def eng3(self):
        self.rr += 1
        return ("dve", "pool", "act")[self.rr % 3]

    def eng2(self):
        self.rr += 1
        return ("dve", "pool")[self.rr % 2]

    def declare(self):
        din = self.din
        din("x", [S_LEN, D])
        for n in ["mix_pre_norm", "mix_post_norm", "ffn_pre_norm", "ffn_post_norm"]:
            din(n, [2, D])
        din("ffn_w_gate", [2, D, DFF]); din("ffn_w_up", [2, D, DFF]); din("ffn_conv", [2, 3, DFF]); din("ffn_w_down", [2, DFF, D])
        din("a_w_in", [D, 1952]); din("a_q_norm", [256]); din("a_w_q_up", [256, 768]); din("a_kv_norm", [128])
        din("a_w_kv_up", [128, 1024]); din("a_w_out", [D, D])
        din("c_w_in", [D, 3584]); din("c_decay", [8]); din("c_short_conv", [3, 1536])
        din("c_filt_w1", [33, 64]); din("c_filt_b1", [64]); din("c_filt_w2", [64, 64]); din("c_filt_b2", [64])
        din("c_filt_w3", [64, 64]); din("c_filt_b3", [64]); din("c_filt_w4", [64, 1024]); din("c_filt_freq", [64])
        din("c_hy_bias", [512]); din("c_w_out", [D, D])
        din("ident_bf", [128, 128], BF16)
        din("mla_cs", [32, 2, S_LEN]); din("ret_cs", [128, 2, S_LEN]); din("ret_np", [128, 2, RET_M])
        din("na_bias", [8, 128, 3, 8, 512])
        din("hy_zT", [33, S_LEN]); din("hy_win", [S_LEN, 512])
        din("dft_c", [32, 128, 32, 128], BF16); din("dft_s", [32, 128, 32, 128], BF16); din("dft_ph", [128, 2, 32])
        ds = self.dscratch
        ds("QT", [8, 96, S_LEN], BF16); ds("KNT", [8, 64, S_LEN], BF16); ds("KRT", [32, S_LEN], BF16)
        ds("MV", [S_LEN, 512], BF16)
        ds("NQT", [512, S_LEN], BF16); ds("NKT", [512, S_LEN], BF16); ds("NV", [S_LEN, 512], BF16)
        ds("CT", [D, S_LEN], BF16)
        ds("X1", [S_LEN, D], F32); ds("X2", [S_LEN, D], F32); ds("X3", [S_LEN, D], F32)
        for l in range(2):
            ds("WG%d" % l, [NFC, 128, 8, 128], BF16); ds("WU%d" % l, [NFC, 128, 8, 128], BF16)
            ds("WD%d" % l, [NFC, 128, D], BF16)
        ds("RQT", [4, 128, S_LEN], BF16); ds("RKT", [4, 128, S_LEN], BF16); ds("RV", [S_LEN, 512], BF16)
        ds("RG", [512, S_LEN], F32); ds("HY", [S_LEN + 2, 1536], F32)
        ds("C1T", [D, S_LEN], BF16)
        ds("XD", [S_LEN, 1536], BF16); ds("UB", [S_LEN, 512], F32); ds("X0", [S_LEN, 512], F32)
        ds("YD", [32, 128, 2, 512], BF16); ds("K2D", [S_LEN, 1024], F32)
        self.d["out"] = self.nc.dram_tensor("out", [S_LEN, D], F32, kind="ExternalOutput").ap()
        self.db["out"] = self.S.buf("out")

    def setup_common(self, es):
        nc, S, d, db = self.nc, self.S, self.d, self.db
        self.ident = self.tile(es, "ident", [128, 128], BF16)
        S.dma("sp", self.ident[:, :], d["ident_bf"][:, :], r=[db["ident_bf"]], w=[self.ident.b])
        self.ones_bf = self.tile(es, "ones_bf", [128, 128], BF16)
        S.op("pool", lambda: nc.gpsimd.memset(self.ones_bf[:, :], 1.0), w=[self.ones_bf.b])
        self.epsb = self.tile(es, "epsb", [128, 1], F32)
        S.op("pool", lambda: nc.gpsimd.memset(self.epsb[:, :], EPS), w=[self.epsb.b])
        self.mhalf = self.tile(es, "mhalf", [128, 1], F32)
        S.op("pool", lambda: nc.gpsimd.memset(self.mhalf[:, :], -0.5), w=[self.mhalf.b])
        self.ones_f = self.tile(es, "ones_f", [128, 128], F32)
        S.op("pool", lambda: nc.gpsimd.memset(self.ones_f[:, :], 1.0), w=[self.ones_f.b])
        self.G = self.tile(es, "gains", [128, 40], F32)
        specs = [(d["mix_pre_norm"][0, :], 0, 8, "mix_pre_norm"), (d["ffn_pre_norm"][0, :], 8, 8, "ffn_pre_norm"),
                 (d["mix_pre_norm"][1, :], 16, 8, "mix_pre_norm"), (d["ffn_pre_norm"][1, :], 24, 8, "ffn_pre_norm"),
                 (d["a_q_norm"][:], 32, 2, "a_q_norm"), (d["a_kv_norm"][:], 34, 1, "a_kv_norm")]
        for ap, c0, n, nm in specs:
            S.dma("sp", self.G[:, c0:c0 + n], ap.rearrange("(c p) -> p c", p=128), r=[db[nm]], w=[self.G.b], acc=True,
                  allow_slow_non_contiguous=True)
        self.PG = self.tile(es, "pgains", [128, 4, D], F32)
        for i, (nm, l) in enumerate([("mix_post_norm", 0), ("ffn_post_norm", 0), ("mix_post_norm", 1), ("ffn_post_norm", 1)]):
            S.dma("sp", self.PG[:, i, :], d[nm][l, :].partition_broadcast(128), r=[db[nm]], w=[self.PG.b], acc=True)

    def conv_piece(self, dst, src, gain_ap, sign=1.0, eng=None):
        nc = self.nc
        eng = eng or self.eng3()
        if gain_ap is None:
            if eng == "act":
                return eng, (lambda: nc.scalar.mul(out=dst, in_=src, mul=float(sign)))
            e = nc.vector if eng == "dve" else nc.gpsimd
            return eng, (lambda: e.tensor_scalar(out=dst, in0=src, scalar1=float(sign), scalar2=1.0, op0=ALU.mult, op1=ALU.mult))
        if eng == "act" and sign == 1.0:
            return eng, (lambda: nc.scalar.activation(out=dst, in_=src, func=AF.Copy, scale=gain_ap))
        if eng == "act":
            eng = "dve"
        e = nc.vector if eng == "dve" else nc.gpsimd
        return eng, (lambda: e.tensor_scalar(out=dst, in0=src, scalar1=gain_ap, scalar2=float(sign), op0=ALU.mult, op1=ALU.mult))

    def load_weight(self, stg, src_ap, src_buf, kchunks, ncols, gain_c0, pieces):
        S = self.S
        for kc in range(kchunks):
            st = stg[kc % len(stg)]
            S.dma("sp", st[:, 0:ncols], src_ap[kc * 128:(kc + 1) * 128, :], r=[src_buf], w=[st.b])
            g = None if gain_c0 is None else self.G[:, gain_c0 + kc:gain_c0 + kc + 1]
            for dst, src, sign, dbuf in pieces(kc, st):
                eng, fn = self.conv_piece(dst, src, g, sign)
                rr = [st.b] + ([self.G.b] if g is not None else [])
                S.op(eng, fn, r=rr, w=[dbuf], acc=True)

    def prep_ffn_gen(self, es, l, eng=None):
        nc, S, d, db = self.nc, self.S, self.d, self.db
        stg = [self.tile(es, "fst%d" % i, [128, DFF], F32) for i in range(2)]
        tmp = [self.tile(es, "ftm%d" % i, [128, DFF], BF16) for i in range(2)]
        gc0 = 8 + 16 * l
        for wname, dname in (("ffn_w_gate", "WG%d" % l), ("ffn_w_up", "WU%d" % l)):
            for kc in range(8):
                st, tm = stg[kc % 2], tmp[kc % 2]
                S.dma("sp", st[:, :], d[wname][l, kc * 128:(kc + 1) * 128, :], r=[db[wname]], w=[st.b])
                yield
                en, fn = self.conv_piece(tm[:, :], st[:, :], self.G[:, gc0 + kc:gc0 + kc + 1], eng=eng)
                S.op(en, fn, r=[st.b, self.G.b], w=[tm.b])
                S.dma("pool", d[dname].rearrange("f p c j -> p f c j")[:, :, kc, :],
                      tm[:, :].rearrange("p (f j) -> p f j", j=128), r=[tm.b], w=[db[dname]], acc=True)
                yield
        for fc in range(NFC):
            st, tm = stg[fc % 2], tmp[fc % 2]
            S.dma("sp", st[:, 0:D], d["ffn_w_down"][l, fc * 128:(fc + 1) * 128, :], r=[db["ffn_w_down"]], w=[st.b])
            yield
            en, fn = self.conv_piece(tm[:, 0:D], st[:, 0:D], None, eng=eng)
            S.op(en, fn, r=[st.b], w=[tm.b])
            S.dma("pool", d["WD%d" % l][fc, :, :], tm[:, 0:D], r=[tm.b], w=[db["WD%d" % l]], acc=True)
            yield

    def prep_ffn_weights(self, es0, l):
        with self.scope() as es:
            for _ in self.prep_ffn_gen(es, l):
                pass

    def norm_setup(self, es, nh=2, halo=False, nhb=2):
        self.nx = [self.tile(es, "nx%d" % i, [128, D], F32) for i in range(2)]
        self.nsq = self.tile(es, "nsq", [128, D], BF16)
        self.nhb = [self.tile(es, "nhb%d" % i, [128, D], BF16) for i in range(nhb)]
        self.nss = [self.tile(es, "nss%d" % i, [128, 2], F32) for i in range(2)]
        self.ncnt_a = 0
        self.ncnt_b = 0
        w = 514 if halo else 512
        self.hT = [self.tile(es, "hT%d" % i, [128, 8, w], BF16) for i in range(nh)]
        self.halo = halo
        self.ncnt = 0

    def norm_block(self, xname, blk, hT, ptr):
        for j in range(4):
            self.norm_block_a(xname, blk, [j])
            self.norm_block_b(hT, ptr, [j])

    def norm_block_a(self, xname, blk, js=range(4)):
        nc, S, d, db = self.nc, self.S, self.d, self.db
        for j in js:
            i = self.ncnt_a
            self.ncnt_a += 1
            xt, hb, ss = self.nx[i % 2], self.nhb[i % len(self.nhb)], self.nss[i % 2]
            r0 = blk * 512 + j * 128
            S.dma("sp", xt[:, :], d[xname][r0:r0 + 128, :], r=[db[xname]], w=[xt.b])
            S.op("act", lambda: nc.scalar.activation(out=self.nsq[:, :], in_=xt[:, :], func=AF.Square, accum_out=ss[:, 0:1]),
                 r=[xt.b], w=[self.nsq.b, ss.b])
            S.op("pool", lambda: nc.gpsimd.tensor_scalar(out=ss[:, 1:2], in0=ss[:, 0:1], scalar1=1.0 / D, scalar2=EPS,
                                                         op0=ALU.mult, op1=ALU.add), r=[ss.b], w=[ss.b])
            S.op("pool", lambda: nc.gpsimd.tensor_tensor(out=ss[:, 1:2], in0=ss[:, 1:2], in1=self.mhalf[:, :], op=ALU.pow),
                 r=[ss.b, self.mhalf.b], w=[ss.b])
            S.op("pool", lambda: nc.gpsimd.tensor_scalar(out=hb[:, :], in0=xt[:, :], scalar1=ss[:, 1:2], scalar2=1.0, op0=ALU.mult, op1=ALU.mult),
                 r=[xt.b, ss.b], w=[hb.b])

    def norm_block_b(self, hT, ptr, js=range(4)):
        nc, S = self.nc, self.S
        off = 1 if self.halo else 0
        for j in js:
            i = self.ncnt_b
            self.ncnt_b += 1
            hb, pt = self.nhb[i % len(self.nhb)], ptr[i % len(ptr)]
            for c in range(8):
                S.op("pe", lambda: nc.tensor.transpose(out=pt[:, c, :], in_=hb[:, c * 128:(c + 1) * 128], identity=self.ident[:, :]),
                     r=[hb.b, self.ident.b], w=[pt.b], acc=(c > 0))
            S.op("dve", lambda: nc.vector.tensor_copy(out=hT[:, :, off + j * 128:off + (j + 1) * 128], in_=pt[:, :, :]),
                 r=[pt.b], w=[hT.b], acc=True)

    def postnorm_residual(self, ps, xin_name, xout_name, r0, gi, bufs, k):
        nc, S, d, db = self.nc, self.S, self.d, self.db
        nb_ = len(bufs["xr"])
        xr, tt, ss = bufs["xr"][k % nb_], bufs["tt"][k % nb_], bufs["ss"][k % nb_]
        S.dma("sp", xr[:, :], d[xin_name][r0:r0 + 128, :], r=[db[xin_name]], w=[xr.b])
        S.op("act", lambda: nc.scalar.activation(out=bufs["junk"][:, :], in_=ps[:, :], func=AF.Square, accum_out=ss[:, 0:1]),
             r=[ps.b], w=[bufs["junk"].b, ss.b])
        S.op("dve", lambda: nc.vector.tensor_scalar(out=ss[:, 1:2], in0=ss[:, 0:1], scalar1=1.0 / D, scalar2=EPS,
                                                    op0=ALU.mult, op1=ALU.add), r=[ss.b], w=[ss.b])
        S.op("pool", lambda: nc.gpsimd.tensor_tensor(out=ss[:, 1:2], in0=ss[:, 1:2], in1=self.mhalf[:, :], op=ALU.pow),
             r=[ss.b, self.mhalf.b], w=[ss.b])
        S.op("dve", lambda: nc.vector.scalar_tensor_tensor(out=tt[:, :], in0=ps[:, :], scalar=ss[:, 1:2], in1=self.PG[:, gi, :],
                                                           op0=ALU.mult, op1=ALU.mult), r=[ps.b, ss.b, self.PG.b], w=[tt.b])
        S.op("dve", lambda: nc.vector.tensor_tensor(out=tt[:, :], in0=tt[:, :], in1=xr[:, :], op=ALU.add),
             r=[tt.b, xr.b], w=[tt.b])
        S.dma("pool", d[xout_name][r0:r0 + 128, :], tt[:, :], r=[tt.b], w=[db[xout_name]], acc=True)

    def postnorm_bufs(self, es, n=2):
        return {"xr": [self.tile(es, "pxr%d" % i, [128, D], F32) for i in range(n)],
                "tt": [self.tile(es, "ptt%d" % i, [128, D], F32) for i in range(n)],
                "ss": [self.tile(es, "pss%d" % i, [128, 2], F32) for i in range(n)],
                "junk": self.tile(es, "pjunk", [128, D], BF16)}

    def rstd_bc(self, pss, sq_list, n, out_t, rows=128):
        nc, S = self.nc, self.S
        for i, (ap, b) in enumerate(sq_list):
            S.op("pe", lambda: nc.tensor.matmul(out=pss[:, 0:512], lhsT=self.ones_bf[:, :], rhs=ap, start=(i == 0),
                                                stop=(i == len(sq_list) - 1)), r=[self.ones_bf.b, b], w=[pss.b], acc=(i > 0))
        S.op("act", lambda: nc.scalar.activation(out=out_t[:, 0:512], in_=pss[:, 0:512], func=AF.Ln, scale=1.0 / n, bias=self.epsb[:, 0:1]),
             r=[pss.b, self.epsb.b], w=[out_t.b])
        S.op("act", lambda: nc.scalar.activation(out=out_t[:, 0:512], in_=out_t[:, 0:512], func=AF.Exp, scale=-0.5), r=[out_t.b], w=[out_t.b])

    def phase_l0a(self):
        nc, S, d, db = self.nc, self.S, self.d, self.db
        with self.scope() as es:
            T = lambda n, s, dt: self.tile(es, n, s, dt)
            win = T("win0", [128, 8, 1984], BF16)
            wq = T("wq", [128, 2, 1024], BF16)
            wkv = T("wkv", [128, 1024], BF16)
            with self.scope() as es2:
                stg = [self.tile(es2, "wst%d" % i, [128, 1952], F32) for i in range(2)]

                def p_win(kc, st):
                    return [(win[:, kc, 0:1952], st[:, 0:1952], 1.0, win.b),
                            (win[:, kc, 1952:1968], st[:, 400:416], -1.0, win.b),
                            (win[:, kc, 1968:1984], st[:, 384:400], 1.0, win.b)]
                self.load_weight(stg, d["a_w_in"], db["a_w_in"], 8, 1952, 0, p_win)

                def p_wq(kc, st):
                    sv = st[:, 0:768].rearrange("p (h e) -> p h e", h=8)
                    dv = wq[:, kc, 768:1024].rearrange("p (h e) -> p h e", h=8)
                    return [(wq[:, kc, 0:768], st[:, 0:768], 1.0, wq.b),
                            (dv[:, :, 0:16], sv[:, :, 80:96], -1.0, wq.b),
                            (dv[:, :, 16:32], sv[:, :, 64:80], 1.0, wq.b)]
                self.load_weight(stg, d["a_w_q_up"], db["a_w_q_up"], 2, 768, 32, p_wq)
                self.load_weight(stg, d["a_w_kv_up"], db["a_w_kv_up"], 1, 1024, 34,
                                 lambda kc, st: [(wkv[:, :], st[:, 0:1024], 1.0, wkv.b)])
            self.norm_setup(es, nhb=4)
            ptr = [self.ptile(es, "ptr%d" % i, [128, 8, 128], BF16) for i in range(2)]
            pp = [self.ptile(es, "pp%d" % i, [128, 512], F32) for i in range(5)]
            pcnt = [0]

            def nextp():
                pcnt[0] += 1
                return pp[pcnt[0] % len(pp)]
            cs = [T("mcs%d" % i, [128, 2, 512], F32) for i in range(2)]
            cqT = [T("cqT%d" % i, [128, 2, 512], BF16) for i in range(2)]
            ckvT = [T("ckvT%d" % i, [128, 512], BF16) for i in range(2)]
            sqq = T("sqq", [128, 2, 512], BF16)
            sqkv = T("sqkv", [128, 512], BF16)
            rq_bc = T("rq_bc", [128, 512], F32)
            rkv_bc = T("rkv_bc", [128, 512], F32)
            rkv_col = T("rkv_col", [128, 8], F32)
            ev = [T("ev%d" % i, [128, 512], BF16) for i in range(4)]
            evc = [0]
            t1 = [T("rt1_%d" % i, [128, 512], F32) for i in range(2)]
            t2 = [T("rt2_%d" % i, [128, 512], F32) for i in range(2)]
            vsb = [T("vsb%d" % i, [128, 512], BF16) for i in range(2)]

            def nextev():
                evc[0] += 1
                return ev[evc[0] % len(ev)]

            def proj(ps_ap, ps_buf, hT, c0, n, first=True, last=True):
                for kc in range(8):
                    S.op("pe", lambda: nc.tensor.matmul(out=ps_ap, lhsT=win[:, kc, c0:c0 + n], rhs=hT[:, kc, 0:512],
                                                        start=(kc == 0), stop=(kc == 7)),
                         r=[win.b, hT.b], w=[ps_buf], acc=(kc > 0))

            for blk in range(NB + 1):
                if blk < NB:
                    self.norm_block_a("x", blk)
                    if blk == 0:
                        self.norm_block_b(self.hT[0], ptr)
                    c = cs[blk % 2]
                    S.dma("sp", c[64:96, :, :], d["mla_cs"][:, :, blk * 512:(blk + 1) * 512], r=[db["mla_cs"]], w=[c.b])
                if blk == 0:
                    continue
                b = blk - 1
                hT, c = self.hT[b % 2], cs[b % 2]
                tsl = slice(b * 512, (b + 1) * 512)
                cq, ckv = cqT[b % 2], ckvT[b % 2]
                for j in range(2):
                    ps = nextp()
                    proj(ps[:, :], ps.b, hT, j * 128, 128)
                    S.op("act", lambda: nc.scalar.copy(out=cq[:, j, :], in_=ps[:, :]), r=[ps.b], w=[cq.b], acc=(j > 0))
                    S.op("act", lambda: nc.scalar.activation(out=sqq[:, j, :], in_=ps[:, :], func=AF.Square), r=[ps.b],
                         w=[sqq.b], acc=(j > 0))
                ps = nextp()
                proj(ps[:, :], ps.b, hT, 256, 128)
                S.op("act", lambda: nc.scalar.copy(out=ckv[:, :], in_=ps[:, :]), r=[ps.b], w=[ckv.b])
                S.op("act", lambda: nc.scalar.activation(out=sqkv[:, :], in_=ps[:, :], func=AF.Square), r=[ps.b], w=[sqkv.b])
                ps = nextp()
                self.rstd_bc(ps, [(sqq[:, 0, :], sqq.b), (sqq[:, 1, :], sqq.b)], 256, rq_bc)
                ps = nextp()
                self.rstd_bc(ps, [(sqkv[:, :], sqkv.b)], 128, rkv_bc)
                ps = nextp()
                for j in range(4):
                    S.op("pe", lambda: nc.tensor.matmul(out=ps[:, j:j + 1], lhsT=sqkv[:, j * 128:(j + 1) * 128], rhs=self.ones_bf[:, 0:1],
                                                        start=True, stop=True), r=[sqkv.b, self.ones_bf.b], w=[ps.b], acc=(j > 0))
                S.op("dve", lambda: nc.vector.tensor_scalar(out=rkv_col[:, 0:4], in0=ps[:, 0:4], scalar1=1.0 / 128, scalar2=EPS,
                                                            op0=ALU.mult, op1=ALU.add), r=[ps.b], w=[rkv_col.b])
                S.op("act", lambda: nc.scalar.activation(out=rkv_col[:, 0:4], in_=rkv_col[:, 0:4], func=AF.Sqrt), r=[rkv_col.b], w=[rkv_col.b])
                S.op("dve", lambda: nc.vector.reciprocal(out=rkv_col[:, 0:4], in_=rkv_col[:, 0:4]), r=[rkv_col.b], w=[rkv_col.b])
                psa, psb = nextp(), nextp()
                proj(psa[64:96, :], psa.b, hT, 384, 32)
                proj(psb[64:96, :], psb.b, hT, 1952, 32)
                ta, tb, e = t1[0], t2[0], nextev()
                S.op("dve", lambda: nc.vector.tensor_tensor(out=ta[64:96, :], in0=psa[64:96, :], in1=c[64:96, 0, :], op=ALU.mult),
                     r=[psa.b, c.b], w=[ta.b])
                S.op("dve", lambda: nc.vector.tensor_tensor(out=tb[64:96, :], in0=psb[64:96, :], in1=c[64:96, 1, :], op=ALU.mult),
                     r=[psb.b, c.b], w=[tb.b])
                S.op("pool", lambda: nc.gpsimd.tensor_tensor(out=e[64:96, :], in0=ta[64:96, :], in1=tb[64:96, :], op=ALU.add),
                     r=[ta.b, tb.b], w=[e.b])
                S.dma("pool", d["KRT"][:, tsl], e[64:96, :], r=[e.b], w=[db["KRT"]], acc=True)
                for (c0, dn) in ((416, "NQT"), (928, "NKT")):
                    for j in range(4):
                        ps = nextp()
                        proj(ps[:, :], ps.b, hT, c0 + j * 128, 128)
                        e = nextev()
                        S.op("act" if j % 2 else "dve",
                             (lambda: nc.scalar.copy(out=e[:, :], in_=ps[:, :])) if j % 2 else
                             (lambda: nc.vector.tensor_copy(out=e[:, :], in_=ps[:, :])), r=[ps.b], w=[e.b])
                        S.dma("pool", d[dn][j * 128:(j + 1) * 128, tsl], e[:, :], r=[e.b], w=[db[dn]], acc=True)
                if blk < NB:
                    self.norm_block_b(self.hT[blk % 2], ptr)
                for j in range(4):
                    ps = nextp()
                    for kc in range(8):
                        S.op("pe", lambda: nc.tensor.matmul(out=ps[:, :], lhsT=hT[:, kc, j * 128:(j + 1) * 128], rhs=win[:, kc, 1440:1952],
                                                            start=(kc == 0), stop=(kc == 7)), r=[win.b, hT.b], w=[ps.b], acc=(kc > 0))
                    e = nextev()
                    S.op("act", lambda: nc.scalar.copy(out=e[:, :], in_=ps[:, :]), r=[ps.b], w=[e.b])
                    r0 = b * 512 + j * 128
                    S.dma("pool", d["NV"][r0:r0 + 128, :], e[:, :], r=[e.b], w=[db["NV"]], acc=True)
                for h in range(8):
                    psa, psb = nextp(), nextp()
                    for kc in range(2):
                        S.op("pe", lambda: nc.tensor.matmul(out=psa[0:96, :], lhsT=wq[:, kc, h * 96:(h + 1) * 96], rhs=cq[:, kc, :],
                                                            start=(kc == 0), stop=(kc == 1)), r=[wq.b, cq.b], w=[psa.b], acc=(kc > 0))
                    for kc in range(2):
                        S.op("pe", lambda: nc.tensor.matmul(out=psb[64:96, :], lhsT=wq[:, kc, 768 + h * 32:768 + (h + 1) * 32], rhs=cq[:, kc, :],
                                                            start=(kc == 0), stop=(kc == 1)), r=[wq.b, cq.b], w=[psb.b], acc=(kc > 0))
                    e, ta, tb = nextev(), t1[h % 2], t2[h % 2]
                    S.op("dve", lambda: nc.vector.tensor_tensor(out=e[0:64, :], in0=psa[0:64, :], in1=rq_bc[0:64, :], op=ALU.mult),
                         r=[psa.b, rq_bc.b], w=[e.b])
                    S.op("dve", lambda: nc.vector.tensor_tensor(out=ta[64:96, :], in0=psa[64:96, :], in1=c[64:96, 0, :], op=ALU.mult),
                         r=[psa.b, c.b], w=[ta.b])
                    S.op("dve", lambda: nc.vector.tensor_tensor(out=tb[64:96, :], in0=psb[64:96, :], in1=c[64:96, 1, :], op=ALU.mult),
                         r=[psb.b, c.b], w=[tb.b])
                    S.op("pool", lambda: nc.gpsimd.tensor_tensor(out=ta[64:96, :], in0=ta[64:96, :], in1=tb[64:96, :], op=ALU.add),
                         r=[ta.b, tb.b], w=[ta.b])
                    S.op("pool", lambda: nc.gpsimd.tensor_tensor(out=e[64:96, :], in0=ta[64:96, :], in1=rq_bc[64:96, :], op=ALU.mult),
                         r=[ta.b, rq_bc.b], w=[e.b], acc=True)
                    S.dma("pool", d["QT"][h, :, tsl], e[0:96, :], r=[e.b], w=[db["QT"]], acc=True)
                for h in range(8):
                    ps = nextp()
                    S.op("pe", lambda: nc.tensor.matmul(out=ps[0:64, :], lhsT=wkv[:, h * 128:h * 128 + 64], rhs=ckv[:, :], start=True, stop=True),
                         r=[wkv.b, ckv.b], w=[ps.b])
                    e = nextev()
                    S.op("dve", lambda: nc.vector.tensor_tensor(out=e[0:64, :], in0=ps[0:64, :], in1=rkv_bc[0:64, :], op=ALU.mult),
                         r=[ps.b, rkv_bc.b], w=[e.b])
                    S.dma("pool", d["KNT"][h, :, tsl], e[0:64, :], r=[e.b], w=[db["KNT"]], acc=True)
                wv = wkv[:, :].rearrange("p (h t e) -> p h t e", h=8, t=2)[:, :, 1, :]
                for j in range(4):
                    ps = nextp()
                    S.op("pe", lambda: nc.tensor.matmul(out=ps[:, :].rearrange("p (h e) -> p h e", h=8), lhsT=ckv[:, j * 128:(j + 1) * 128], rhs=wv,
                                                        start=True, stop=True), r=[wkv.b, ckv.b], w=[ps.b])
                    v = vsb[j % 2]
                    S.op("act", lambda: nc.scalar.activation(out=v[:, :], in_=ps[:, :], func=AF.Copy, scale=rkv_col[:, j:j + 1]),
                         r=[ps.b, rkv_col.b], w=[v.b])
                    r0 = b * 512 + j * 128
                    S.dma("pool", d["MV"][r0:r0 + 128, :], v[:, :], r=[v.b], w=[db["MV"]], acc=True)
                S.rotate()

    def attn_phase(self, mode, bg=None, bg_every=40):
        nc, S, d, db = self.nc, self.S, self.d, self.db
        H = 4 if mode == "ret" else 8
        dk = {"mla": 96, "na": 64, "ret": 128}[mode]
        dvw = 128 if mode == "ret" else 65
        scale = {"mla": 96 ** -0.5, "na": 64 ** -0.5, "ret": 1.0}[mode]
        with self.scope() as es:
            T = lambda n, s, dt: self.tile(es, n, s, dt)
            qT = [T("aq%d" % i, [128, S_LEN], BF16) for i in range(2)]
            kT = [T("ak%d" % i, [128, S_LEN], BF16) for i in range(2)]
            vw = 128 if mode == "na" else dvw
            vt = [T("av%d" % i, [128, NT, vw], BF16) for i in range(2)]
            if mode != "ret":
                for v in vt:
                    if mode == "na":
                        S.op("dve", lambda: nc.vector.memset(v[:, :, :], 0.0), w=[v.b])
                    S.op("dve", lambda: nc.vector.memset(v[:, :, 64:65], 1.0), w=[v.b])
            if mode == "na":
                for t__ in qT + kT:
                    S.op("pool", lambda: nc.gpsimd.memset(t__[64:128, :], 0.0), w=[t__.b])
                dk = 128
            NP = 6
            pT = [T("apT%d" % i, [128, 512], BF16) for i in range(NP)]
            pS = [self.ptile(es, "aps%d" % i, [128, 512], F32) for i in range(4)]
            pO = [self.ptile(es, "apo%d" % i, [128, 512], F32) for i in range(2)]
            pB = [self.ptile(es, "apb%d" % i, [128, 512], F32) for i in range(2)]
            osb = [T("aosb%d" % i, [128, 512], F32) for i in range(2)]
            oout = [T("aoo%d" % i, [128, 512], BF16) for i in range(2)]
            rhl = [T("arhl%d" % i, [128, 2, 512], BF16) for i in range(2)]
            if mode == "na":
                bias = [T("abias%d" % i, [128, 3, 8, 512], BF16) for i in range(2)]
                bstg = [T("abstg%d" % i, [128, 8, 512], F32) for i in range(2)]
                bcnt = [0]
            if mode == "ret":
                E = [T("aE%d" % i, [128, RET_M], BF16) for i in range(2)]
                lg = T("alg", [128, 16], F32)
                npst = [T("anp%d" % i, [128, 2, 2016], F32) for i in range(2)]
                et = [T("aet%d" % i, [128, 2016], F32) for i in range(2)]
                rg = [T("arg%d" % i, [128, 512], F32) for i in range(2)]
                sqo = [T("asqo%d" % i, [128, 512], BF16) for i in range(2)]
                rbc = [T("arbc%d" % i, [128, 512], F32) for i in range(2)]
                S.dma("sp", lg[:, 0:8], d["c_decay"].partition_broadcast(128), r=[db["c_decay"]], w=[lg.b])
                S.op("act", lambda: nc.scalar.activation(out=lg[:, 8:16], in_=lg[:, 0:8], func=AF.Exp, scale=-1.0), r=[lg.b], w=[lg.b])
                S.op("act", lambda: nc.scalar.activation(out=lg[:, 8:16], in_=lg[:, 8:16], func=AF.Ln, bias=1.0), r=[lg.b], w=[lg.b])
                S.op("dve", lambda: nc.vector.tensor_scalar(out=lg[:, 8:16], in0=lg[:, 8:16], scalar1=-1.0, scalar2=None, op0=ALU.mult),
                     r=[lg.b], w=[lg.b])
            cnt = {"s": 0, "p": 0, "np": 0}

            def load_head(h):
                q, k, v = qT[h % 2], kT[h % 2], vt[h % 2]
                if mode == "mla":
                    S.dma("sp", q[0:96, :], d["QT"][h, :, :], r=[db["QT"]], w=[q.b])
                    S.dma("sp", k[0:64, :], d["KNT"][h, :, :], r=[db["KNT"]], w=[k.b])
                    S.dma("sp", k[64:96, :], d["KRT"][:, :], r=[db["KRT"]], w=[k.b], acc=True)
                    S.dma("sp", v[:, :, 0:64], d["MV"].rearrange("(t p) (h e) -> p t h e", p=128, h=8)[:, :, h, :], r=[db["MV"]], w=[v.b], acc=True)
                elif mode == "na":
                    S.dma("sp", q[0:64, :], d["NQT"][h * 64:(h + 1) * 64, :], r=[db["NQT"]], w=[q.b], acc=True)
                    S.dma("sp", k[0:64, :], d["NKT"][h * 64:(h + 1) * 64, :], r=[db["NKT"]], w=[k.b], acc=True)
                    S.dma("sp", v[:, :, 0:64], d["NV"].rearrange("(t p) (h e) -> p t h e", p=128, h=8)[:, :, h, :], r=[db["NV"]], w=[v.b])
                    bt = bias[h % 2]
                    for pat in range(3):
                        st_ = bstg[bcnt[0] % 2]
                        bcnt[0] += 1
                        S.dma("sp", st_[:, :, :], d["na_bias"][h, :, pat, :, :], r=[db["na_bias"]], w=[st_.b])
                        S.op("pool", lambda: nc.gpsimd.tensor_scalar(out=bt[:, pat, :, :], in0=st_[:, :, :], scalar1=1.0 / scale, scalar2=1.0, op0=ALU.mult, op1=ALU.mult),
                             r=[st_.b], w=[bt.b], acc=(pat > 0))
                else:
                    S.dma("sp", q[:, :], d["RQT"][h, :, :], r=[db["RQT"]], w=[q.b])
                    S.dma("sp", k[:, :], d["RKT"][h, :, :], r=[db["RKT"]], w=[k.b])
                    S.dma("sp", v[:, :, :], d["RV"].rearrange("(t p) (h e) -> p t h e", p=128, h=4)[:, :, h, :], r=[db["RV"]], w=[v.b])
                    Eh = E[h % 2]
                    for cch in range(4):
                        st, e_ = npst[cnt["np"] % 2], et[cnt["np"] % 2]
                        cnt["np"] += 1
                        sl = slice(cch * 2016, (cch + 1) * 2016)
                        S.dma("sp", st[:, :, :], d["ret_np"][:, :, sl], r=[db["ret_np"]], w=[st.b])
                        S.op("pool", lambda: nc.gpsimd.tensor_scalar(out=e_[:, :], in0=st[:, 0, :], scalar1=lg[:, 8 + h:9 + h], scalar2=0.0, op0=ALU.mult, op1=ALU.add),
                             r=[st.b, lg.b], w=[e_.b])
                        S.op("dve", lambda: nc.vector.scalar_tensor_tensor(out=e_[:, :], in0=st[:, 1, :], scalar=lg[:, 12 + h:13 + h], in1=e_[:, :],
                                                                           op0=ALU.mult, op1=ALU.add), r=[st.b, lg.b, e_.b], w=[e_.b])
                        S.op("act", lambda: nc.scalar.activation(out=Eh[:, sl], in_=e_[:, :], func=AF.Exp, bias=self.lnscale[:, 0:1]),
                             r=[e_.b, self.lnscale.b], w=[Eh.b], acc=(cch > 0))

            if mode == "ret":
                self.lnscale = T("alnsc", [128, 1], F32)
                import math
                S.op("pool", lambda: nc.gpsimd.memset(self.lnscale[:, :], math.log(128 ** -0.5)), w=[self.lnscale.b])

            tasks = []
            ncr = na_col_ranges()
            for h in range(H):
                for qc in range(NB):
                    if mode == "na":
                        pat_ = 0 if qc == 0 else (2 if qc == 7 else 1)
                        kt0 = min(max(4 * qc - 2, 0), 24)
                        kts = [(kt0 + r_, r_, ncr[pat_][r_]) for r_ in range(8) if ncr[pat_][r_] is not None]
                    else:
                        kts = [(kt_, kt_, (0, 512)) for kt_ in range(NT)]
                    for i, (kt, krel, cr) in enumerate(kts):
                        tasks.append((h, qc, i, kt, len(kts), krel, cr))
            LA = 3
            sps = {}
            pending = []
            heads_loaded = [0]

            def ensure_head(h):
                while heads_loaded[0] <= min(h, H - 1):
                    load_head(heads_loaded[0])
                    heads_loaded[0] += 1

            def emit_s(n):
                h, qc, i, kt, nk, krel, (c_lo, c_hi) = tasks[n]
                ensure_head(h)
                q, k = qT[h % 2], kT[h % 2]
                ps = pS[n % 4]
                S.op("pe", lambda: nc.tensor.matmul(out=ps[:, c_lo:c_hi], lhsT=k[0:dk, kt * 128:(kt + 1) * 128],
                                                    rhs=q[0:dk, qc * 512 + c_lo:qc * 512 + c_hi],
                                                    start=True, stop=(mode != "na")), r=[k.b, q.b], w=[ps.b])
                if mode == "na":
                    pat = 0 if qc == 0 else (2 if qc == 7 else 1)
                    bt = bias[h % 2]
                    S.op("pe", lambda: nc.tensor.matmul(out=ps[:, c_lo:c_hi], lhsT=self.ident[:, :], rhs=bt[:, pat, krel, c_lo:c_hi], start=False, stop=True),
                         r=[self.ident.b, bt.b], w=[ps.b], acc=True)
                sps[n] = ps

            ensure_head(0)
            for n in range(min(LA, len(tasks))):
                emit_s(n)
            f = -1
            bgen = bg(es) if bg is not None else None
            for n, (h, qc, i, kt, nk, krel, (c_lo, c_hi)) in enumerate(tasks):
                if bgen is not None and n % bg_every == bg_every - 1:
                    if next(bgen, "done") == "done":
                        bgen = None
                if i == 0:
                    f += 1
                    if qc == 0:
                        ensure_head(h + 1)
                if n + LA < len(tasks):
                    emit_s(n + LA)
                qsl = slice(qc * 512, (qc + 1) * 512)
                v = vt[h % 2]
                po = pO[f % 2]
                ps = sps.pop(n)
                p = pT[n % NP]
                if mode != "ret":
                    S.op("act", lambda: nc.scalar.activation(out=p[:, c_lo:c_hi], in_=ps[:, c_lo:c_hi], func=AF.Exp, scale=scale), r=[ps.b], w=[p.b])
                else:
                    Eh = E[h % 2]
                    c0 = RET_OFF - (kt * 128 - qc * 512)
                    S.op("dve", lambda: nc.vector.tensor_tensor(out=p[:, :], in0=ps[:, :], in1=Eh[:, c0:c0 + 512], op=ALU.mult),
                         r=[ps.b, Eh.b], w=[p.b])
                S.op("pe", lambda: nc.tensor.matmul(out=po[0:vw, c_lo:c_hi], lhsT=v[:, kt, :], rhs=p[:, c_lo:c_hi], start=(i == 0), stop=(i == nk - 1),
                                                    skip_group_check=(mode == "na")),
                     r=[v.b, p.b], w=[po.b], acc=(i > 0))
                for stage in (1, 3, 5):
                    if i == stage and pending and pending[0][0] == stage:
                        pending.pop(0)[1]()
                if i < nk - 1:
                    continue
                ob, oo, pb = osb[f % 2], oout[f % 2], pB[f % 2]
                if mode != "ret":
                    rr_ = rhl[f % 2]
                    S.op("act", lambda: nc.scalar.activation(out=ob[64:65, :], in_=po[64:65, :], func=AF.Ln), r=[po.b], w=[ob.b])
                    S.op("act", lambda: nc.scalar.activation(out=ob[64:65, :], in_=ob[64:65, :], func=AF.Exp, scale=-1.0), r=[ob.b], w=[ob.b])

                    def st1(ob=ob, po=po, rr_=rr_):
                        S.op("dve", lambda: nc.vector.tensor_copy(out=ob[0:64, :], in_=po[0:64, :]), r=[po.b], w=[ob.b], acc=True)
                        S.op("dve", lambda: nc.vector.tensor_copy(out=rr_[64:65, 0, :], in_=ob[64:65, :]), r=[ob.b], w=[rr_.b])
                        S.op("dve", lambda: nc.vector.tensor_tensor(out=rr_[64:65, 1, :], in0=ob[64:65, :], in1=rr_[64:65, 0, :], op=ALU.subtract),
                             r=[ob.b, rr_.b], w=[rr_.b])

                    def st3(pb=pb, rr_=rr_):
                        for t_ in range(2):
                            S.op("pe", lambda: nc.tensor.matmul(out=pb[0:64, :], lhsT=self.ones_bf[64:65, 0:64], rhs=rr_[64:65, t_, :],
                                                                start=(t_ == 0), stop=(t_ == 1)), r=[self.ones_bf.b, rr_.b], w=[pb.b], acc=(t_ > 0))

                    def st5(ob=ob, oo=oo, pb=pb, h=h, qsl=qsl):
                        S.op("dve", lambda: nc.vector.tensor_tensor(out=oo[0:64, :], in0=ob[0:64, :], in1=pb[0:64, :], op=ALU.mult),
                             r=[ob.b, pb.b], w=[oo.b])
                        row0 = (0 if mode == "mla" else 512) + h * 64
                        S.dma("pool", d["CT"][row0:row0 + 64, qsl], oo[0:64, :], r=[oo.b], w=[db["CT"]], acc=True)
                else:
                    g = rg[f % 2]
                    sq_ = sqo[f % 2]
                    rb = rbc[f % 2]
                    S.dma("sp", g[:, :], d["RG"][h * 128:(h + 1) * 128, qsl], r=[db["RG"]], w=[g.b])
                    S.op("act", lambda: nc.scalar.copy(out=ob[:, :], in_=po[:, :]), r=[po.b], w=[ob.b])
                    S.op("act", lambda: nc.scalar.activation(out=sq_[:, :], in_=po[:, :], func=AF.Square), r=[po.b], w=[sq_.b])

                    def st1(ob=ob, g=g):
                        S.op("pool", lambda: nc.gpsimd.tensor_tensor(out=ob[:, :], in0=ob[:, :], in1=g[:, :], op=ALU.mult), r=[ob.b, g.b], w=[ob.b])

                    def st3(pb=pb, sq_=sq_, rb=rb):
                        self.rstd_bc(pb, [(sq_[:, :], sq_.b)], 128, rb)

                    def st5(ob=ob, oo=oo, rb=rb, h=h, qsl=qsl):
                        S.op("dve", lambda: nc.vector.tensor_tensor(out=oo[:, :], in0=ob[:, :], in1=rb[:, :], op=ALU.mult), r=[ob.b, rb.b], w=[oo.b])
                        S.dma("pool", d["C1T"][h * 128:(h + 1) * 128, qsl], oo[:, :], r=[oo.b], w=[db["C1T"]], acc=True)
                pending.extend([(1, st1), (3, st3), (5, st5)])
                if qc == NB - 1:
                    S.rotate()
            while pending:
                pending.pop(0)[1]()
            if bgen is not None:
                for _ in bgen:
                    pass

    def phase_outproj(self, cname, wname, xin, xout, gi):
        nc, S, d, db = self.nc, self.S, self.d, self.db
        with self.scope() as es:
            T = lambda n, s, dt: self.tile(es, n, s, dt)
            wo = T("wo", [128, 8, D], BF16)
            with self.scope() as es2:
                stg = [self.tile(es2, "ost%d" % i, [128, D], F32) for i in range(2)]
                self.load_weight(stg, d[wname], db[wname], 8, D, None, lambda kc, st: [(wo[:, kc, :], st[:, 0:D], 1.0, wo.b)])
            cT = [T("ocT%d" % i, [128, 8, 512], BF16) for i in range(2)]
            pm = [self.ptile(es, "opm%d" % i, [128, D], F32) for i in range(4)]
            pb = self.postnorm_bufs(es, n=4)
            k = 0
            for blk in range(NB + 1):
                if blk < NB:
                    c = cT[blk % 2]
                    S.dma("sp", c[:, :, :], d[cname].rearrange("(c p) t -> p c t", p=128)[:, :, blk * 512:(blk + 1) * 512], r=[db[cname]], w=[c.b])
                if blk == 0:
                    continue
                b = blk - 1
                c = cT[b % 2]
                for j in range(4):
                    ps = pm[k % 4]
                    for n in range(2):
                        for kc in range(8):
                            S.op("pe", lambda: nc.tensor.matmul(out=ps[:, n * 512:(n + 1) * 512], lhsT=c[:, kc, j * 128:(j + 1) * 128],
                                                                rhs=wo[:, kc, n * 512:(n + 1) * 512], start=(kc == 0), stop=(kc == 7)),
                                 r=[c.b, wo.b], w=[ps.b], acc=(kc > 0 or n > 0))
                    self.postnorm_residual(ps, xin, xout, b * 512 + j * 128, gi, pb, k)
                    k += 1
                S.rotate()

    def phase_ffn(self, l, xin, xout):
        nc, S, d, db = self.nc, self.S, self.d, self.db
        gi = 1 + 2 * l
        with self.scope() as es:
            T = lambda n, s, dt: self.tile(es, n, s, dt)
            wd = T("fwd", [128, NFC, D], BF16)
            S.dma("sp", wd[:, :, :], d["WD%d" % l].rearrange("f p n -> p f n"), r=[db["WD%d" % l]], w=[wd.b])
            cw = T("fcw", [128, NFC, 3], F32)
            for t_ in range(3):
                S.dma("sp", cw[:, :, t_], d["ffn_conv"][l, t_, :].rearrange("(c p) -> p c", p=128), r=[db["ffn_conv"]], w=[cw.b],
                      acc=(t_ > 0), allow_slow_non_contiguous=True)
            self.norm_setup(es, nh=3, halo=True, nhb=4)
            for hT in self.hT:
                S.op("dve", lambda: nc.vector.memset(hT[:, :, :], 0.0), w=[hT.b])
            _e = S.engs["dve"]
            S._wait(_e, _e.sem, _e.sem.count)
            ptr = [self.ptile(es, "fptr", [128, 8, 128], BF16)]
            pg = [self.ptile(es, "fpg%d" % i, [128, 512], F32) for i in range(2)]
            pu = [self.ptile(es, "fpu%d" % i, [128, 512], F32) for i in range(2)]
            ph = self.ptile(es, "fph", [128, 512], F32)
            phb = [ph.b, ph.b]
            pdn = self.ptile(es, "fpd", [128, D], F32)
            wgu = [T("fwgu%d" % i, [128, 2, 8, 128], BF16) for i in range(4)]
            gsb = [T("fg%d" % i, [128, 514], F32) for i in range(3)]
            csb = [T("fc%d" % i, [128, 512], F32) for i in range(3)]
            gel = [T("fge%d" % i, [128, 512], F32) for i in range(3)]
            aTs = [T("faT%d" % i, [128, NFC, 512], BF16) for i in range(2)]
            pb = self.postnorm_bufs(es)
            k = 0
            wcnt = 0
            def halo_x(bl, br):
                hl, hr = self.hT[bl % 3], self.hT[br % 3]
                S.op("pool", lambda: nc.gpsimd.tensor_copy(out=hl[:, :, 513:514], in_=hr[:, :, 1:2]), r=[hr.b], w=[hl.b], acc=True)
                S.op("pool", lambda: nc.gpsimd.tensor_copy(out=hr[:, :, 0:1], in_=hl[:, :, 512:513]), r=[hl.b], w=[hr.b], acc=True)
                if br == NB - 1:
                    S.op("dve", lambda: nc.vector.memset(hr[:, :, 513:514], 0.0), w=[hr.b], acc=True)
            self.norm_block_a(xin, 0)
            self.norm_block_b(self.hT[0], ptr)
            self.norm_block_a(xin, 1)
            self.norm_block_b(self.hT[1], ptr)
            halo_x(0, 1)
            kcnt = [0]

            def emit_down_mm(bb, j):
                aT_ = aTs[bb % 2]
                for n in range(2):
                    for fc_ in range(NFC):
                        S.op("pe", lambda: nc.tensor.matmul(out=pdn[:, n * 512:(n + 1) * 512], lhsT=aT_[:, fc_, j * 128:(j + 1) * 128],
                                                            rhs=wd[:, fc_, n * 512:(n + 1) * 512], start=(fc_ == 0), stop=(fc_ == NFC - 1)),
                             r=[aT_.b, wd.b], w=[pdn.b], acc=(fc_ > 0 or n > 0))

            def emit_down_pn(bb, j):
                self.postnorm_residual(pdn, xin, xout, bb * 512 + j * 128, gi, pb, kcnt[0])
                kcnt[0] += 1

            for b in range(NB):
                hT = self.hT[b % 3]
                aT = aTs[b % 2]
                for fc in range(NFC):
                    if b >= 1 and fc in (2, 7, 12, 17):
                        emit_down_mm(b - 1, (2, 7, 12, 17).index(fc))
                    if b >= 1 and fc in (5, 10, 15, 20):
                        emit_down_pn(b - 1, (5, 10, 15, 20).index(fc))
                    if b + 2 < NB and fc in (1, 6, 11, 16):
                        self.norm_block_a(xin, b + 2, [(1, 6, 11, 16).index(fc)])
                    if b + 2 < NB and fc in (4, 9, 14, 19):
                        self.norm_block_b(self.hT[(b + 2) % 3], ptr, [(4, 9, 14, 19).index(fc)])
                        if fc == 19:
                            halo_x(b + 1, b + 2)
                    w = wgu[wcnt % 4]
                    wcnt += 1
                    S.dma("sp", w[:, 0, :, :], d["WG%d" % l][fc], r=[db["WG%d" % l]], w=[w.b])
                    S.dma("sp", w[:, 1, :, :], d["WU%d" % l][fc], r=[db["WU%d" % l]], w=[w.b], acc=True)
                    g, u = pg[fc % 2], pu[fc % 2]
                    for kc in range(8):
                        S.op("pe", lambda: nc.tensor.matmul(out=g[:, :], lhsT=w[:, 0, kc, :], rhs=hT[:, kc, 1:513], start=(kc == 0), stop=(kc == 7)),
                             r=[w.b, hT.b], w=[g.b], acc=(kc > 0))
                    for kc in range(8):
                        S.op("pe", lambda: nc.tensor.matmul(out=u[:, :], lhsT=w[:, 1, kc, :], rhs=hT[:, kc, 1:513], start=(kc == 0), stop=(kc == 7)),
                             r=[w.b, hT.b], w=[u.b], acc=(kc > 0))
                    hb_ = phb[fc % 2]
                    if fc < NFC - 1 or b + 1 >= NB or True:
                        for kc in range(8):
                            S.op("pe", lambda: nc.tensor.matmul(out=ph[:, 2 * fc:2 * fc + 2], lhsT=w[:, 0, kc, :], rhs=hT[:, kc, 0:514:513],
                                                                start=(kc == 0), stop=(kc == 7)), r=[w.b, hT.b], w=[hb_], acc=(kc > 0))
                    gs, cs_, ge = gsb[fc % 3], csb[fc % 3], gel[fc % 3]
                    S.op("act", lambda: nc.scalar.copy(out=gs[:, 1:513], in_=g[:, :]), r=[g.b], w=[gs.b])
                    S.op("act", lambda: nc.scalar.activation(out=cs_[:, :], in_=g[:, :], func=AF.Copy, scale=cw[:, fc, 1:2]), r=[g.b, cw.b], w=[cs_.b])
                    S.op("act", lambda: nc.scalar.copy(out=gs[:, 0:514:513], in_=ph[:, 2 * fc:2 * fc + 2]), r=[hb_], w=[gs.b], acc=True)
                    S.op("dve", lambda: nc.vector.scalar_tensor_tensor(out=cs_[:, :], in0=gs[:, 0:512], scalar=cw[:, fc, 0:1], in1=cs_[:, :],
                                                                       op0=ALU.mult, op1=ALU.add), r=[gs.b, cw.b, cs_.b], w=[cs_.b])
                    S.op("dve", lambda: nc.vector.scalar_tensor_tensor(out=cs_[:, :], in0=gs[:, 2:514], scalar=cw[:, fc, 2:3], in1=cs_[:, :],
                                                                       op0=ALU.mult, op1=ALU.add), r=[gs.b, cw.b, cs_.b], w=[cs_.b])
                    S.op("act", lambda: nc.scalar.activation(out=ge[:, :], in_=cs_[:, :], func=AF.Gelu_apprx_tanh), r=[cs_.b], w=[ge.b])
                    S.op("dve", lambda: nc.vector.tensor_tensor(out=aT[:, fc, :], in0=u[:, :], in1=ge[:, :], op=ALU.mult),
                         r=[u.b, ge.b], w=[aT.b], acc=(fc > 0))
                S.rotate()
            for j in range(4):
                emit_down_mm(NB - 1, j)
                emit_down_pn(NB - 1, j)

    def phase_l1a(self):
        nc, S, d, db = self.nc, self.S, self.d, self.db
        with self.scope() as es:
            T = lambda n, s, dt: self.tile(es, n, s, dt)
            win = T("win1", [128, 8, 4608], BF16)
            with self.scope() as es2:
                stg = [self.tile(es2, "w1st%d" % i, [128, 3584], F32) for i in range(2)]

                def p_win(kc, st):
                    out = [(win[:, kc, 0:3584], st[:, 0:3584], 1.0, win.b)]
                    for (s0, d0) in ((0, 3584), (512, 4096)):
                        sv = st[:, s0:s0 + 512].rearrange("p (h t e) -> p h t e", h=4, t=2)
                        dv = win[:, kc, d0:d0 + 512].rearrange("p (h t e) -> p h t e", h=4, t=2)
                        out.append((dv[:, :, 0, :], sv[:, :, 1, :], -1.0, win.b))
                        out.append((dv[:, :, 1, :], sv[:, :, 0, :], 1.0, win.b))
                    return out
                self.load_weight(stg, d["c_w_in"], db["c_w_in"], 8, 3584, 16, p_win)
            self.norm_setup(es, nhb=4)
            ptr = [self.ptile(es, "l1ptr%d" % i, [128, 8, 128], BF16) for i in range(2)]
            pp = [self.ptile(es, "l1pp%d" % i, [128, 512], F32) for i in range(6)]
            pcnt = [0]

            def nextp():
                pcnt[0] += 1
                return pp[pcnt[0] % len(pp)]
            cs = [T("rcs%d" % i, [128, 2, 512], F32) for i in range(2)]
            t1 = [T("l1t1_%d" % i, [128, 512], F32) for i in range(2)]
            t2 = [T("l1t2_%d" % i, [128, 512], F32) for i in range(2)]
            ev = [T("l1ev%d" % i, [128, 512], BF16) for i in range(4)]
            evf = [T("l1evf%d" % i, [128, 512], F32) for i in range(4)]
            zt = T("l1z", [128, 1536], F32)
            S.op("pool", lambda: nc.gpsimd.memset(zt[:, :], 0.0), w=[zt.b])
            S.dma("pool", d["HY"][0:1, :], zt[0:1, :], r=[zt.b], w=[db["HY"]], acc=True)
            S.dma("pool", d["HY"][S_LEN + 1:S_LEN + 2, :], zt[0:1, :], r=[zt.b], w=[db["HY"]], acc=True)
            ec = [0]

            def proj(ps, hT, c0):
                for kc in range(8):
                    S.op("pe", lambda: nc.tensor.matmul(out=ps[:, :], lhsT=win[:, kc, c0:c0 + 128], rhs=hT[:, kc, 0:512],
                                                        start=(kc == 0), stop=(kc == 7)), r=[win.b, hT.b], w=[ps.b], acc=(kc > 0))

            for blk in range(NB + 1):
                if blk < NB:
                    self.norm_block_a("X2", blk)
                    if blk == 0:
                        self.norm_block_b(self.hT[0], ptr)
                    c = cs[blk % 2]
                    S.dma("sp", c[:, :, :], d["ret_cs"][:, :, blk * 512:(blk + 1) * 512], r=[db["ret_cs"]], w=[c.b])
                if blk == 0:
                    continue
                b = blk - 1
                hT, c = self.hT[b % 2], cs[b % 2]
                tsl = slice(b * 512, (b + 1) * 512)
                for (c0, r0c, dn) in ((0, 3584, "RQT"), (512, 4096, "RKT")):
                    for h in range(4):
                        psa, psb = nextp(), nextp()
                        proj(psa, hT, c0 + h * 128)
                        proj(psb, hT, r0c + h * 128)
                        ta, tb = t1[h % 2], t2[h % 2]
                        ec[0] += 1
                        e = ev[ec[0] % 4]
                        S.op("dve", lambda: nc.vector.tensor_tensor(out=ta[:, :], in0=psa[:, :], in1=c[:, 0, :], op=ALU.mult), r=[psa.b, c.b], w=[ta.b])
                        S.op("dve", lambda: nc.vector.tensor_tensor(out=tb[:, :], in0=psb[:, :], in1=c[:, 1, :], op=ALU.mult), r=[psb.b, c.b], w=[tb.b])
                        S.op("pool", lambda: nc.gpsimd.tensor_tensor(out=e[:, :], in0=ta[:, :], in1=tb[:, :], op=ALU.add), r=[ta.b, tb.b], w=[e.b])
                        S.dma("pool", d[dn][h, :, tsl], e[:, :], r=[e.b], w=[db[dn]], acc=True)
                if blk < NB:
                    self.norm_block_b(self.hT[blk % 2], ptr)
                for j in range(4):
                    ps = nextp()
                    proj(ps, hT, 1536 + j * 128)
                    ec[0] += 1
                    e = evf[ec[0] % 4]
                    S.op("act", lambda: nc.scalar.activation(out=e[:, :], in_=ps[:, :], func=AF.Silu), r=[ps.b], w=[e.b])
                    S.dma("pool", d["RG"][j * 128:(j + 1) * 128, tsl], e[:, :], r=[e.b], w=[db["RG"]], acc=True)
                for j in range(4):
                    r0 = b * 512 + j * 128
                    for n in range(4):
                        ps = nextp()
                        c0 = 1024 if n == 0 else 2048 + (n - 1) * 512
                        for kc in range(8):
                            S.op("pe", lambda: nc.tensor.matmul(out=ps[:, :], lhsT=hT[:, kc, j * 128:(j + 1) * 128], rhs=win[:, kc, c0:c0 + 512],
                                                                start=(kc == 0), stop=(kc == 7)), r=[win.b, hT.b], w=[ps.b], acc=(kc > 0))
                        ec[0] += 1
                        if n == 0:
                            e = ev[ec[0] % 4]
                            S.op("act", lambda: nc.scalar.copy(out=e[:, :], in_=ps[:, :]), r=[ps.b], w=[e.b])
                            S.dma("pool", d["RV"][r0:r0 + 128, :], e[:, :], r=[e.b], w=[db["RV"]], acc=True)
                        else:
                            e = evf[ec[0] % 4]
                            if n % 2:
                                S.op("act", lambda: nc.scalar.copy(out=e[:, :], in_=ps[:, :]), r=[ps.b], w=[e.b])
                            else:
                                S.op("dve", lambda: nc.vector.tensor_copy(out=e[:, :], in_=ps[:, :]), r=[ps.b], w=[e.b])
                            S.dma("pool", d["HY"][1 + r0:1 + r0 + 128, (n - 1) * 512:n * 512], e[:, :], r=[e.b], w=[db["HY"]], acc=True)
                S.rotate()

    def phase_hyena(self):
        nc, S, d, db = self.nc, self.S, self.d, self.db
        with self.scope() as es:
            T = lambda n, s, dt: self.tile(es, n, s, dt)
            zT = T("hzT", [128, S_LEN], F32)
            S.dma("sp", zT[0:33, :], d["hy_zT"][:, :], r=[db["hy_zT"]], w=[zT.b])
            w1 = T("hw1", [128, 64], F32); w2 = T("hw2", [128, 64], F32); w3 = T("hw3", [128, 64], F32); w4 = T("hw4", [128, 1024], F32)
            S.dma("sp", w1[0:33, :], d["c_filt_w1"][:, :], r=[db["c_filt_w1"]], w=[w1.b])
            S.dma("sp", w2[0:64, :], d["c_filt_w2"][:, :], r=[db["c_filt_w2"]], w=[w2.b])
            S.dma("sp", w3[0:64, :], d["c_filt_w3"][:, :], r=[db["c_filt_w3"]], w=[w3.b])
            S.dma("sp", w4[0:64, :], d["c_filt_w4"][:, :], r=[db["c_filt_w4"]], w=[w4.b])
            fb = T("hfb", [128, 8], F32)
            for i, nm in enumerate(["c_filt_freq", "c_filt_b1", "c_filt_b2", "c_filt_b3"]):
                S.dma("sp", fb[0:64, i:i + 1], d[nm].rearrange("(p o) -> p o", o=1), r=[db[nm]], w=[fb.b], acc=(i > 0))
            S.op("dve", lambda: nc.vector.tensor_scalar(out=fb[0:64, 4:5], in0=fb[0:64, 0:1], scalar1=1.0 / 3.0, scalar2=None, op0=ALU.mult),
                 r=[fb.b], w=[fb.b])
            S.op("dve", lambda: nc.vector.tensor_scalar(out=fb[0:64, 5:8], in0=fb[0:64, 1:4], scalar1=fb[0:64, 4:5], scalar2=None, op0=ALU.mult),
                 r=[fb.b], w=[fb.b])
            h3T = T("hh3T", [128, S_LEN], F32)
            hA = [T("hhA%d" % i, [128, 512], F32) for i in range(2)]
            hB = [T("hhB%d" % i, [128, 512], F32) for i in range(2)]
            sS = [T("hsS%d" % i, [128, 512], F32) for i in range(2)]
            tS = [T("htS%d" % i, [128, 512], F32) for i in range(2)]
            pp = [self.ptile(es, "hpp%d" % i, [128, 512], F32) for i in range(4)]
            pc = [0]

            def nextp():
                pc[0] += 1
                return pp[pc[0] % 4]

            def sin3(ps, li, out_ap, out_buf, k, acc):
                s_, t_ = sS[k % 2], tS[k % 2]
                S.op("act", lambda: nc.scalar.activation(out=s_[0:64, :], in_=ps[0:64, :], func=AF.Sin, scale=fb[0:64, 4:5], bias=fb[0:64, 4 + li:5 + li]),
                     r=[ps.b, fb.b], w=[s_.b])
                S.op("pool", lambda: nc.gpsimd.tensor_tensor(out=t_[0:64, :], in0=s_[0:64, :], in1=s_[0:64, :], op=ALU.mult), r=[s_.b], w=[t_.b])
                S.op("dve", lambda: nc.vector.tensor_scalar(out=t_[0:64, :], in0=t_[0:64, :], scalar1=-4.0, scalar2=3.0, op0=ALU.mult, op1=ALU.add),
                     r=[t_.b], w=[t_.b])
                S.op("dve", lambda: nc.vector.tensor_tensor(out=out_ap, in0=t_[0:64, :], in1=s_[0:64, :], op=ALU.mult), r=[t_.b, s_.b], w=[out_buf], acc=acc)

            for c in range(8):
                csl = slice(c * 512, (c + 1) * 512)
                ps = nextp()
                S.op("pe", lambda: nc.tensor.matmul(out=ps[0:64, :], lhsT=w1[0:33, :], rhs=zT[0:33, csl], start=True, stop=True), r=[w1.b, zT.b], w=[ps.b])
                a, b_ = hA[c % 2], hB[c % 2]
                sin3(ps, 1, a[0:64, :], a.b, 3 * c, False)
                ps = nextp()
                S.op("pe", lambda: nc.tensor.matmul(out=ps[0:64, :], lhsT=w2[0:64, :], rhs=a[0:64, :], start=True, stop=True), r=[w2.b, a.b], w=[ps.b])
                sin3(ps, 2, b_[0:64, :], b_.b, 3 * c + 1, False)
                ps = nextp()
                S.op("pe", lambda: nc.tensor.matmul(out=ps[0:64, :], lhsT=w3[0:64, :], rhs=b_[0:64, :], start=True, stop=True), r=[w3.b, b_.b], w=[ps.b])
                sin3(ps, 3, h3T[0:64, csl], h3T.b, 3 * c + 2, c > 0)
            win = [T("hwin%d" % i, [128, 512], F32) for i in range(2)]
            xo = [T("hxo%d" % i, [128, 512], BF16) for i in range(4)]
            zb = T("hzb", [128, 512], BF16)
            S.op("pool", lambda: nc.gpsimd.memset(zb[:, :], 0.0), w=[zb.b])
            S.dma("pool", d["XD"][S_LEN - 1:S_LEN, 1024:1536], zb[0:1, :], r=[zb.b], w=[db["XD"]], acc=True)
            k = 0
            for j in range(NT):
                wt = win[j % 2]
                S.dma("sp", wt[:, :], d["hy_win"][j * 128:(j + 1) * 128, :], r=[db["hy_win"]], w=[wt.b])
                for n in range(2):
                    ps = nextp()
                    S.op("pe", lambda: nc.tensor.matmul(out=ps[:, :], lhsT=h3T[0:64, j * 128:(j + 1) * 128], rhs=w4[0:64, n * 512:(n + 1) * 512],
                                                        start=True, stop=True), r=[h3T.b, w4.b], w=[ps.b])
                    o = xo[k % 4]
                    k += 1
                    S.op("dve", lambda: nc.vector.tensor_tensor(out=o[:, :], in0=ps[:, :], in1=wt[:, :], op=ALU.mult), r=[ps.b, wt.b], w=[o.b])
                    if n == 0:
                        S.dma("pool", d["XD"][j * 128:(j + 1) * 128, 512:1024], o[:, :], r=[o.b], w=[db["XD"]], acc=True)
                    elif j == 0:
                        S.dma("pool", d["XD"][0:127, 1024:1536], o[1:128, :], r=[o.b], w=[db["XD"]], acc=True)
                    else:
                        S.dma("pool", d["XD"][j * 128 - 1:j * 128 + 127, 1024:1536], o[:, :], r=[o.b], w=[db["XD"]], acc=True)
        def conv_gen(es):
            T = lambda n, s, dt: self.tile(es, n, s, dt)
            cwb = T("hcwb", [128, 3, 1536], F32)
            for t_ in range(3):
                S.dma("sp", cwb[:, t_, :], d["c_short_conv"][t_, :].partition_broadcast(128), r=[db["c_short_conv"]], w=[cwb.b], acc=(t_ > 0))
            bb = T("hbb", [128, 512], F32)
            S.dma("sp", bb[:, :], d["c_hy_bias"].partition_broadcast(128), r=[db["c_hy_bias"]], w=[bb.b])
            sh = [[T("hsh%d_%d" % (i, t_), [128, 1536], F32) for t_ in range(3)] for i in range(2)]
            za = [T("hza%d" % i, [128, 1536], F32) for i in range(1)]
            zc = [T("hzc%d" % i, [128, 1536], F32) for i in range(1)]
            uu = [T("huu%d" % i, [128, 512], F32) for i in range(2)]
            ub = [T("hub%d" % i, [128, 512], F32) for i in range(2)]
            ubf = [T("hubf%d" % i, [128, 512], BF16) for i in range(2)]
            for j in range(NT):
                s3, a, c_ = sh[j % 2], za[0], zc[0]
                for t_ in range(3):
                    S.dma("sp", s3[t_][:, :], d["HY"][j * 128 + t_:j * 128 + t_ + 128, :], r=[db["HY"]], w=[s3[t_].b])
                yield
                for (en, E_, c0_, c1_) in (("dve", nc.vector, 0, 1152), ("pool", nc.gpsimd, 1152, 1536)):
                    S.op(en, lambda: E_.tensor_tensor(out=a[:, c0_:c1_], in0=s3[0][:, c0_:c1_], in1=cwb[:, 0, c0_:c1_], op=ALU.mult),
                         r=[s3[0].b, cwb.b], w=[a.b], acc=(c0_ > 0))
                    S.op(en, lambda: E_.tensor_tensor(out=c_[:, c0_:c1_], in0=s3[1][:, c0_:c1_], in1=cwb[:, 1, c0_:c1_], op=ALU.mult),
                         r=[s3[1].b, cwb.b], w=[c_.b], acc=(c0_ > 0))
                    S.op(en, lambda: E_.tensor_tensor(out=a[:, c0_:c1_], in0=a[:, c0_:c1_], in1=c_[:, c0_:c1_], op=ALU.add), r=[a.b, c_.b], w=[a.b], acc=True)
                    S.op(en, lambda: E_.tensor_tensor(out=c_[:, c0_:c1_], in0=s3[2][:, c0_:c1_], in1=cwb[:, 2, c0_:c1_], op=ALU.mult),
                         r=[s3[2].b, cwb.b], w=[c_.b], acc=True)
                    S.op(en, lambda: E_.tensor_tensor(out=a[:, c0_:c1_], in0=a[:, c0_:c1_], in1=c_[:, c0_:c1_], op=ALU.add), r=[a.b, c_.b], w=[a.b], acc=True)
                u, ub_, uf = uu[j % 2], ub[j % 2], ubf[j % 2]
                rows = slice(j * 128, (j + 1) * 128)
                S.op("dve", lambda: nc.vector.tensor_tensor(out=u[:, :], in0=a[:, 1024:1536], in1=a[:, 512:1024], op=ALU.mult), r=[a.b], w=[u.b])
                S.op("act", lambda: nc.scalar.copy(out=uf[:, :], in_=u[:, :]), r=[u.b], w=[uf.b])
                S.op("dve", lambda: nc.vector.tensor_tensor(out=ub_[:, :], in0=u[:, :], in1=bb[:, :], op=ALU.mult), r=[u.b, bb.b], w=[ub_.b])
                S.dma("pool", d["XD"][rows, 0:512], uf[:, :], r=[uf.b], w=[db["XD"]], acc=True)
                S.dma("pool", d["UB"][rows, :], ub_[:, :], r=[ub_.b], w=[db["UB"]], acc=True)
                S.dma("pool", d["X0"][rows, :], a[:, 0:512], r=[a.b], w=[db["X0"]], acc=True)
                yield

        def load_tab(ft, t_):
            S.dma("sp", t_[:, 0, :, :], d["dft_c"][ft], r=[db["dft_c"]], w=[t_.b])
            S.dma("sp", t_[:, 1, :, :], d["dft_s"][ft], r=[db["dft_s"]], w=[t_.b], acc=True)

        with self.scope() as es:
            T = lambda n, s, dt: self.tile(es, n, s, dt)
            Xf = T("hXf", [128, NT, 1024], BF16)
            for q4 in range(4):
                S.dma("sp", Xf[:, q4 * 8:(q4 + 1) * 8, :], d["XD"].rearrange("(t p) c -> p t c", p=128)[:, q4 * 8:(q4 + 1) * 8, 512:1536],
                      r=[db["XD"]], w=[Xf.b], acc=(q4 > 0))
            ph = T("hph", [128, 3, 32], F32)
            S.dma("sp", ph[:, 0:2, :], d["dft_ph"][:, :, :], r=[db["dft_ph"]], w=[ph.b])
            S.op("dve", lambda: nc.vector.tensor_scalar(out=ph[:, 2, :], in0=ph[:, 1, :], scalar1=-1.0, scalar2=None, op0=ALU.mult), r=[ph.b], w=[ph.b])
            tb = [T("htb%d" % i, [128, 2, NT, 128], BF16) for i in range(2)]
            bank = [self.ptile(es, "hbk%d" % i, [128, 512], F32) for i in range(8)]
            A1 = T("hA1", [128, 512], F32); A2 = T("hA2", [128, 512], F32)
            k2o = [T("hk2o%d" % i, [128, 2, 512], F32) for i in range(2)]
            cg = conv_gen(es)
            bi = 0
            load_tab(0, tb[0])
            for ft in range(NT):
                if ft + 1 < NT:
                    load_tab(ft + 1, tb[(ft + 1) % 2])
                t_ = tb[ft % 2]
                cs_ = []
                for n in range(2):
                    Cb_, Sb_ = bank[bi % 8], bank[(bi + 1) % 8]
                    bi += 2
                    cs_.append((Cb_, Sb_))
                    for (pb_, ti) in ((Cb_, 0), (Sb_, 1)):
                        for st in range(NT):
                            S.op("pe", lambda: nc.tensor.matmul(out=pb_[:, :], lhsT=t_[:, ti, st, :], rhs=Xf[:, st, n * 512:(n + 1) * 512],
                                                                start=(st == 0), stop=(st == NT - 1)), r=[t_.b, Xf.b], w=[pb_.b], acc=(st > 0))
                    if n == 0:
                        next(cg, None)
                (Cf, Sf), (Cbk, Sbk) = cs_
                V, P = nc.vector, nc.gpsimd
                k2 = k2o[ft % 2]
                S.op("act", lambda: nc.scalar.copy(out=A1[:, :], in_=Cf[:, :]), r=[Cf.b], w=[A1.b])
                S.op("act", lambda: nc.scalar.copy(out=A2[:, :], in_=Sf[:, :]), r=[Sf.b], w=[A2.b])
                S.op("dve", lambda: V.tensor_tensor(out=A1[:, :], in0=Cbk[:, :], in1=A1[:, :], op=ALU.add), r=[Cbk.b, A1.b], w=[A1.b])
                S.op("dve", lambda: V.tensor_tensor(out=A2[:, :], in0=Sbk[:, :], in1=A2[:, :], op=ALU.subtract), r=[Sbk.b, A2.b], w=[A2.b])
                S.op("pool", lambda: P.tensor_scalar(out=k2[:, 0, :], in0=A1[:, :], scalar1=ph[:, 0, ft:ft + 1], scalar2=0.0, op0=ALU.mult, op1=ALU.add),
                     r=[A1.b, ph.b], w=[k2.b])
                S.op("dve", lambda: V.scalar_tensor_tensor(out=k2[:, 0, :], in0=A2[:, :], scalar=ph[:, 2, ft:ft + 1], in1=k2[:, 0, :], op0=ALU.mult, op1=ALU.add),
                     r=[A2.b, ph.b, k2.b], w=[k2.b])
                S.op("pool", lambda: P.tensor_scalar(out=k2[:, 1, :], in0=A1[:, :], scalar1=ph[:, 1, ft:ft + 1], scalar2=0.0, op0=ALU.mult, op1=ALU.add),
                     r=[A1.b, ph.b], w=[k2.b], acc=True)
                S.op("dve", lambda: V.scalar_tensor_tensor(out=k2[:, 1, :], in0=A2[:, :], scalar=ph[:, 0, ft:ft + 1], in1=k2[:, 1, :], op0=ALU.mult, op1=ALU.add),
                     r=[A2.b, ph.b, k2.b], w=[k2.b], acc=True)
                S.dma("pool", d["K2D"][ft * 128:(ft + 1) * 128, :].rearrange("p (r c) -> p r c", r=2), k2[:, :, :], r=[k2.b], w=[db["K2D"]], acc=True)
                next(cg, None)
                S.rotate()
            for _ in cg:
                pass
        with self.scope() as es:
            T = lambda n, s, dt: self.tile(es, n, s, dt)
            Xu = T("hXu", [128, NT, 512], BF16)
            for q4 in range(4):
                S.dma("sp", Xu[:, q4 * 8:(q4 + 1) * 8, :], d["XD"].rearrange("(t p) c -> p t c", p=128)[:, q4 * 8:(q4 + 1) * 8, 0:512],
                      r=[db["XD"]], w=[Xu.b], acc=(q4 > 0))
            tb = [T("htc%d" % i, [128, 2, NT, 128], BF16) for i in range(3)]
            bank = [self.ptile(es, "hbl%d" % i, [128, 512], F32) for i in range(8)]
            k2i = [T("hk2i%d" % i, [128, 2, 512], F32) for i in range(3)]
            Dm = [[T("hD%d_%d" % (i, j_), [128, 512], F32) for j_ in range(4)] for i in range(2)]
            yo = [T("hyo%d" % i, [128, 2, 512], BF16) for i in range(2)]
            bi = 0
            load_tab(0, tb[0])
            load_tab(1, tb[1])
            for ft in range(NT):
                if ft + 2 < NT:
                    load_tab(ft + 2, tb[(ft + 2) % 3])
                t_ = tb[ft % 3]
                k2 = k2i[ft % 3]
                S.dma("sp", k2[:, :, :], d["K2D"][ft * 128:(ft + 1) * 128, :].rearrange("p (r c) -> p r c", r=2), r=[db["K2D"]], w=[k2.b])
                Cu, Su = bank[bi % 8], bank[(bi + 1) % 8]
                bi += 2
                for (pb_, ti) in ((Cu, 0), (Su, 1)):
                    for st in range(NT):
                        S.op("pe", lambda: nc.tensor.matmul(out=pb_[:, :], lhsT=t_[:, ti, st, :], rhs=Xu[:, st, :],
                                                            start=(st == 0), stop=(st == NT - 1)), r=[t_.b, Xu.b], w=[pb_.b], acc=(st > 0))
                V, P = nc.vector, nc.gpsimd
                D1, D2, D3, D4 = Dm[ft % 2]
                S.op("dve", lambda: V.tensor_tensor(out=D1[:, :], in0=Cu[:, :], in1=k2[:, 0, :], op=ALU.mult), r=[Cu.b, k2.b], w=[D1.b])
                S.op("dve", lambda: V.tensor_tensor(out=D2[:, :], in0=Su[:, :], in1=k2[:, 1, :], op=ALU.mult), r=[Su.b, k2.b], w=[D2.b])
                S.op("dve", lambda: V.tensor_tensor(out=D3[:, :], in0=Su[:, :], in1=k2[:, 0, :], op=ALU.mult), r=[Su.b, k2.b], w=[D3.b])
                S.op("dve", lambda: V.tensor_tensor(out=D4[:, :], in0=Cu[:, :], in1=k2[:, 1, :], op=ALU.mult), r=[Cu.b, k2.b], w=[D4.b])
                y = yo[ft % 2]
                S.op("pool", lambda: P.tensor_tensor(out=y[:, 0, :], in0=D1[:, :], in1=D2[:, :], op=ALU.add), r=[D1.b, D2.b], w=[y.b])
                S.op("pool", lambda: P.tensor_tensor(out=y[:, 1, :], in0=D3[:, :], in1=D4[:, :], op=ALU.subtract), r=[D3.b, D4.b], w=[y.b], acc=True)
                S.dma("pool", d["YD"][ft], y[:, :, :], r=[y.b], w=[db["YD"]], acc=True)
                S.rotate()
        with self.scope() as es:
            T = lambda n, s, dt: self.tile(es, n, s, dt)
            Y = T("hY", [128, NT, 2, 512], BF16)
            for q4 in range(4):
                S.dma("sp", Y[:, q4 * 8:(q4 + 1) * 8, :, :], d["YD"].rearrange("f p r c -> p f r c")[:, q4 * 8:(q4 + 1) * 8, :, :],
                      r=[db["YD"]], w=[Y.b], acc=(q4 > 0))
            tb = [T("hitb%d" % i, [128, 2, NT, 128], BF16) for i in range(3)]
            bank = [self.ptile(es, "hibk%d" % i, [128, 512], F32) for i in range(3)]
            ptr = [self.ptile(es, "hiptr%d" % i, [128, 8, 128], BF16) for i in range(2)]
            ubt = [T("hiub%d" % i, [128, 512], F32) for i in range(2)]
            x0t = [T("hix0%d" % i, [128, 512], F32) for i in range(2)]
            t1 = [T("hit1%d" % i, [128, 512], F32) for i in range(2)]
            dd = [T("hidd%d" % i, [128, 512], BF16) for i in range(2)]
            dT = [T("hidT%d" % i, [128, 4, 128], BF16) for i in range(2)]

            def load_tab2(tt, t_):
                S.dma("sp", t_[:, 0, :, :], d["dft_c"][tt], r=[db["dft_c"]], w=[t_.b])
                S.dma("sp", t_[:, 1, :, :], d["dft_s"][tt], r=[db["dft_s"]], w=[t_.b], acc=True)
            load_tab2(0, tb[0])
            load_tab2(1, tb[1])
            for tt in range(NT):
                if tt + 2 < NT:
                    load_tab2(tt + 2, tb[(tt + 2) % 3])
                t_, pb_ = tb[tt % 3], bank[tt % 3]
                rows = slice(tt * 128, (tt + 1) * 128)
                u_, x_ = ubt[tt % 2], x0t[tt % 2]
                S.dma("sp", u_[:, :], d["UB"][rows, :], r=[db["UB"]], w=[u_.b])
                S.dma("sp", x_[:, :], d["X0"][rows, :], r=[db["X0"]], w=[x_.b])
                for ft in range(NT):
                    for ri in range(2):
                        S.op("pe", lambda: nc.tensor.matmul(out=pb_[:, :], lhsT=t_[:, ri, ft, :], rhs=Y[:, ft, ri, :],
                                                            start=(ft == 0 and ri == 0), stop=(ft == NT - 1 and ri == 1)),
                             r=[t_.b, Y.b], w=[pb_.b], acc=(ft > 0 or ri > 0))
                a, o, p_, dt_ = t1[tt % 2], dd[tt % 2], ptr[tt % 2], dT[tt % 2]
                S.op("dve", lambda: nc.vector.scalar_tensor_tensor(out=a[:, :], in0=pb_[:, :], scalar=2.0 / (2 * S_LEN), in1=u_[:, :],
                                                                   op0=ALU.mult, op1=ALU.add), r=[pb_.b, u_.b], w=[a.b])
                S.op("pool", lambda: nc.gpsimd.tensor_tensor(out=o[:, :], in0=a[:, :], in1=x_[:, :], op=ALU.mult), r=[a.b, x_.b], w=[o.b])
                for cc in range(4):
                    S.op("pe", lambda: nc.tensor.transpose(out=p_[:, cc, :], in_=o[:, cc * 128:(cc + 1) * 128], identity=self.ident[:, :]),
                         r=[o.b, self.ident.b], w=[p_.b], acc=(cc > 0))
                S.op("act", lambda: nc.scalar.copy(out=dt_[:, :, :], in_=p_[:, 0:4, :]), r=[p_.b], w=[dt_.b])
                S.dma("pool", d["C1T"][512:1024, rows].rearrange("(c p) t -> p c t", p=128), dt_[:, :, :], r=[dt_.b], w=[db["C1T"]], acc=True)
            S.rotate()

    def build(self):
        self.declare()
        ph = self.phases
        with self.scope() as es:
            self.setup_common(es)
            if ph is None or "l0a" in ph:
                self.phase_l0a()
            if ph is None or "mla" in ph:
                self.attn_phase("mla", bg=(lambda es_: self.prep_ffn_gen(es_, 0, eng="pool")), bg_every=24)
            if ph is None or "na" in ph:
                self.attn_phase("na")
            if ph is None or "out0" in ph:
                self.phase_outproj("CT", "a_w_out", "x", "X1", 0)
            if ph is None or "ffn0" in ph:
                if ph is not None and "mla" not in ph:
                    self.prep_ffn_weights(es, 0)
                self.phase_ffn(0, "X1", "X2")
            if ph is None or "l1a" in ph:
                self.phase_l1a()
            if ph is None or "ret" in ph:
                self.attn_phase("ret", bg=(lambda es_: self.prep_ffn_gen(es_, 1, eng="pool")), bg_every=12)
            if ph is None or "hy" in ph:
                self.phase_hyena()
            if ph is None or "out1" in ph:
                self.phase_outproj("C1T", "c_w_out", "X2", "X3", 2)
            if ph is None or "ffn1" in ph:
                if ph is not None and "ret" not in ph:
                    self.prep_ffn_weights(es, 1)
                self.phase_ffn(1, "X3", "out")
            outs = [self.db[n] for n in self.debug] + [self.db["out"]]
            self.S.finish(outs)
        return self.nc


def make_inputs(inputs, b):
    c = host_consts()
    f = lambda a: np.ascontiguousarray(np.asarray(a, dtype=np.float32))
    m = {"x": f(inputs["x"][b])}
    for n in ["mix_pre_norm", "mix_post_norm", "ffn_pre_norm", "ffn_post_norm", "ffn_w_gate", "ffn_w_up", "ffn_conv", "ffn_w_down"]:
        m[n] = f(inputs[n])
    for n in ["a_w_in", "a_q_norm", "a_w_q_up", "a_kv_norm", "a_w_kv_up", "a_w_out", "c_w_in", "c_short_conv", "c_filt_w1", "c_filt_b1",
              "c_filt_w2", "c_filt_b2", "c_filt_w3", "c_filt_b3", "c_filt_w4", "c_filt_freq", "c_hy_bias", "c_w_out"]:
        m[n] = f(inputs[n][0])
    m["c_decay"] = f(np.concatenate([np.asarray(inputs["c_decay_fwd"][0]), np.asarray(inputs["c_decay_bwd"][0])]))
    m["na_bias"] = na_bias_layout(np.asarray(inputs["a_rpb"][0], dtype=np.float32))
    for k_ in ["ident_bf", "mla_cs", "ret_cs", "ret_np", "hy_zT", "hy_win", "dft_c", "dft_s", "dft_ph"]:
        m[k_] = c[k_]
    return m


def kernel(**inputs):
    bld = Builder()
    nc = bld.build()
    shared = make_inputs(inputs, 0)
    in_maps = []
    for b in range(8):
        m = dict(shared)
        m["x"] = np.ascontiguousarray(np.asarray(inputs["x"][b], dtype=np.float32))
        in_maps.append(m)
    res = run_bass_kernel_spmd(nc, in_maps, core_ids=list(range(8)))
    return np.stack([r["out"] for r in res.results], 0).astype(np.float32)
```
